# Optimizing a Trainium2 kernel written in Bass

```python
import math
import jax, jax.numpy as jnp
from jax import lax
import numpy as np


D_MODEL = 1024
BATCH = 16
SEQ = 2048
DEPTH = 2
DEC_BATCH = 8
DEC_SEQ = 8192
PAST_LEN = 128

GRID_W = 64
N_BRANCH = 4
BRANCH_W = 256
HEAD_DIM = 64
ATT_Q_HEADS = 4
ATT_KV_HEADS = 2
ATT_GROUP = ATT_Q_HEADS // ATT_KV_HEADS
Q_BLOCK = 128
ROPE_BASE = 10000.0
ROPE_FREQS = HEAD_DIM // 4
HY_WIDTH = BRANCH_W
HY_EMB = 33
HY_BANDS = (HY_EMB - 1) // 2
HY_FILTER_HIDDEN = 64
HY_FAST_DECAY = 0.3
HY_SLOW_DECAY = 1.5
HY_TARGET = 1e-2
RET_HEADS = 4
RET_W = RET_HEADS * HEAD_DIM
RET_CHUNK = 128
SC_WIDTH = BRANCH_W
D_FF = ((-(-8 * D_MODEL // 3)) + 255) // 256 * 256
NORM_EPS = 1e-6

ATT_Q_W = ATT_Q_HEADS * HEAD_DIM
ATT_KV_W = ATT_KV_HEADS * HEAD_DIM
A_K_OFF = ATT_Q_W
A_V_OFF = A_K_OFF + ATT_KV_W
HY_OFF = A_V_OFF + ATT_KV_W
RET_OFF = HY_OFF + 3 * HY_WIDTH
SC_OFF = RET_OFF + 4 * RET_W
GATE_OFF = SC_OFF + 3 * SC_WIDTH
IN_COLS = GATE_OFF + N_BRANCH * D_MODEL

kernel_name = 'hybrid_gated_parallel_encoder'

F32 = jnp.float32


def rms_norm(x, gain=None):
    x32 = x.astype(F32)
    y = x32 * lax.rsqrt(jnp.mean(x32 * x32, axis=-1, keepdims=True) + NORM_EPS)
    if gain is not None:
        y = y * gain.astype(F32)
    return y.astype(x.dtype)


def axial_rope_tables(L):
    rows = L // GRID_W
    r = jnp.repeat(jnp.arange(rows, dtype=F32), GRID_W)
    c = jnp.tile(jnp.arange(GRID_W, dtype=F32), rows)
    inv = ROPE_BASE ** (-jnp.arange(ROPE_FREQS, dtype=F32) / ROPE_FREQS)
    ang = jnp.stack([r[:, None] * inv, c[:, None] * inv], axis=1)
    return jnp.cos(ang), jnp.sin(ang)


def apply_rope(x, cos, sin):
    B, L, H, _ = x.shape
    xr = x.astype(F32).reshape(B, L, H, 2, 2, ROPE_FREQS)
    a = xr[..., 0, :]
    b = xr[..., 1, :]
    c = cos[None, :, None]
    s = sin[None, :, None]
    out = jnp.stack([a * c - b * s, b * c + a * s], axis=-2)
    return out.reshape(B, L, H, HEAD_DIM).astype(x.dtype)


def conv3(x, w, b=None):
    xp = jnp.pad(x, ((0, 0), (1, 1), (0, 0)))
    y = xp[:, :-2] * w[0] + xp[:, 1:-1] * w[1] + xp[:, 2:] * w[2]
    if b is not None:
        y = y + b
    return y


def block_attention(q, k, v):
    B, L = q.shape[0], q.shape[1]
    nb = L // Q_BLOCK
    qg = q.reshape(B, nb, Q_BLOCK, ATT_KV_HEADS, ATT_GROUP, HEAD_DIM).transpose(1, 0, 2, 3, 4, 5)
    scale = HEAD_DIM ** -0.5

    def one_block(qb):
        s = jnp.einsum('bqkgd,bskd->bkgqs', qb, k).astype(F32) * scale
        p = jax.nn.softmax(s, axis=-1).astype(v.dtype)
        return jnp.einsum('bkgqs,bskd->bqkgd', p, v)

    o = lax.map(one_block, qg)
    return o.transpose(1, 0, 2, 3, 4, 5).reshape(B, L, ATT_Q_W)


def hyena_filter(L, w1, b1, w2, b2, w3, freq):
    t = jnp.linspace(0.0, 1.0, L, dtype=F32)[:, None]
    f = jnp.linspace(1e-4, HY_BANDS - 1, HY_BANDS, dtype=F32)
    ang = (2.0 * math.pi / L) * jnp.arange(L, dtype=F32)[:, None] * f[None, :]
    z = jnp.concatenate([t, jnp.cos(ang), -jnp.sin(ang)], axis=-1)
    h = jnp.sin(freq[0].astype(F32) * (z @ w1.astype(F32) + b1.astype(F32)))
    h = jnp.sin(freq[1].astype(F32) * (h @ w2.astype(F32) + b2.astype(F32)))
    h = (h @ w3.astype(F32)).reshape(L, 2, HY_WIDTH)
    deltas = jnp.abs(jnp.linspace(math.log(HY_TARGET) / HY_SLOW_DECAY,
                                  math.log(HY_TARGET) / HY_FAST_DECAY, HY_WIDTH, dtype=F32))
    h = h * jnp.exp(-t * deltas)[:, None, :]
    k2 = jnp.concatenate([h[:, 0], jnp.zeros((1, HY_WIDTH), F32), h[:0:-1, 1]], axis=0)
    return k2 * lax.rsqrt(jnp.sum(k2 * k2, axis=0, keepdims=True) + NORM_EPS)


def long_conv(u, k2):
    L = u.shape[1]
    n = 2 * L
    uf = jnp.fft.rfft(u, n=n, axis=1)
    kf = jnp.fft.rfft(k2, n=n, axis=0)
    return jnp.fft.irfft(uf * kf[None], n=n, axis=1)[:, :L]


def retention_scan(q, k, v, log_g, inclusive):
    B, L, H, D = q.shape
    C = RET_CHUNK
    nc = L // C
    qc = q.reshape(B, nc, C, H, D)
    kc = k.reshape(B, nc, C, H, D)
    vc = v.reshape(B, nc, C, H, D)
    i = jnp.arange(C, dtype=F32)
    diff = i[:, None] - i[None, :]
    mask = (diff >= 0) if inclusive else (diff > 0)
    decay = jnp.where(mask[None], jnp.exp(jnp.where(mask, diff, 0.0)[None] * log_g[:, None, None]), 0.0)
    s = jnp.einsum('bnihd,bnjhd->bnhij', qc, kc) * decay[None, None]
    o_intra = jnp.einsum('bnhij,bnjhd->bnihd', s, vc)
    w_k = jnp.exp((C - 1 - i)[None, :] * log_g[:, None])
    kv = jnp.einsum('bnjhd,bnjhe,hj->nbhde', kc, vc, w_k)
    g_chunk = jnp.exp(C * log_g)[None, :, None, None]

    def step(S, kv_n):
        return g_chunk * S + kv_n, S

    _, states = lax.scan(step, jnp.zeros((B, H, D, D), F32), kv)
    w_q = jnp.exp((i + 1)[None, :] * log_g[:, None])
    o_cross = jnp.einsum('bnihd,nbhde,hi->bnihe', qc, states, w_q)
    return (o_intra + o_cross).reshape(B, L, H, D)


def layer(x, cos, sin, k2, ng, w_in, qkn, hcw, hcb, hbias, rde, scw, wb, wo, wfi, wfo):
    B, L, _ = x.shape
    dt = x.dtype
    xn = rms_norm(x, ng[0])
    p = xn @ w_in

    q = p[..., 0:ATT_Q_W].reshape(B, L, ATT_Q_HEADS, HEAD_DIM)
    k = p[..., A_K_OFF:A_V_OFF].reshape(B, L, ATT_KV_HEADS, HEAD_DIM)
    v = p[..., A_V_OFF:HY_OFF].reshape(B, L, ATT_KV_HEADS, HEAD_DIM)
    q = apply_rope(rms_norm(q, qkn[0]), cos, sin)
    k = apply_rope(rms_norm(k, qkn[1]), cos, sin)
    out_a = block_attention(q, k, v)

    u = conv3(p[..., HY_OFF:RET_OFF], hcw, hcb)
    x0 = u[..., 0:HY_WIDTH]
    x1 = u[..., HY_WIDTH:2 * HY_WIDTH]
    hv = u[..., 2 * HY_WIDTH:3 * HY_WIDTH]
    z = (hv * x1).astype(F32)
    z = long_conv(z, k2) + z * hbias.astype(F32)
    out_b = (x0.astype(F32) * z).astype(dt)

    rq = p[..., RET_OFF:RET_OFF + RET_W].reshape(B, L, RET_HEADS, HEAD_DIM)
    rk = p[..., RET_OFF + RET_W:RET_OFF + 2 * RET_W].reshape(B, L, RET_HEADS, HEAD_DIM)
    rv = p[..., RET_OFF + 2 * RET_W:RET_OFF + 3 * RET_W].reshape(B, L, RET_HEADS, HEAD_DIM)
    rg = p[..., RET_OFF + 3 * RET_W:SC_OFF]
    rq = apply_rope(rq, cos, sin).astype(F32)
    rk = apply_rope(rk, cos, sin).astype(F32) * (HEAD_DIM ** -0.5)
    rv = rv.astype(F32)
    log_g = jnp.log1p(-jnp.exp2(-rde.astype(F32)))
    fwd = retention_scan(rq, rk, rv, log_g[0], True)
    bwd = jnp.flip(retention_scan(jnp.flip(rq, 1), jnp.flip(rk, 1), jnp.flip(rv, 1), log_g[1], False), 1)
    ret = rms_norm(fwd + bwd).reshape(B, L, RET_W).astype(dt)
    out_c = ret * jax.nn.silu(rg)

    sb = p[..., SC_OFF:SC_OFF + SC_WIDTH]
    sc = p[..., SC_OFF + SC_WIDTH:SC_OFF + 2 * SC_WIDTH]
    sh = p[..., SC_OFF + 2 * SC_WIDTH:GATE_OFF]
    out_d = sb * conv3(sc * sh, scw)

    merged = None
    for n, br in enumerate((out_a, out_b, out_c, out_d)):
        gate = jax.nn.sigmoid(p[..., GATE_OFF + n * D_MODEL:GATE_OFF + (n + 1) * D_MODEL])
        term = gate * (br @ wb[n])
        merged = term if merged is None else merged + term
    h = x + rms_norm(merged @ wo, ng[1])

    gu = rms_norm(h, ng[2]) @ wfi
    f = (jax.nn.silu(gu[..., :D_FF]) * gu[..., D_FF:]) @ wfo
    return h + rms_norm(f, ng[3])


def trunk(x, norm_gains, w_in, qk_norm, hy_conv_w, hy_conv_b, hy_w1, hy_b1, hy_w2, hy_b2, hy_w3,
          hy_freq, hy_bias, ret_decay_exp, sc_conv_w, w_branch, w_out, w_ffn_in, w_ffn_out):
    L = x.shape[1]
    cos, sin = axial_rope_tables(L)
    for l in range(DEPTH):
        k2 = hyena_filter(L, hy_w1[l], hy_b1[l], hy_w2[l], hy_b2[l], hy_w3[l], hy_freq[l])
        x = layer(x, cos, sin, k2, norm_gains[l], w_in[l], qk_norm[l], hy_conv_w[l], hy_conv_b[l],
                  hy_bias[l], ret_decay_exp[l], sc_conv_w[l], w_branch[l], w_out[l], w_ffn_in[l], w_ffn_out[l])
    return x


def setup_inputs(seed: int = 0) -> dict:
    key = jax.random.key(seed)
    ks = jax.random.split(key, 22)
    nrm = lambda k, shape, s: jax.random.normal(k, shape, F32) * s
    return {
        'x_prompt': nrm(ks[0], (BATCH, SEQ, D_MODEL), 1.0),
        'x_sample': nrm(ks[1], (DEC_BATCH, DEC_SEQ, D_MODEL), 1.0),
        'norm_gains': 1.0 + nrm(ks[2], (DEPTH, 4, D_MODEL), 0.01),
        'w_in': nrm(ks[3], (DEPTH, D_MODEL, IN_COLS), D_MODEL ** -0.5),
        'qk_norm': 1.0 + nrm(ks[4], (DEPTH, 2, HEAD_DIM), 0.01),
        'hy_conv_w': nrm(ks[5], (DEPTH, 3, 3 * HY_WIDTH), 3 ** -0.5),
        'hy_conv_b': nrm(ks[6], (DEPTH, 3 * HY_WIDTH), 0.01),
        'hy_w1': nrm(ks[7], (DEPTH, HY_EMB, HY_FILTER_HIDDEN), HY_EMB ** -0.5),
        'hy_b1': nrm(ks[8], (DEPTH, HY_FILTER_HIDDEN), 0.01),
        'hy_w2': nrm(ks[9], (DEPTH, HY_FILTER_HIDDEN, HY_FILTER_HIDDEN), HY_FILTER_HIDDEN ** -0.5),
        'hy_b2': nrm(ks[10], (DEPTH, HY_FILTER_HIDDEN), 0.01),
        'hy_w3': nrm(ks[11], (DEPTH, HY_FILTER_HIDDEN, 2 * HY_WIDTH), HY_FILTER_HIDDEN ** -0.5),
        'hy_freq': 1.0 + nrm(ks[12], (DEPTH, 2, HY_FILTER_HIDDEN), 0.01),
        'hy_bias': nrm(ks[13], (DEPTH, HY_WIDTH), 0.1),
        'ret_decay_exp': 5.0 + jnp.arange(RET_HEADS, dtype=F32)[None, None, :] + nrm(ks[14], (DEPTH, 2, RET_HEADS), 0.1),
        'sc_conv_w': nrm(ks[15], (DEPTH, 3, SC_WIDTH), 3 ** -0.5),
        'w_branch': nrm(ks[16], (DEPTH, N_BRANCH, BRANCH_W, D_MODEL), BRANCH_W ** -0.5),
        'w_out': nrm(ks[17], (DEPTH, D_MODEL, D_MODEL), D_MODEL ** -0.5),
        'w_ffn_in': nrm(ks[18], (DEPTH, D_MODEL, 2 * D_FF), D_MODEL ** -0.5),
        'w_ffn_out': nrm(ks[19], (DEPTH, D_FF, D_MODEL), D_FF ** -0.5),
    }


def reference(x_prompt, x_sample, norm_gains, w_in, qk_norm, hy_conv_w, hy_conv_b, hy_w1, hy_b1, hy_w2, hy_b2,
              hy_w3, hy_freq, hy_bias, ret_decay_exp, sc_conv_w, w_branch, w_out, w_ffn_in, w_ffn_out):
    y_prompt = trunk(x_prompt, norm_gains, w_in, qk_norm, hy_conv_w, hy_conv_b, hy_w1, hy_b1, hy_w2, hy_b2,
                     hy_w3, hy_freq, hy_bias, ret_decay_exp, sc_conv_w, w_branch, w_out, w_ffn_in, w_ffn_out)
    y_sample = trunk(x_sample, norm_gains, w_in, qk_norm, hy_conv_w, hy_conv_b, hy_w1, hy_b1, hy_w2, hy_b2,
                     hy_w3, hy_freq, hy_bias, ret_decay_exp, sc_conv_w, w_branch, w_out, w_ffn_in, w_ffn_out)
    return (y_prompt, y_sample)
```

```python
import math
import numpy as np
import concourse.bass as bass
import concourse.mybir as mybir
from concourse.bass_utils import run_bass_kernel_spmd

F32 = mybir.dt.float32
BF16 = mybir.dt.bfloat16
AF = mybir.ActivationFunctionType
ALU = mybir.AluOpType
AX = mybir.AxisListType

D = 1024
DEPTH = 2
DFF = 2816
NCORE = 8
HD = 64
EPS = 1e-6
ENGS = ("pe", "act", "dve", "pool", "sp")
import os as _os
SAME_ENGINE_SYNC = _os.environ.get("SES", "1") == "1"


class Buf:
    __slots__ = ("name", "w", "r_eng", "r_dma")

    def __init__(self, name=""):
        self.name = name
        self.w = None
        self.r_eng = {}
        self.r_dma = []


class _Op:
    __slots__ = ("eng", "fn", "deps", "dma", "signal", "sigval", "ring", "ringval", "ringprev")

    def __init__(self, eng, fn, deps, dma):
        self.eng = eng
        self.fn = fn
        self.deps = deps
        self.dma = dma
        self.signal = False
        self.sigval = 0
        self.ring = None
        self.ringval = 0
        self.ringprev = 0


class MK:
    def __init__(self, nc, ring_k=8):
        self.nc = nc
        self.ops = []
        self.eng_ops = {e: [] for e in ENGS}
        self.ring_k = ring_k
        self.ring_use = {}
        self.ring_next = {e: 0 for e in ENGS}
        self.world = Buf("world")

    def op(self, eng, fn, reads=(), writes=(), dma=False, barrier=False):
        idx = len(self.ops)
        deps = set()
        reads = list(reads)
        writes = list(writes)
        if barrier:
            writes.append(self.world)
        else:
            reads.append(self.world)
        for b in reads:
            if b.w is not None:
                deps.add(b.w)
        for b in writes:
            if b.w is not None:
                deps.add(b.w)
            deps.update(b.r_eng.values())
            deps.update(b.r_dma)
        o = _Op(eng, fn, deps, dma)
        if dma:
            slot = self.ring_next[eng]
            self.ring_next[eng] = (slot + 1) % self.ring_k
            key = (eng, slot)
            u = self.ring_use.get(key, 0)
            o.ring = key
            o.ringprev = 16 * u
            o.ringval = 16 * (u + 1)
            self.ring_use[key] = u + 1
        self.ops.append(o)
        self.eng_ops[eng].append(idx)
        for b in reads:
            if dma:
                b.r_dma.append(idx)
            else:
                b.r_eng[eng] = idx
        for b in writes:
            b.w = idx
            b.r_eng = {}
            b.r_dma = []
        return idx

    def emit(self):
        nc = self.nc
        ops = self.ops
        esem = {e: nc.alloc_semaphore(name=f"es_{e}") for e in ENGS}
        rsem = {key: nc.alloc_semaphore(name=f"rs_{key[0]}_{key[1]}") for key in self.ring_use}

        def needs_sync(p, c):
            if p.dma:
                return True
            if p.eng == c.eng:
                if p.eng == "pe":
                    return False
                return SAME_ENGINE_SYNC
            return True

        for c in ops:
            for d in c.deps:
                p = ops[d]
                if (not p.dma) and needs_sync(p, c):
                    p.signal = True
        cnt = {e: 0 for e in ENGS}
        for o in ops:
            if (not o.dma) and o.signal:
                cnt[o.eng] += 1
                o.sigval = cnt[o.eng]
        known = {e: {} for e in ENGS}
        waits_of = [None] * len(ops)
        for idx, c in enumerate(ops):
            w = {}
            for d in c.deps:
                p = ops[d]
                if not needs_sync(p, c):
                    continue
                if p.dma:
                    s, v = ("r", p.ring), p.ringval
                else:
                    s, v = ("e", p.eng), p.sigval
                if v > w.get(s, 0):
                    w[s] = v
            if c.dma and c.ringprev > 0:
                s = ("r", c.ring)
                if c.ringprev > w.get(s, 0):
                    w[s] = c.ringprev
            kn = known[c.eng]
            lst = []
            for s, v in w.items():
                if v > kn.get(s, 0):
                    kn[s] = v
                    lst.append((s, v))
            waits_of[idx] = lst
        self.n_waits = sum(len(v) for v in waits_of)
        final_ring = {key: 16 * u for key, u in self.ring_use.items()}

        def semh(s):
            return rsem[s[1]] if s[0] == "r" else esem[s[1]]

        def replay(ename, eng):
            for idx in self.eng_ops[ename]:
                o = ops[idx]
                for s, v in waits_of[idx]:
                    eng.wait_ge(semh(s), v)
                ins = o.fn(eng)
                if o.dma:
                    ins.then_inc(rsem[o.ring], 16)
                elif o.signal:
                    ins.then_inc(esem[ename], 1)
            if ename == "sp":
                for key, v in final_ring.items():
                    eng.wait_ge(rsem[key], v)

        with nc.Block() as block:
            @block.tensor
            def _(e):
                replay("pe", e)

            @block.scalar
            def _(e):
                replay("act", e)

            @block.vector
            def _(e):
                replay("dve", e)

            @block.gpsimd
            def _(e):
                replay("pool", e)

            @block.sync
            def _(e):
                replay("sp", e)


class T:
    __slots__ = ("h", "b", "F", "h_off")

    def __init__(self, h, name=""):
        self.h = h
        self.b = Buf(name)
        sh = list(h.shape)
        f = 1
        for s in sh[1:]:
            f *= s
        self.F = f

    def __getitem__(self, k):
        return self.h[k]

    def ap(self, off, dims, p0=0, np_=128):
        return bass.AP(self.h, p0 * self.F + off, [[self.F, np_]] + [list(d) for d in dims])


def _dsize(dt):
    return 4 if dt == F32 else 2


class Arena:
    def __init__(self, nc, base, top):
        self.nc = nc
        self.base = base
        self.top = top
        self.off = base
        self.n = 0

    def mark(self):
        return self.off

    def reset(self, to=None):
        self.off = self.base if to is None else to

    def alloc(self, name, shape, dt):
        nb = _dsize(dt)
        for s in shape[1:]:
            nb *= s
        off = (self.off + 31) // 32 * 32
        assert off + nb <= self.top, f"SBUF arena overflow allocating {name}: {off}+{nb} > {self.top}"
        self.n += 1
        h = self.nc.alloc_sbuf_tensor_at(f"{name}_{self.n}", list(shape), dt, offset=off)
        self.off = off + nb
        t = T(h, name)
        t.h_off = off
        return t


class Cfg:
    def __init__(self, LS=8192, LP=2048, NP=2, depth=DEPTH):
        self.LS, self.LP, self.NP, self.depth = LS, LP, NP, depth
        self.seqs = [(0, LS)] + [(LS + i * LP, LP) for i in range(NP)]
        self.NT = LS + NP * LP
        self.groups = [(LS, [0])] + ([(LP, list(range(1, 1 + NP)))] if NP else [])
        self.LMAX = max(LS, LP)


def host_tables(cfg):
    tabs = {}
    L = cfg.LMAX
    t = np.arange(L)
    r = (t // 64).astype(np.float32)
    c = (t % 64).astype(np.float32)
    inv = (10000.0 ** (-np.arange(16, dtype=np.float32) / 16)).astype(np.float32)
    ang = np.stack([r[:, None] * inv, c[:, None] * inv], axis=1).astype(np.float32)
    cs, sn = np.cos(ang).astype(np.float32), np.sin(ang).astype(np.float32)
    C = np.stack([cs, cs], axis=2)
    S = np.stack([-sn, sn], axis=2)
    tabs["ropeC"] = np.ascontiguousarray(C.reshape(L, 64))
    tabs["ropeS"] = np.ascontiguousarray(S.reshape(L, 64))
    tabs["ident"] = np.eye(128, dtype=np.float32)
    for gi, (Lg, _) in enumerate(cfg.groups):
        m = np.arange(2 * Lg)
        n = np.abs(m - Lg)
        n = np.minimum(n, Lg - 1)
        tt = np.linspace(0.0, 1.0, Lg, dtype=np.float32)
        f = np.linspace(1e-4, 15.0, 16, dtype=np.float32)
        angf = ((2.0 * math.pi / Lg) * np.arange(Lg, dtype=np.float32)[:, None] * f[None, :]).astype(np.float32)
        z = np.concatenate([tt[:, None], np.cos(angf), -np.sin(angf)], axis=-1).astype(np.float32)
        tabs[f"zext{gi}"] = np.ascontiguousarray(z[n].T)
        tabs[f"trow{gi}"] = np.ascontiguousarray(tt[n][None, :])
    deltas = np.abs(np.linspace(math.log(1e-2) / 1.5, math.log(1e-2) / 0.3, 256, dtype=np.float32))
    tabs["negdelta"] = np.ascontiguousarray((-deltas).reshape(2, 128).T.astype(np.float32))
    i = np.arange(128, dtype=np.float32)
    diff = i[None, :] - i[:, None]
    tabs["dpos"] = np.maximum(diff, 0).astype(np.float32)
    tabs["dneg"] = np.maximum(-diff, 0).astype(np.float32)
    tabs["iqf"] = np.tile((i + 1)[None, :], (128, 1)).astype(np.float32)
    tabs["iqb"] = np.tile((128 - i)[None, :], (128, 1)).astype(np.float32)
    tabs["jk"] = np.stack([127 - i, i], axis=1).astype(np.float32)
    return tabs


DEBUG_OUT = set()


def build(cfg):
    nc = bass.Bass("TRN2", target_bir_lowering=False)
    mk = MK(nc)
    NT, LS = cfg.NT, cfg.LS
    depth = cfg.depth
    NTILE = NT // 512

    def din(name, shape, dt=F32):
        return nc.dram_tensor(name, list(shape), dt, kind="ExternalInput")

    def dscr(name, shape, dt):
        if name in DEBUG_OUT:
            return T(nc.dram_tensor(name, list(shape), dt, kind="ExternalOutput"), name)
        return T(nc.dram_tensor(name, list(shape), dt), name)

    xin = T(din("xin", [NT, D]), "xin")
    yout = T(nc.dram_tensor("yout", [NT, D], F32, kind="ExternalOutput"), "yout")
    w_in = din("w_in", [depth, D, 7168])
    w_br = din("w_branch", [depth, 4, 256, D])
    w_o = din("w_out", [depth, D, D])
    w_fi = din("w_ffn_in", [depth, D, 2 * DFF])
    w_fo = din("w_ffn_out", [depth, DFF, D])
    ngain = din("norm_gains", [depth, 4, D])
    qkn = din("qk_norm", [depth, 128])
    convp = din("convp", [depth, 128, 6, 4])
    scwp = din("scwp", [depth, 128, 2, 3])
    hbiasp = din("hbiasp", [depth, 128, 2])
    hy_w1 = din("hy_w1", [depth, 33, 64])
    hy_w2 = din("hy_w2", [depth, 64, 64])
    hy_w3 = din("hy_w3", [depth, 64, 512])
    hyvec = din("hyvec", [depth, 64, 4])
    rde = din("rde", [depth, 8])
    tabs = host_tables(cfg)
    tin = {k: din(k, v.shape) for k, v in tabs.items()}

    x1buf = dscr("x1buf", [NT, D], F32)
    hbuf = dscr("hbuf", [NT, D], F32)
    xnT_s = dscr("xnT_s", [D, NT], BF16)
    hnT_s = dscr("hnT_s", [D, NT], BF16)
    brT_s = [dscr(f"brT{n}", [256, NT], BF16) for n in range(4)]
    qT_s = dscr("qT_s", [256, NT], BF16)
    kT_s = dscr("kT_s", [128, NT], BF16)
    v_s = dscr("v_s", [NT, 130], BF16)
    ret_s = dscr("ret_s", [NT, 1024], BF16)
    x0T_s = dscr("x0T_s", [256, NT], BF16)
    zT_s = dscr("zT_s", [256, NT], BF16)
    zr_s = dscr("zr_s", [64, NT // 64, 256], BF16)
    kfl = [dscr(f"kfl{gi}", [256, 2 * Lg], BF16) for gi, (Lg, _) in enumerate(cfg.groups)]

    ar = Arena(nc, 18432, 229344)
    ident = ar.alloc("ident", [128, 128], BF16)
    epsT = ar.alloc("eps", [128, 1], F32)
    gT = ar.alloc("gains", [128, 4, D], F32)
    gqk = ar.alloc("gqk", [128, 6, 64], F32)
    cvp = ar.alloc("cvp", [128, 6, 4], F32)
    scw = ar.alloc("scw", [128, 2, 3], F32)
    hbi = ar.alloc("hbi", [128, 2], F32)
    ndl = ar.alloc("ndl", [128, 2], F32)
    rnrm = [ar.alloc(f"rnrm{gi}", [128, 2], F32) for gi in range(len(cfg.groups))]
    Dm = ar.alloc("Dm", [128, 4, 128], F32)
    wqf = ar.alloc("wqf", [128, 2, 128], F32)
    wqb = ar.alloc("wqb", [128, 2, 128], F32)
    wkk = ar.alloc("wkk", [128, 4, 2], F32)
    gch = ar.alloc("gch", [128, 2, 2], F32)
    PERSIST = ar.mark()
    ar.base = PERSIST

    PF = [T(nc.alloc_psum_tensor(f"pf{i}", [128, 512], F32), f"pf{i}") for i in range(6)]
    PT = [T(nc.alloc_psum_tensor(f"pt{i}", [128, 1024], BF16), f"pt{i}") for i in range(2)]
    rot = {"pf": 0, "pt": 0}

    def next_pf(n=6, base=0):
        i = rot["pf"] % n + base
        rot["pf"] += 1
        return PF[i]

    def next_pt():
        i = rot["pt"] % 2
        rot["pt"] += 1
        return PT[i]

    def bl(x):
        return [t.b if isinstance(t, T) else t for t in x]

    def mm(out, lhsT, rhs, start, stop, R, W, skip=False):
        mk.op("pe", lambda e: e.matmul(out, lhsT=lhsT, rhs=rhs, start=start, stop=stop, skip_group_check=skip), bl(R), bl(W))

    def tr(out, in_, R, W):
        idn = ident[:]
        mk.op("pe", lambda e: e.transpose(out=out, in_=in_, identity=idn), bl(R) + [ident.b], bl(W))

    def act(out, in_, func, R, W, scale=1.0, bias=None, accum=None):
        def f(e):
            kw = {}
            if bias is not None:
                kw["bias"] = bias
            if accum is not None:
                kw["accum_out"] = accum
            return e.activation(out=out, in_=in_, func=func, scale=scale, **kw)
        mk.op("act", f, bl(R), bl(W))

    def tt(eng, out, in0, in1, op, R, W):
        mk.op(eng, lambda e: e.tensor_tensor(out=out, in0=in0, in1=in1, op=op), bl(R), bl(W))

    def ts(eng, out, in0, s1, s2, op0, op1, R, W):
        if s2 is None:
            mk.op(eng, lambda e: e.tensor_scalar(out=out, in0=in0, scalar1=s1, scalar2=None, op0=op0), bl(R), bl(W))
        else:
            mk.op(eng, lambda e: e.tensor_scalar(out=out, in0=in0, scalar1=s1, scalar2=s2, op0=op0, op1=op1), bl(R), bl(W))

    def stt(eng, out, in0, sc, in1, op0, op1, R, W):
        mk.op(eng, lambda e: e.scalar_tensor_tensor(out=out, in0=in0, scalar=sc, in1=in1, op0=op0, op1=op1), bl(R), bl(W))

    def cp(eng, out, in_, R, W):
        if eng == "act":
            act(out, in_, AF.Copy, R, W)
        else:
            mk.op(eng, lambda e: e.tensor_copy(out=out, in_=in_), bl(R), bl(W))

    def red(eng, out, in_, R, W):
        mk.op(eng, lambda e: e.tensor_reduce(out=out, in_=in_, axis=AX.X, op=ALU.add), bl(R), bl(W))

    def rcp(out, in_, R, W):
        mk.op("dve", lambda e: e.reciprocal(out=out, in_=in_), bl(R), bl(W))

    def mset(eng, out, val, W):
        mk.op(eng, lambda e: e.memset(out, val), [], bl(W))

    def dma(q, out, in_, R, W):
        mk.op(q, lambda e: e.dma_start(out=out, in_=in_), bl(R), bl(W), dma=True)

    def barrier():
        mk.op("pool", lambda e: e.memset(epsT[:, 0:1], EPS), [], [epsT.b], barrier=True)

    def rstd_from_ss(ss, n, R):
        act(ss, ss, AF.Sqrt, R, R, scale=1.0 / n, bias=epsT[:, 0:1])
        rcp(ss, ss, R, R)

    mset("pool", epsT[:], EPS, [epsT])
    dma("pool", ident[:], tin["ident"].ap(), [], [ident])
    dma("sp", ndl[:], tin["negdelta"].ap(), [], [ndl])

    def phase_A(l, src):
        ar.reset()
        xt = [ar.alloc("xt", [128, D], F32) for _ in range(3)]
        junk = ar.alloc("junk", [128, D], BF16)
        ssA = [ar.alloc("ss", [128, 1], F32) for _ in range(3)]
        xn = [ar.alloc("xn", [128, D], BF16) for _ in range(2)]
        xT = [ar.alloc("xT", [128, 8, 512], BF16) for _ in range(2)]
        for ti in range(NTILE):
            xTt = xT[ti % 2]
            for s in range(4):
                i = ti * 4 + s
                x_, ss_, xn_ = xt[i % 3], ssA[i % 3], xn[i % 2]
                dma("sp", x_[:], src.h.ap()[i * 128:(i + 1) * 128, :], [src], [x_])
                act(junk[:], x_[:], AF.Square, [x_], [junk, ss_], accum=ss_[:])
                rstd_from_ss(ss_[:], D, [ss_])
                stt("dve", xn_[:], x_[:], ss_[:, 0:1], gT[:, 0, :], ALU.mult, ALU.mult, [x_, ss_, gT], [xn_])
                pt = next_pt()
                for k in range(8):
                    tr(pt[:, k * 128:(k + 1) * 128], xn_[:, k * 128:(k + 1) * 128], [xn_], [pt])
                cp("act" if s % 2 else "dve", xTt[:, :, s * 128:(s + 1) * 128],
                   pt[:].rearrange("p (k t) -> p k t", t=128), [pt], [xTt])
            dma("pool", xnT_s.h.ap().rearrange("(k p) t -> p k t", p=128)[:, :, ti * 512:(ti + 1) * 512], xTt[:], [xTt], [xnT_s])

    def load_layer_consts(l):
        for j in range(4):
            dma("sp", gT[:, j, :], bass.AP(ngain, (l * 4 + j) * D, [[0, 128], [1, D]]), [], [gT])
        for h in range(6):
            off = l * 128 + (0 if h < 4 else 64)
            dma("sp", gqk[:, h, :], bass.AP(qkn, off, [[0, 128], [1, 64]]), [], [gqk])
        dma("sp", cvp[:], convp.ap()[l], [], [cvp])
        dma("sp", scw[:], scwp.ap()[l], [], [scw])
        dma("sp", hbi[:], hbiasp.ap()[l], [], [hbi])

    def phase_R(l):
        ar.reset()
        rd = ar.alloc("rd", [128, 8], F32)
        lg = ar.alloc("lg", [128, 8], F32)
        tmp = ar.alloc("tmpR", [128, 128], F32)
        dpos = ar.alloc("dpos", [128, 128], F32)
        dneg = ar.alloc("dneg", [128, 128], F32)
        iqf = ar.alloc("iqf", [128, 128], F32)
        iqb = ar.alloc("iqb", [128, 128], F32)
        jk = ar.alloc("jk", [128, 2], F32)
        lsel = ar.alloc("lsel", [128, 2, 2], F32)
        dma("sp", rd[:], bass.AP(rde, l * 8, [[0, 128], [1, 8]]), [], [rd])
        dma("sp", dpos[:], tin["dpos"].ap(), [], [dpos])
        dma("sp", dneg[:], tin["dneg"].ap(), [], [dneg])
        dma("sp", iqf[:], tin["iqf"].ap(), [], [iqf])
        dma("sp", iqb[:], tin["iqb"].ap(), [], [iqb])
        dma("sp", jk[:], tin["jk"].ap(), [], [jk])
        act(lg[:], rd[:], AF.Exp, [rd], [lg], scale=-math.log(2.0))
        act(lg[:], lg[:], AF.Ln, [lg], [lg], scale=-1.0, bias=1.0)
        for h in range(4):
            ts("dve", tmp[:], dpos[:], lg[:, h:h + 1], None, ALU.mult, None, [dpos, lg], [tmp])
            stt("dve", tmp[:], dneg[:], lg[:, 4 + h:5 + h], tmp[:], ALU.mult, ALU.add, [dneg, lg, tmp], [tmp])
            act(Dm[:, h, :], tmp[:], AF.Exp, [tmp], [Dm])
            act(wkk[:, h, 0:1], jk[:, 0:1], AF.Exp, [jk, lg], [wkk], scale=lg[:, h:h + 1])
            act(wkk[:, h, 1:2], jk[:, 1:2], AF.Exp, [jk, lg], [wkk], scale=lg[:, 4 + h:5 + h])
        for p in range(2):
            for half in range(2):
                h = 2 * p + half
                r0, r1 = half * 64, half * 64 + 64
                for d in range(2):
                    cp("dve", lsel[r0:r1, p, d:d + 1], lg[r0:r1, 4 * d + h:4 * d + h + 1], [lg], [lsel])
            act(wqf[:, p, :], iqf[:], AF.Exp, [iqf, lsel], [wqf], scale=lsel[:, p, 0:1])
            act(wqb[:, p, :], iqb[:], AF.Exp, [iqb, lsel], [wqb], scale=lsel[:, p, 1:2])
            act(gch[:, p, :], lsel[:, p, :], AF.Exp, [lsel], [gch], scale=128.0)

    def phase_F(l, gi):
        Lg, _ = cfg.groups[gi]
        ar.reset()
        NCH = (2 * Lg) // 512
        w1 = ar.alloc("w1", [33, 64], F32)
        w2 = ar.alloc("w2", [64, 64], F32)
        w3 = ar.alloc("w3", [64, 512], F32)
        hv = ar.alloc("hv", [64, 4], F32)
        ssq = ar.alloc("ssq", [128, 2, NCH], F32)
        ze = [ar.alloc("ze", [33, 512], F32) for _ in range(2)]
        trw = [ar.alloc("trw", [128, 512], F32) for _ in range(2)]
        a1 = [ar.alloc("a1", [64, 512], F32) for _ in range(2)]
        tw = [ar.alloc("tw", [64, 512], F32) for _ in range(2)]
        h1 = [ar.alloc("h1", [64, 512], F32) for _ in range(2)]
        dec = [ar.alloc("dec", [128, 512], F32) for _ in range(2)]
        kf = [ar.alloc("kf", [128, 512], F32) for _ in range(2)]
        kb = [ar.alloc("kb", [128, 512], BF16) for _ in range(4)]
        jk2 = ar.alloc("jk2", [128, 512], BF16)
        dma("sp", w1[:], hy_w1.ap()[l], [], [w1])
        dma("sp", w2[:], hy_w2.ap()[l], [], [w2])
        dma("sp", w3[:], hy_w3.ap()[l], [], [w3])
        dma("sp", hv[:], hyvec.ap()[l], [], [hv])
        mset("pool", ssq[:], 0.0, [ssq])
        PI, TWO_PI = math.pi, 2 * math.pi

        def sin_layer(ps, dst, bcol, fcol, a_, t_):
            ts("dve", a_[:], ps[0:64, :], hv[:, bcol:bcol + 1], hv[:, fcol:fcol + 1], ALU.add, ALU.mult, [hv, ps], [a_])
            for _ in range(1):
                ts("dve", t_[:], a_[:], PI, None, ALU.is_gt, None, [a_], [t_])
                stt("dve", a_[:], t_[:], -TWO_PI, a_[:], ALU.mult, ALU.add, [t_, a_], [a_])
                ts("dve", t_[:], a_[:], -PI, None, ALU.is_lt, None, [a_], [t_])
                stt("dve", a_[:], t_[:], TWO_PI, a_[:], ALU.mult, ALU.add, [t_, a_], [a_])
            act(dst[:], a_[:], AF.Sin, [a_], [dst])

        for ch in range(NCH):
            m0 = ch * 512
            ze_, tr_, a_, t_, h_ = ze[ch % 2], trw[ch % 2], a1[ch % 2], tw[ch % 2], h1[ch % 2]
            dma("sp", ze_[:], tin[f"zext{gi}"].ap()[:, m0:m0 + 512], [], [ze_])
            dma("sp", tr_[:], bass.AP(tin[f"trow{gi}"], m0, [[0, 128], [1, 512]]), [], [tr_])
            p1 = next_pf()
            mm(p1[0:64, :], w1[:], ze_[:], True, True, [w1, ze_], [p1])
            sin_layer(p1, h_, 0, 2, a_, t_)
            p2 = next_pf()
            mm(p2[0:64, :], w2[:], h_[:], True, True, [w2, h_], [p2])
            sin_layer(p2, h_, 1, 3, a_, t_)
            dirn = 1 if m0 < Lg else 0
            for cc in range(2):
                p3 = next_pf()
                c0 = dirn * 256 + cc * 128
                mm(p3[:], w3[:, c0:c0 + 128], h_[:], True, True, [w3, h_], [p3])
                d_, k_, kb_ = dec[cc], kf[cc], kb[(ch * 2 + cc) % 4]
                act(d_[:], tr_[:], AF.Exp, [tr_, ndl], [d_], scale=ndl[:, cc:cc + 1])
                tt("dve", k_[:], p3[:], d_[:], ALU.mult, [p3, d_], [k_])
                if ch == 0:
                    mset("pool", k_[:, 0:1], 0.0, [k_])
                act(jk2[:], k_[:], AF.Square, [k_], [jk2, ssq], accum=ssq[:, cc, ch:ch + 1])
                cp("pool", kb_[:], k_[:], [k_], [kb_])
                dma("pool", kfl[gi].h.ap()[cc * 128:(cc + 1) * 128, m0:m0 + 512], kb_[:], [kb_], [kfl[gi]])
        red("dve", rnrm[gi][:], ssq[:], [ssq], [rnrm[gi]])
        rstd_from_ss(rnrm[gi][:], 1.0, [rnrm[gi]])

    def phase_1b(l):
        ar.reset()
        W1 = ar.alloc("W1", [128, 8, 3072], BF16)
        for k in range(8):
            dma("pool", W1[:, k, :], w_in.ap()[l, k * 128:(k + 1) * 128, 0:3072], [], [W1])
        xe = [ar.alloc("xe", [128, 8, 514], BF16) for _ in range(2)]
        Ct = [ar.alloc("Ct", [128, 4, 8, 64], F32) for _ in range(2)]
        St = [ar.alloc("St", [128, 4, 8, 64], F32) for _ in range(2)]
        qk32 = [ar.alloc("qk32", [128, 384], F32) for _ in range(2)]
        sq = ar.alloc("sq", [128, 384], F32)
        ss6 = [ar.alloc("ss6", [128, 6], F32) for _ in range(2)]
        t1 = [ar.alloc("t1", [128, 512], F32) for _ in range(2)]
        t2 = [ar.alloc("t2", [128, 512], F32) for _ in range(2)]
        qkbf = [ar.alloc("qkbf", [128, 384], BF16) for _ in range(2)]
        qkT = [ar.alloc("qkT", [128, 3, 512], BF16) for _ in range(2)]
        vt = [ar.alloc("vt", [128, 4, 2, 65], BF16) for _ in range(2)]
        r32 = [ar.alloc("r32", [128, 512], F32) for _ in range(2)]
        rtok = [ar.alloc("rtok", [128, 4, 1024], BF16) for _ in range(2)]
        pext = [ar.alloc("pext", [128, 514], F32) for _ in range(6)]
        uu = [ar.alloc("uu", [128, 512], F32) for _ in range(6)]
        z32 = [ar.alloc("z32", [128, 512], F32) for _ in range(2)]
        obf = [ar.alloc("obf", [128, 512], BF16) for _ in range(4)]
        zrv = [ar.alloc("zrv", [128, 512], BF16) for _ in range(2)]
        zst = [ar.alloc("zst", [64, 8, 256], BF16)] * 2
        gext = [ar.alloc("gext", [128, 514], F32) for _ in range(2)]
        for v_ in vt:
            mset("pool", v_[:], 1.0, [v_])
        PH = PF[5]
        it = [0]

        for ti in range(NTILE):
            t0 = ti * 512
            sidx = [i for i, (s0, L) in enumerate(cfg.seqs) if s0 <= t0 < s0 + L][0]
            s0, L = cfg.seqs[sidx]
            pos0 = t0 - s0
            xe_, Ct_, St_ = xe[ti % 2], Ct[ti % 2], St[ti % 2]
            lo = max(t0 - 1, s0)
            hi = min(t0 + 513, s0 + L)
            c_lo = lo - (t0 - 1)
            dma("sp", xe_[:, :, c_lo:c_lo + (hi - lo)], xnT_s.h.ap().rearrange("(k p) t -> p k t", p=128)[:, :, lo:hi], [xnT_s], [xe_])
            if c_lo > 0:
                mset("pool", xe_[:, :, 0:1], 0.0, [xe_])
            if hi < t0 + 513:
                mset("pool", xe_[:, :, 513:514], 0.0, [xe_])
            for s in range(4):
                dma("sp", Ct_[:, s, :, :], bass.AP(tin["ropeC"], (pos0 + s * 128) * 64, [[64, 128], [0, 8], [1, 64]]), [], [Ct_])
                dma("sp", St_[:, s, :, :], bass.AP(tin["ropeS"], (pos0 + s * 128) * 64, [[64, 128], [0, 8], [1, 64]]), [], [St_])
            qkT_, vt_, rtok_ = qkT[ti % 2], vt[ti % 2], rtok[ti % 2]

            def rope(src, nh, Cs, Ss, t1_, t2_):
                w = nh * 64
                tt("dve", t1_[:, 0:w], src[:, 0:w], Cs, ALU.mult, [src, Ct_], [t1_])
                sw = src.ap(16, [[32, nh * 2], [-16, 2], [1, 16]])
                tt("dve", t2_[:, 0:w].rearrange("p (a h f) -> p a h f", h=2, f=16), sw, Ss, ALU.mult, [src, St_], [t2_])

            deferred = []

            def flush():
                while deferred:
                    deferred.pop(0)()

            for s in range(4):
                j = it[0]
                it[0] += 1
                q32, ss_, t1_, t2_, qb_ = qk32[j % 2], ss6[j % 2], t1[j % 2], t2[j % 2], qkbf[j % 2]
                xs = slice(1 + s * 128, 1 + s * 128 + 128)
                pa = next_pf(5)
                for k in range(8):
                    mm(pa[:], xe_[:, k, xs], W1[:, k, 0:512], k == 0, k == 7, [xe_, W1], [pa])
                pr = next_pf(5)
                for k in range(8):
                    mm(pr[:], xe_[:, k, xs], W1[:, k, 1280:1792], k == 0, k == 7, [xe_, W1], [pr])
                pv = next_pf(5)
                for k in range(8):
                    mm(pv[:], xe_[:, k, xs], W1[:, k, 1792:2304], k == 0, k == 7, [xe_, W1], [pv])
                flush()
                act(q32[:], pa[:, 0:384], AF.Copy, [pa], [q32])
                act(vt_[:, s, :, 0:64], pa[:, 384:512].rearrange("p (g d) -> p g d", d=64), AF.Copy, [pa], [vt_])
                tt("dve", sq[:], q32[:], q32[:], ALU.mult, [q32], [sq])
                red("dve", ss_[:], sq[:].rearrange("p (h d) -> p h d", d=64), [sq], [ss_])
                rstd_from_ss(ss_[:], 64.0, [ss_])
                tt("dve", q32[:].rearrange("p (h d) -> p h d", d=64), q32[:].rearrange("p (h d) -> p h d", d=64),
                   ss_.ap(0, [[1, 6], [0, 64]]), ALU.mult, [q32, ss_], [q32])
                tt("dve", q32[:], q32[:], gqk[:].rearrange("p h d -> p (h d)"), ALU.mult, [q32, gqk], [q32])
                rope(q32, 6, Ct_[:, s, 0:6, :].rearrange("p h d -> p (h d)"),
                     St_[:, s, 0:6, :].rearrange("p h (a b f) -> p (h a) b f", a=2, b=2, f=16), t1_, t2_)
                tt("dve", qb_.ap(0, [[64, 2], [128, 2], [1, 64]]), t1_.ap(0, [[128, 2], [64, 2], [1, 64]]),
                   t2_.ap(0, [[128, 2], [64, 2], [1, 64]]), ALU.add, [t1_, t2_], [qb_])
                tt("dve", qb_[:, 256:384], t1_[:, 256:384], t2_[:, 256:384], ALU.add, [t1_, t2_], [qb_])

                def do_tr(qb_=qb_, s=s):
                    pt = next_pt()
                    for c3 in range(3):
                        tr(pt[:, c3 * 128:(c3 + 1) * 128], qb_[:, c3 * 128:(c3 + 1) * 128], [qb_], [pt])
                    cp("act", qkT_[:, :, s * 128:(s + 1) * 128], pt[:, 0:384].rearrange("p (c t) -> p c t", t=128), [pt], [qkT_])
                deferred.append(do_tr)
                r_, rt1, rt2 = r32[j % 2], t1[(j + 1) % 2], t2[(j + 1) % 2]
                act(r_[:, 0:256], pr[:, 0:256], AF.Copy, [pr], [r_])
                act(r_[:, 256:512], pr[:, 256:512], AF.Copy, [pr], [r_], scale=0.125)
                rope(r_, 8, Ct_[:, s, :, :].rearrange("p h d -> p (h d)"),
                     St_[:, s, :, :].rearrange("p h (a b f) -> p (h a) b f", a=2, b=2, f=16), rt1, rt2)
                tt("dve", rtok_[:, s, 0:512], rt1[:], rt2[:], ALU.add, [rt1, rt2], [rtok_])
                act(rtok_[:, s, 512:768], pv[:, 0:256], AF.Copy, [pv], [rtok_])
                act(rtok_[:, s, 768:1024], pv[:, 256:512], AF.Silu, [pv], [rtok_])

            def tail_dmas():
                dma("pool", qT_s.h.ap().rearrange("(c p) t -> p c t", p=128)[:, :, t0:t0 + 512], qkT_[:, 0:2, :], [qkT_], [qT_s])
                dma("pool", kT_s.h.ap()[:, t0:t0 + 512], qkT_[:, 2, :], [qkT_], [kT_s])
            deferred.append(tail_dmas)
            dma("pool", v_s.h.ap()[t0:t0 + 512, :].rearrange("(s p) c -> p s c", p=128), vt_[:].rearrange("p s g d -> p s (g d)"), [vt_], [v_s])
            dma("pool", ret_s.h.ap()[t0:t0 + 512, :].rearrange("(s p) c -> p s c", p=128), rtok_[:], [rtok_], [ret_s])

            def fm_chunk(col0, hslot, dst, halo):
                pm = next_pf(5)
                for k in range(8):
                    mm(pm[:], W1[:, k, col0:col0 + 128], xe_[:, k, 1:513], k == 0, k == 7, [xe_, W1], [pm])
                if halo:
                    for k in range(8):
                        mm(PH[:, hslot * 2:hslot * 2 + 2], W1[:, k, col0:col0 + 128], xe_.ap(k * 514, [[513, 2]]),
                           k == 0, k == 7, [xe_, W1], [PH])
                    cp("dve", dst.ap(0, [[513, 2]]), PH[:, hslot * 2:hslot * 2 + 2], [PH], [dst])
                act(dst[:, 1:513], pm[:], AF.Copy, [pm], [dst])

            def conv3(eng, dst, src, wcol, wt, bias):
                if bias is not None:
                    ts(eng, dst[:], src[:, 1:513], wt[:, wcol, 1:2], bias, ALU.mult, ALU.add, [src, wt], [dst])
                else:
                    ts(eng, dst[:], src[:, 1:513], wt[:, wcol, 1:2], None, ALU.mult, None, [src, wt], [dst])
                stt(eng, dst[:], src[:, 0:512], wt[:, wcol, 0:1], dst[:], ALU.mult, ALU.add, [src, wt, dst], [dst])
                stt(eng, dst[:], src[:, 2:514], wt[:, wcol, 2:3], dst[:], ALU.mult, ALU.add, [src, wt, dst], [dst])

            for jc in range(6):
                fm_chunk(512 + jc * 128, jc, pext[jc], True)
                if jc == 0:
                    flush()
                conv3("dve", uu[jc], pext[jc], jc, cvp, cvp[:, jc, 3:4])
            for jc in range(6):
                fm_chunk(2304 + jc * 128, 6 + jc, pext[jc], jc >= 2)
            for cc in range(2):
                ob = obf[cc]
                cp("pool", ob[:], uu[cc][:], [uu[cc]], [ob])
                dma("pool", x0T_s.h.ap()[cc * 128:(cc + 1) * 128, t0:t0 + 512], ob[:], [ob], [x0T_s])
                z_ = z32[cc]
                tt("pool", z_[:], uu[4 + cc][:], uu[2 + cc][:], ALU.mult, [uu[4 + cc], uu[2 + cc]], [z_])
                ob2 = obf[2 + cc]
                cp("pool", ob2[:], z_[:], [z_], [ob2])
                dma("pool", zT_s.h.ap()[cc * 128:(cc + 1) * 128, t0:t0 + 512], ob2[:], [ob2], [zT_s])
                zr_ = zrv[cc]
                cp("pool", zr_[:].rearrange("p (a b) -> p a b", b=64), z_.ap(63, [[64, 8], [-1, 64]]), [z_], [zr_])
                pt = next_pt()
                for b8 in range(8):
                    tr(pt[0:64, b8 * 128:(b8 + 1) * 128], zr_[:, b8 * 64:(b8 + 1) * 64], [zr_], [pt])
                zs_ = zst[ti % 2]
                cp("act", zs_[:, :, cc * 128:(cc + 1) * 128], pt[0:64, :].rearrange("p (a c) -> p a c", c=128), [pt], [zs_])
            dma("pool", zr_s.h.ap()[:, t0 // 64:t0 // 64 + 8, :], zst[ti % 2][:], [zst[ti % 2]], [zr_s])
            for cc in range(2):
                g_ = gext[cc]
                tt("pool", g_[:], pext[2 + cc][:], pext[4 + cc][:], ALU.mult, [pext[2 + cc], pext[4 + cc]], [g_])
                cv = uu[cc]
                conv3("dve", cv, g_, cc, scw, None)
                ob = obf[cc]
                tt("dve", ob[:], cv[:], pext[cc][:, 1:513], ALU.mult, [cv, pext[cc]], [ob])
                dma("pool", brT_s[3].h.ap()[cc * 128:(cc + 1) * 128, t0:t0 + 512], ob[:], [ob], [brT_s[3]])

    def phase_2A(sidx):
        s0, L = cfg.seqs[sidx]
        ar.reset()
        NKC = L // 128
        qT = ar.alloc("qT", [128, 2, L], BF16)
        kT = ar.alloc("kT", [128, L], BF16)
        V = ar.alloc("V", [128, NKC, 130], BF16)
        dma("sp", qT[:], qT_s.h.ap().rearrange("(c p) t -> p c t", p=128)[:, :, s0:s0 + L], [qT_s], [qT])
        dma("sp", kT[:], kT_s.h.ap()[:, s0:s0 + L], [kT_s], [kT])
        for v0 in range(0, NKC, 8):
            v1 = min(NKC, v0 + 8)
            dma("sp", V[:, v0:v1, :], v_s.h.ap()[s0 + v0 * 128:s0 + v1 * 128, :].rearrange("(n p) c -> p n c", p=128), [v_s], [V])
        Pt = [ar.alloc("Pt", [128, 512], BF16) for _ in range(3)]
        oa = [ar.alloc("oa", [128, 4, 256], BF16) for _ in range(2)]
        rden = [ar.alloc("rden", [128, 4], F32) for _ in range(2)]
        oaT = [ar.alloc("oaT", [128, 2, 512], BF16) for _ in range(2)]
        steps = [(qb, hh, kc) for qb in range(L // 512) for hh in range(4) for kc in range(NKC)]
        nst = len(steps)
        LOOK = 2

        def emit_ST(i):
            qb, hh, kc = steps[i]
            cq, g = hh // 2, hh % 2
            rows = slice(64 * g, 64 * g + 64)
            ps = PF[i % 4]
            mm(ps[:], kT[rows, kc * 128:(kc + 1) * 128], qT[rows, cq, qb * 512:(qb + 1) * 512], True, True, [kT, qT], [ps])

        for i in range(min(LOOK, nst)):
            emit_ST(i)
        for i in range(nst):
            if i + LOOK < nst:
                emit_ST(i + LOOK)
            qb, hh, kc = steps[i]
            cq, g = hh // 2, hh % 2
            head = 2 * g + cq
            po = PF[4 + (hh % 2)]
            ps = PF[i % 4]
            P_ = Pt[i % 3]
            oa_ = oa[qb % 2]
            act(P_[:], ps[:], AF.Exp, [ps], [P_], scale=0.125)
            for qs in range(4):
                mm(po[:, qs * 65:qs * 65 + 65], P_[:, qs * 128:(qs + 1) * 128], V[:, kc, g * 65:g * 65 + 65],
                   kc == 0 and qs == 0, kc == NKC - 1, [P_, V], [po], skip=True)
            if kc < NKC - 1:
                continue
            rd_ = rden[hh % 2]
            rcp(rd_[:], po.ap(64, [[65, 4]]), [po], [rd_])
            tt("dve", oa_[:, :, head * 64:(head + 1) * 64], po.ap(0, [[65, 4], [1, 64]]), rd_.ap(0, [[1, 4], [0, 64]]),
               ALU.mult, [po, rd_], [oa_])
            if hh < 3:
                continue
            oT = oaT[qb % 2]
            for qs in range(4):
                pt = next_pt()
                for c2 in range(2):
                    tr(pt[:, c2 * 128:(c2 + 1) * 128], oa_[:, qs, c2 * 128:(c2 + 1) * 128], [oa_], [pt])
                cp("dve", oT[:, :, qs * 128:(qs + 1) * 128], pt[:, 0:256].rearrange("p (c t) -> p c t", t=128), [pt], [oT])
            t0 = s0 + qb * 512
            dma("pool", brT_s[0].h.ap().rearrange("(c p) t -> p c t", p=128)[:, :, t0:t0 + 512], oT[:], [oT], [brT_s[0]])

    def phase_2B(gi):
        Lg, sl = cfg.groups[gi]
        ns = len(sl)
        A = Lg // 128
        A2 = Lg // 64
        NB, NB2 = ns * A, ns * A2
        ar.reset()
        WW = 2 * Lg - 64
        WWA = max(WW, NB2 * 128)
        Zr = ar.alloc("Zr", [128, 128, NB2], BF16)
        Wc = [ar.alloc("Wc", [128, WWA], BF16) for _ in range(3)]
        wb = {(i, cc): Buf(f"wb{i}{cc}") for i in range(3) for cc in range(2)}
        Ysb = [ar.alloc("Ysb", [128, 128, NB], BF16) for _ in range(2)]
        x0t = [ar.alloc("x0t", [128, 512], BF16) for _ in range(2)]
        zt = [ar.alloc("zt", [128, 512], BF16) for _ in range(2)]
        tmpc = [ar.alloc("tmpc", [128, 512], F32) for _ in range(2)]
        obb = [ar.alloc("obb", [128, 512], BF16) for _ in range(2)]
        gb0 = cfg.seqs[sl[0]][0] // 64
        ar.n += 1
        Zl = T(nc.alloc_sbuf_tensor_at(f"Zl_{ar.n}", [128, NB2, 128], BF16, offset=Wc[2].h_off), "Zl")
        zlb = [wb[(2, 0)], wb[(2, 1)]]
        for cc in range(2):
            dma("sp", Zl[cc * 64:(cc + 1) * 64, :, :], zr_s.h.ap()[:, gb0:gb0 + NB2, cc * 128:(cc + 1) * 128], [zr_s], [zlb[cc]])
        for q4 in range(4):
            cp("dve" if q4 % 2 else "pool", Zr[:, q4 * 32:(q4 + 1) * 32, :],
               Zl[:, :, q4 * 32:(q4 + 1) * 32].rearrange("p n c -> p c n"), zlb, [Zr])
        deltas = [0] + [d for d in range(-(2 * A - 1), 2 * A - 1) if d != 0]
        for cl in range(128):
            wi = cl % 3
            W_ = Wc[wi]
            dma("sp", W_[:, 0:WW], bass.AP(kfl[gi].h, cl * 2 * Lg + 1, [[128 * 2 * Lg, 2], [1, 64], [1, WW]]),
                [kfl[gi]], [wb[(wi, 0)], wb[(wi, 1)]])
            slot = cl % 8
            if slot == 0:
                py = [PF[(cl // 8) % 2], PF[2 + (cl // 8) % 2]]
            for di, d in enumerate(deltas):
                off = 64 * (d + 2 * A - 1)
                b0 = max(0, (d + 1) // 2)
                b1 = min(A - 1, (2 * A - 1 + d) // 2)
                n = b1 - b0 + 1
                a0 = 2 * b0 - d
                for cc in range(2):
                    rhs = Zr.ap(cl * NB2 + a0, [[A2, ns], [2, n]], cc * 64, 64)
                    out = py[cc].ap(slot * NB + b0, [[A, ns], [1, n]])
                    mm(out, W_[cc * 64:(cc + 1) * 64, off:off + 128], rhs, di == 0, di == len(deltas) - 1,
                       [wb[(wi, cc)], Zr], [py[cc]], skip=True)
            if slot == 7:
                for cc in range(2):
                    act(Ysb[cc][:, cl - 7:cl + 1, :], py[cc][:, 0:8 * NB].rearrange("p (c n) -> p c n", n=NB), AF.Copy, [py[cc]], [Ysb[cc]])
        blk = 0
        for cc in range(2):
            for b4 in range(NB // 4):
                pt = next_pt()
                for bb in range(4):
                    b = b4 * 4 + bb
                    tr(pt[:, bb * 128:(bb + 1) * 128], Ysb[cc].ap(b, [[NB, 128]]), [Ysb[cc]], [pt])
                t0 = gb0 * 64 + b4 * 512
                x0_, z_, tm, ob = x0t[blk % 2], zt[blk % 2], tmpc[blk % 2], obb[blk % 2]
                blk += 1
                dma("sp", x0_[:], x0T_s.h.ap()[cc * 128:(cc + 1) * 128, t0:t0 + 512], [x0T_s], [x0_])
                dma("sp", z_[:], zT_s.h.ap()[cc * 128:(cc + 1) * 128, t0:t0 + 512], [zT_s], [z_])
                ts("dve", tm[:], pt[:, 0:512], rnrm[gi][:, cc:cc + 1], None, ALU.mult, None, [pt, rnrm[gi]], [tm])
                stt("dve", tm[:], z_[:], hbi[:, cc:cc + 1], tm[:], ALU.mult, ALU.add, [z_, hbi, tm], [tm])
                tt("dve", ob[:], tm[:], x0_[:], ALU.mult, [tm, x0_], [ob])
                dma("pool", brT_s[1].h.ap()[cc * 128:(cc + 1) * 128, t0:t0 + 512], ob[:], [ob], [brT_s[1]])

    def phase_2C(sidx):
        s0, L = cfg.seqs[sidx]
        NCK = L // 128
        ar.reset()
        Sb_all = ar.alloc("Sb_all", [128, 2, NCK, 64], BF16)
        Sst = ar.alloc("Sst", [128, 2, 2, 64], F32)
        Sfb = [ar.alloc("Sfb", [128, 2, 64], BF16) for _ in range(2)]
        rt = [ar.alloc("rt", [128, 4, 1024], BF16) for _ in range(2)]
        vk = [ar.alloc("vk", [128, 4, 64], BF16) for _ in range(2)]
        qkTs = [ar.alloc("qkTs", [128, 4, 128], BF16) for _ in range(2)]
        qsf = [ar.alloc("qsf", [128, 2, 128], BF16) for _ in range(2)]
        qsb = [ar.alloc("qsb", [128, 2, 128], BF16) for _ in range(2)]
        Pm = [ar.alloc("Pm", [128, 4, 128], BF16) for _ in range(2)]
        sqo = ar.alloc("sqo", [128, 256], F32)
        sso = [ar.alloc("sso", [128, 4], F32) for _ in range(2)]
        oc32 = [ar.alloc("oc32", [128, 256], F32) for _ in range(2)]
        ocb = [ar.alloc("ocb", [128, 256], BF16) for _ in range(2)]
        ocT = [ar.alloc("ocT", [128, 2, 512], BF16) for _ in range(2)]
        NG = NCK // 4

        def load_group(g, slot):
            t0 = s0 + g * 512
            dma("sp", rt[slot][:], ret_s.h.ap()[t0:t0 + 512, :].rearrange("(s p) c -> p s c", p=128), [ret_s], [rt[slot]])

        def kv_update(rt_, s, d, j):
            vk_ = vk[j % 2]
            tt("dve", vk_[:], rt_[:, s, 512:768].rearrange("p (h e) -> p h e", e=64), wkk.ap(d, [[2, 4], [0, 64]]),
               ALU.mult, [rt_, wkk], [vk_])
            for p in range(2):
                pk = next_pf(4)
                mm(pk[:, 0:128], rt_[:, s, 256 + p * 128:256 + (p + 1) * 128], vk_[:, 2 * p:2 * p + 2, :].rearrange("p h e -> p (h e)"),
                   True, True, [rt_, vk_], [pk])
                for half in range(2):
                    r = slice(half * 64, half * 64 + 64)
                    stt("dve", Sst[r, p, d, :], Sst[r, p, d, :], gch[r, p, d:d + 1], pk[r, half * 64:half * 64 + 64],
                        ALU.mult, ALU.add, [Sst, gch, pk], [Sst])

        mset("pool", Sst[:], 0.0, [Sst])
        j = 0
        for g in reversed(range(NG)):
            slot = g % 2
            load_group(g, slot)
            for s in reversed(range(4)):
                n = g * 4 + s
                cp("act", Sb_all[:, :, n, :], Sst[:, :, 1, :], [Sst], [Sb_all])
                if n > 0:
                    kv_update(rt[slot], s, 1, j)
                    j += 1
        import os
        DBG = int(os.environ.get("DBG2C", "9"))
        for g in range(NG if DBG > 0 else 0):
            slot = g % 2
            load_group(g, slot)
            rt_ = rt[slot]
            ocT_ = ocT[g % 2]
            for s in range(4):
                n = g * 4 + s
                qk_, qf_, qb_, Pm_, Sf_ = qkTs[n % 2], qsf[n % 2], qsb[n % 2], Pm[n % 2], Sfb[n % 2]
                SK = os.environ.get("SKIP", "").split(",")
                pt = next_pt()
                if "tr" not in SK:
                    for c4 in range(4):
                        tr(pt[:, c4 * 128:(c4 + 1) * 128], rt_[:, s, c4 * 128:(c4 + 1) * 128], [rt_], [pt])
                if "cpq" not in SK:
                    cp("act", qk_[:], pt[:, 0:512].rearrange("p (c t) -> p c t", t=128), [pt], [qk_])
                if "qf" not in SK:
                    tt("dve", qf_[:], qk_[:, 0:2, :], wqf[:], ALU.mult, [qk_, wqf], [qf_])
                    tt("dve", qb_[:], qk_[:, 0:2, :], wqb[:], ALU.mult, [qk_, wqb], [qb_])
                if "sf" not in SK:
                    cp("act", Sf_[:], Sst[:, :, 0, :], [Sst], [Sf_])
                psa = PF[(2 * n) % 4]
                psb = PF[(2 * n + 1) % 4]
                for h in range(4):
                    p, half = h // 2, h % 2
                    r = slice(half * 64, half * 64 + 64)
                    pdst = psb if half else psa
                    mm(pdst[:, p * 128:(p + 1) * 128], qk_[r, 2 + p, :], qk_[r, p, :], True, True, [qk_], [pdst])
                for half, pdst in ((0, psa), (1, psb)):
                    tt("dve", Pm_.ap(half * 128, [[256, 2], [1, 128]]), pdst[:, 0:256].rearrange("p (a i) -> p a i", i=128),
                       Dm.ap(half * 128, [[256, 2], [1, 128]]), ALU.mult, [pdst, Dm], [Pm_])
                if DBG < 2:
                    continue
                po = PF[4 + n % 2]
                for h in range(4):
                    p, half = h // 2, h % 2
                    r = slice(half * 64, half * 64 + 64)
                    oh = po[:, h * 64:(h + 1) * 64]
                    mm(oh, Pm_[:, h, :], rt_[:, s, 512 + h * 64:512 + (h + 1) * 64], True, False, [Pm_, rt_], [po], skip=True)
                    mm(oh, qf_[r, p, :], Sf_[r, p, :], False, False, [qf_, Sf_], [po], skip=True)
                    mm(oh, qb_[r, p, :], Sb_all[r, p, n, :], False, True, [qb_, Sb_all], [po], skip=True)
                if DBG < 3:
                    continue
                ss_, o32, ob_ = sso[n % 2], oc32[n % 2], ocb[n % 2]
                act(o32[:], po[:, 0:256], AF.Copy, [po], [o32])
                tt("dve", sqo[:], o32[:], o32[:], ALU.mult, [o32], [sqo])
                red("dve", ss_[:], sqo[:].rearrange("p (h e) -> p h e", e=64), [sqo], [ss_])
                rstd_from_ss(ss_[:], 64.0, [ss_])
                tt("dve", o32[:].rearrange("p (h e) -> p h e", e=64), o32[:].rearrange("p (h e) -> p h e", e=64),
                   ss_.ap(0, [[1, 4], [0, 64]]), ALU.mult, [o32, ss_], [o32])
                tt("dve", ob_[:], o32[:], rt_[:, s, 768:1024], ALU.mult, [o32, rt_], [ob_])
                pt2 = next_pt()
                for c2 in range(2):
                    tr(pt2[:, c2 * 128:(c2 + 1) * 128], ob_[:, c2 * 128:(c2 + 1) * 128], [ob_], [pt2])
                cp("act", ocT_[:, :, s * 128:(s + 1) * 128], pt2[:, 0:256].rearrange("p (c t) -> p c t", t=128), [pt2], [ocT_])
                if n < NCK - 1:
                    kv_update(rt_, s, 0, n)
            t0 = s0 + g * 512
            if DBG < 3:
                continue
            dma("pool", brT_s[2].h.ap().rearrange("(c p) t -> p c t", p=128)[:, :, t0:t0 + 512], ocT_[:], [ocT_], [brT_s[2]])

    def phase_3a(l, xsrc):
        ar.reset()
        Wg = ar.alloc("Wg", [128, 8, 4096], BF16)
        Wb = ar.alloc("Wb", [128, 4, 2, D], BF16)
        Wo = ar.alloc("Wo", [128, 8, D], BF16)
        for k in range(8):
            dma("pool", Wg[:, k, :], w_in.ap()[l, k * 128:(k + 1) * 128, 3072:7168], [], [Wg])
            dma("pool", Wo[:, k, :], w_o.ap()[l, k * 128:(k + 1) * 128, :], [], [Wo])
        for n in range(4):
            dma("pool", Wb[:, n, :, :], w_br.ap()[l, n].rearrange("(k p) c -> p k c", p=128), [], [Wb])
        xT = [ar.alloc("xT3", [128, 8, 512], BF16) for _ in range(2)]
        br = [ar.alloc("br3", [128, 4, 2, 512], BF16) for _ in range(2)]
        sg = [ar.alloc("sg", [128, 512], F32) for _ in range(2)]
        mg = [ar.alloc("mg", [128, 512], F32) for _ in range(2)]
        tm = [ar.alloc("tm3", [128, 512], F32) for _ in range(2)]
        mT = ar.alloc("mT", [128, 8, 512], BF16)
        xt = [ar.alloc("xt3", [128, D], F32)] * 2
        y32 = [ar.alloc("y32", [128, D], F32) for _ in range(2)]
        junk = ar.alloc("junk3", [128, D], BF16)
        ss2 = [ar.alloc("ss2", [128, 2], F32) for _ in range(2)]
        ss1 = [ar.alloc("ss1", [128, 1], F32) for _ in range(2)]
        hn = [ar.alloc("hn", [128, D], BF16) for _ in range(2)]
        hT = [ar.alloc("hT", [128, 8, 512], BF16) for _ in range(2)]
        jj = 0
        deferred3 = []

        def flush3():
            while deferred3:
                deferred3.pop(0)()

        for ti in range(NTILE):
            t0 = ti * 512
            xT_, br_ = xT[ti % 2], br[ti % 2]
            dma("sp", xT_[:], xnT_s.h.ap().rearrange("(k p) t -> p k t", p=128)[:, :, t0:t0 + 512], [xnT_s], [xT_])
            for n in range(4):
                dma("sp", br_[:, n, :, :], brT_s[n].h.ap().rearrange("(c p) t -> p c t", p=128)[:, :, t0:t0 + 512], [brT_s[n]], [br_])
            for j in range(8):
                mg_ = mg[j % 2]
                for n in range(4):
                    pg = next_pf()
                    for k in range(8):
                        mm(pg[:], Wg[:, k, n * D + j * 128:n * D + (j + 1) * 128], xT_[:, k, :], k == 0, k == 7, [Wg, xT_], [pg])
                    pp = next_pf()
                    for kk in range(2):
                        mm(pp[:], Wb[:, n, kk, j * 128:(j + 1) * 128], br_[:, n, kk, :], kk == 0, kk == 1, [Wb, br_], [pp])
                    if j == 0 and n == 0:
                        flush3()
                    sg_ = sg[jj % 2]
                    jj += 1
                    act(sg_[:], pg[:], AF.Sigmoid, [pg], [sg_])
                    if n == 0:
                        tt("dve", mg_[:], sg_[:], pp[:], ALU.mult, [sg_, pp], [mg_])
                    else:
                        tm_ = tm[jj % 2]
                        tt("dve", tm_[:], sg_[:], pp[:], ALU.mult, [sg_, pp], [tm_])
                        if n < 3:
                            tt("pool", mg_[:], mg_[:], tm_[:], ALU.add, [mg_, tm_], [mg_])
                        else:
                            tt("pool", mT[:, j, :], mg_[:], tm_[:], ALU.add, [mg_, tm_], [mT])
            hT_ = hT[ti % 2]
            for s in range(4):
                i = ti * 4 + s
                x_, y_, s2, s1, hn_ = xt[i % 2], y32[i % 2], ss2[i % 2], ss1[i % 2], hn[i % 2]
                dma("sp", x_[:], xsrc.h.ap()[i * 128:(i + 1) * 128, :], [xsrc], [x_])
                pos = []
                for nn in range(2):
                    po = next_pf()
                    pos.append(po)
                    for k in range(8):
                        mm(po[:], mT[:, k, s * 128:(s + 1) * 128], Wo[:, k, nn * 512:(nn + 1) * 512], k == 0, k == 7, [mT, Wo], [po])
                    act(junk[:, nn * 512:(nn + 1) * 512], po[:], AF.Square, [po], [junk, s2], accum=s2[:, nn:nn + 1])
                flush3()
                tt("dve", s1[:], s2[:, 0:1], s2[:, 1:2], ALU.add, [s2], [s1])
                rstd_from_ss(s1[:], D, [s1])
                for nn in range(2):
                    stt("dve", y_[:, nn * 512:(nn + 1) * 512], pos[nn][:], s1[:, 0:1], gT[:, 1, nn * 512:(nn + 1) * 512],
                        ALU.mult, ALU.mult, [pos[nn], s1, gT], [y_])
                tt("pool", y_[:], y_[:], x_[:], ALU.add, [y_, x_], [y_])
                dma("pool", hbuf.h.ap()[i * 128:(i + 1) * 128, :], y_[:], [y_], [hbuf])
                act(junk[:], y_[:], AF.Square, [y_], [junk, s1], accum=s1[:])
                rstd_from_ss(s1[:], D, [s1])
                stt("dve", hn_[:], y_[:], s1[:, 0:1], gT[:, 2, :], ALU.mult, ALU.mult, [y_, s1, gT], [hn_])
                def do_tr(hn_=hn_, hT_=hT_, s=s):
                    pt = next_pt()
                    for k in range(8):
                        tr(pt[:, k * 128:(k + 1) * 128], hn_[:, k * 128:(k + 1) * 128], [hn_], [pt])
                    cp("act", hT_[:, :, s * 128:(s + 1) * 128], pt[:].rearrange("p (k t) -> p k t", t=128), [pt], [hT_])
                deferred3.append(do_tr)

            def do_store(hT_=hT_, t0=t0):
                dma("pool", hnT_s.h.ap().rearrange("(k p) t -> p k t", p=128)[:, :, t0:t0 + 512], hT_[:], [hT_], [hnT_s])
            deferred3.append(do_store)
        flush3()

    def phase_3b(l, dst):
        ar.reset()
        Wi = ar.alloc("Wi", [128, 8, 2 * DFF], BF16)
        Wf = ar.alloc("Wf", [128, 22, D], BF16)
        for k in range(8):
            dma("pool", Wi[:, k, :], w_fi.ap()[l, k * 128:(k + 1) * 128, :], [], [Wi])
        for k in range(22):
            dma("pool", Wf[:, k, :], w_fo.ap()[l, k * 128:(k + 1) * 128, :], [], [Wf])
        hT = [ar.alloc("hT3", [128, 8, 512], BF16) for _ in range(2)]
        sg = [ar.alloc("sgb", [128, 512], BF16) for _ in range(2)]
        fT = ar.alloc("fT", [128, 22, 512], BF16)
        ht = [ar.alloc("ht", [128, D], F32)] * 2
        y32 = [ar.alloc("y32b", [128, D], F32)] * 2
        junk = ar.alloc("junkb", [128, D], BF16)
        ss2 = [ar.alloc("ss2b", [128, 2], F32) for _ in range(2)]
        ss1 = [ar.alloc("ss1b", [128, 1], F32) for _ in range(2)]
        jj = 0
        for ti in range(NTILE):
            t0 = ti * 512
            hT_ = hT[ti % 2]
            dma("sp", hT_[:], hnT_s.h.ap().rearrange("(k p) t -> p k t", p=128)[:, :, t0:t0 + 512], [hnT_s], [hT_])
            for j in range(22):
                pg = next_pf()
                for k in range(8):
                    mm(pg[:], Wi[:, k, j * 128:(j + 1) * 128], hT_[:, k, :], k == 0, k == 7, [Wi, hT_], [pg])
                pu = next_pf()
                for k in range(8):
                    mm(pu[:], Wi[:, k, DFF + j * 128:DFF + (j + 1) * 128], hT_[:, k, :], k == 0, k == 7, [Wi, hT_], [pu])
                sg_ = sg[jj % 2]
                jj += 1
                act(sg_[:], pg[:], AF.Silu, [pg], [sg_])
                tt("dve", fT[:, j, :], sg_[:], pu[:], ALU.mult, [sg_, pu], [fT])
            for s in range(4):
                i = ti * 4 + s
                h_, y_, s2, s1 = ht[i % 2], y32[i % 2], ss2[i % 2], ss1[i % 2]
                dma("sp", h_[:], hbuf.h.ap()[i * 128:(i + 1) * 128, :], [hbuf], [h_])
                pos = []
                for nn in range(2):
                    po = next_pf()
                    pos.append(po)
                    for k in range(22):
                        mm(po[:], fT[:, k, s * 128:(s + 1) * 128], Wf[:, k, nn * 512:(nn + 1) * 512], k == 0, k == 21, [fT, Wf], [po])
                    act(junk[:, nn * 512:(nn + 1) * 512], po[:], AF.Square, [po], [junk, s2], accum=s2[:, nn:nn + 1])
                tt("dve", s1[:], s2[:, 0:1], s2[:, 1:2], ALU.add, [s2], [s1])
                rstd_from_ss(s1[:], D, [s1])
                for nn in range(2):
                    stt("dve", y_[:, nn * 512:(nn + 1) * 512], pos[nn][:], s1[:, 0:1], gT[:, 3, nn * 512:(nn + 1) * 512],
                        ALU.mult, ALU.mult, [pos[nn], s1, gT], [y_])
                tt("pool", y_[:], y_[:], h_[:], ALU.add, [y_, h_], [y_])
                dma("pool", dst.h.ap()[i * 128:(i + 1) * 128, :], y_[:], [y_], [dst])

    PH = getattr(cfg, "phases", None)

    def on(name):
        return PH is None or name in PH

    mk.marks = []

    def mark(name):
        mk.marks.append((name, len(mk.eng_ops["pe"]), len(mk.eng_ops["act"]), len(mk.eng_ops["dve"])))

    for l in range(depth):
        src = xin if l == 0 else x1buf
        dst = x1buf if l < depth - 1 else yout
        barrier()
        mark(f"L{l}:start")
        load_layer_consts(l)
        if on("R"):
            phase_R(l)
        barrier()
        mark(f"L{l}:R")
        if on("F"):
            for gi in range(len(cfg.groups)):
                phase_F(l, gi)
                barrier()
        mark(f"L{l}:F")
        if on("A"):
            phase_A(l, src)
            barrier()
        mark(f"L{l}:A")
        if on("1b"):
            phase_1b(l)
            barrier()
        mark(f"L{l}:1b")
        if on("2A"):
            for sidx in range(len(cfg.seqs)):
                phase_2A(sidx)
                barrier()
                mark(f"L{l}:2A.{sidx}")
        if on("2B"):
            for gi in range(len(cfg.groups)):
                phase_2B(gi)
                barrier()
                mark(f"L{l}:2B.{gi}")
        if on("2C"):
            for sidx in range(len(cfg.seqs)):
                phase_2C(sidx)
                barrier()
            mark(f"L{l}:2C")
        if on("3a"):
            phase_3a(l, src)
            barrier()
        mark(f"L{l}:3a")
        if on("3b"):
            phase_3b(l, dst)
            barrier()
        mark(f"L{l}:3b")

    mk.emit()
    return nc, mk, tabs


def make_in_maps(cfg, inputs, tabs, ncore=NCORE):
    f = lambda a: np.ascontiguousarray(np.asarray(a, dtype=np.float32))
    xs, xp = f(inputs["x_sample"]), f(inputs["x_prompt"])
    dp = cfg.depth
    shared = {
        "w_in": f(inputs["w_in"])[:dp], "w_branch": f(inputs["w_branch"])[:dp], "w_out": f(inputs["w_out"])[:dp],
        "w_ffn_in": f(inputs["w_ffn_in"])[:dp], "w_ffn_out": f(inputs["w_ffn_out"])[:dp],
        "norm_gains": f(inputs["norm_gains"])[:dp],
        "qk_norm": f(inputs["qk_norm"])[:dp].reshape(dp, 128),
        "hy_w1": f(inputs["hy_w1"])[:dp], "hy_w2": f(inputs["hy_w2"])[:dp], "hy_w3": f(inputs["hy_w3"])[:dp],
        "rde": f(inputs["ret_decay_exp"])[:dp].reshape(dp, 8),
    }
    hcw, hcb = f(inputs["hy_conv_w"])[:dp], f(inputs["hy_conv_b"])[:dp]
    cv = np.concatenate([hcw, hcb[:, None, :]], axis=1)
    shared["convp"] = np.ascontiguousarray(cv.reshape(dp, 4, 6, 128).transpose(0, 3, 2, 1))
    shared["scwp"] = np.ascontiguousarray(f(inputs["sc_conv_w"])[:dp].reshape(dp, 3, 2, 128).transpose(0, 3, 2, 1))
    shared["hbiasp"] = np.ascontiguousarray(f(inputs["hy_bias"])[:dp].reshape(dp, 2, 128).transpose(0, 2, 1))
    hv = np.stack([f(inputs["hy_b1"])[:dp], f(inputs["hy_b2"])[:dp], f(inputs["hy_freq"])[:dp, 0], f(inputs["hy_freq"])[:dp, 1]], axis=-1)
    shared["hyvec"] = np.ascontiguousarray(hv)
    shared.update(tabs)
    maps = []
    for c in range(ncore):
        parts = [xs[c]] + [xp[cfg.NP * c + i] for i in range(cfg.NP)]
        m = dict(shared)
        m["xin"] = np.ascontiguousarray(np.concatenate(parts, axis=0))
        maps.append(m)
    return maps


_CACHE = {}


def kernel(**inputs):
    cfg = Cfg()
    if "nc" not in _CACHE:
        _CACHE["nc"] = build(cfg)
    nc, mk, tabs = _CACHE["nc"]
    maps = make_in_maps(cfg, inputs, tabs)
    res = run_bass_kernel_spmd(nc, maps, core_ids=list(range(NCORE)))
    ys = np.stack([r["yout"][:cfg.LS] for r in res.results], axis=0)
    yp = np.concatenate([r["yout"][cfg.LS:].reshape(cfg.NP, cfg.LP, D) for r in res.results], axis=0)
    return (np.ascontiguousarray(yp.astype(np.float32)), np.ascontiguousarray(ys.astype(np.float32)))
```

```python
import math
import numpy as np
import concourse.bass as bass
import concourse.mybir as mybir
from concourse.bass_utils import run_bass_kernel_spmd

F32 = mybir.dt.float32
BF16 = mybir.dt.bfloat16
AF = mybir.ActivationFunctionType
ALU = mybir.AluOpType
AX = mybir.AxisListType

D = 1024
DEPTH = 2
DFF = 2816
NCORE = 8
HD = 64
EPS = 1e-6
ENGS = ("pe", "act", "dve", "pool", "sp")
import os as _os
SAME_ENGINE_SYNC = _os.environ.get("SES", "1") == "1"


class Buf:
    __slots__ = ("name", "w", "r_eng", "r_dma")

    def __init__(self, name=""):
        self.name = name
        self.w = None
        self.r_eng = {}
        self.r_dma = []


class _Op:
    __slots__ = ("eng", "fn", "deps", "dma", "signal", "sigval", "ring", "ringval", "ringprev")

    def __init__(self, eng, fn, deps, dma):
        self.eng = eng
        self.fn = fn
        self.deps = deps
        self.dma = dma
        self.signal = False
        self.sigval = 0
        self.ring = None
        self.ringval = 0
        self.ringprev = 0


class MK:
    def __init__(self, nc, ring_k=8):
        self.nc = nc
        self.ops = []
        self.eng_ops = {e: [] for e in ENGS}
        self.ring_k = ring_k
        self.ring_use = {}
        self.ring_next = {e: 0 for e in ENGS}
        self.world = Buf("world")

    def op(self, eng, fn, reads=(), writes=(), dma=False, barrier=False):
        idx = len(self.ops)
        deps = set()
        reads = list(reads)
        writes = list(writes)
        if barrier:
            writes.append(self.world)
        else:
            reads.append(self.world)
        for b in reads:
            if b.w is not None:
                deps.add(b.w)
        for b in writes:
            if b.w is not None:
                deps.add(b.w)
            deps.update(b.r_eng.values())
            deps.update(b.r_dma)
        o = _Op(eng, fn, deps, dma)
        if dma:
            slot = self.ring_next[eng]
            self.ring_next[eng] = (slot + 1) % self.ring_k
            key = (eng, slot)
            u = self.ring_use.get(key, 0)
            o.ring = key
            o.ringprev = 16 * u
            o.ringval = 16 * (u + 1)
            self.ring_use[key] = u + 1
        self.ops.append(o)
        self.eng_ops[eng].append(idx)
        for b in reads:
            if dma:
                b.r_dma.append(idx)
            else:
                b.r_eng[eng] = idx
        for b in writes:
            b.w = idx
            b.r_eng = {}
            b.r_dma = []
        return idx

    def emit(self):
        nc = self.nc
        ops = self.ops
        esem = {e: nc.alloc_semaphore(name=f"es_{e}") for e in ENGS}
        rsem = {key: nc.alloc_semaphore(name=f"rs_{key[0]}_{key[1]}") for key in self.ring_use}

        def needs_sync(p, c):
            if p.dma:
                return True
            if p.eng == c.eng:
                if p.eng == "pe":
                    return False
                return SAME_ENGINE_SYNC
            return True

        for c in ops:
            for d in c.deps:
                p = ops[d]
                if (not p.dma) and needs_sync(p, c):
                    p.signal = True
        cnt = {e: 0 for e in ENGS}
        for o in ops:
            if (not o.dma) and o.signal:
                cnt[o.eng] += 1
                o.sigval = cnt[o.eng]
        known = {e: {} for e in ENGS}
        waits_of = [None] * len(ops)
        for idx, c in enumerate(ops):
            w = {}
            for d in c.deps:
                p = ops[d]
                if not needs_sync(p, c):
                    continue
                if p.dma:
                    s, v = ("r", p.ring), p.ringval
                else:
                    s, v = ("e", p.eng), p.sigval
                if v > w.get(s, 0):
                    w[s] = v
            if c.dma and c.ringprev > 0:
                s = ("r", c.ring)
                if c.ringprev > w.get(s, 0):
                    w[s] = c.ringprev
            kn = known[c.eng]
            lst = []
            for s, v in w.items():
                if v > kn.get(s, 0):
                    kn[s] = v
                    lst.append((s, v))
            waits_of[idx] = lst
        self.n_waits = sum(len(v) for v in waits_of)
        final_ring = {key: 16 * u for key, u in self.ring_use.items()}

        def semh(s):
            return rsem[s[1]] if s[0] == "r" else esem[s[1]]

        def replay(ename, eng):
            for idx in self.eng_ops[ename]:
                o = ops[idx]
                for s, v in waits_of[idx]:
                    eng.wait_ge(semh(s), v)
                ins = o.fn(eng)
                if o.dma:
                    ins.then_inc(rsem[o.ring], 16)
                elif o.signal:
                    ins.then_inc(esem[ename], 1)
            if ename == "sp":
                for key, v in final_ring.items():
                    eng.wait_ge(rsem[key], v)

        with nc.Block() as block:
            @block.tensor
            def _(e):
                replay("pe", e)

            @block.scalar
            def _(e):
                replay("act", e)

            @block.vector
            def _(e):
                replay("dve", e)

            @block.gpsimd
            def _(e):
                replay("pool", e)

            @block.sync
            def _(e):
                replay("sp", e)


class T:
    __slots__ = ("h", "b", "F", "h_off")

    def __init__(self, h, name=""):
        self.h = h
        self.b = Buf(name)
        sh = list(h.shape)
        f = 1
        for s in sh[1:]:
            f *= s
        self.F = f

    def __getitem__(self, k):
        return self.h[k]

    def ap(self, off, dims, p0=0, np_=128):
        return bass.AP(self.h, p0 * self.F + off, [[self.F, np_]] + [list(d) for d in dims])


def _dsize(dt):
    return 4 if dt == F32 else 2


class Arena:
    def __init__(self, nc, base, top):
        self.nc = nc
        self.base = base
        self.top = top
        self.off = base
        self.n = 0

    def mark(self):
        return self.off

    def reset(self, to=None):
        self.off = self.base if to is None else to

    def alloc(self, name, shape, dt):
        nb = _dsize(dt)
        for s in shape[1:]:
            nb *= s
        off = (self.off + 31) // 32 * 32
        assert off + nb <= self.top, f"SBUF arena overflow allocating {name}: {off}+{nb} > {self.top}"
        self.n += 1
        h = self.nc.alloc_sbuf_tensor_at(f"{name}_{self.n}", list(shape), dt, offset=off)
        self.off = off + nb
        t = T(h, name)
        t.h_off = off
        return t


class Cfg:
    def __init__(self, LS=8192, LP=2048, NP=2, depth=DEPTH):
        self.LS, self.LP, self.NP, self.depth = LS, LP, NP, depth
        self.seqs = [(0, LS)] + [(LS + i * LP, LP) for i in range(NP)]
        self.NT = LS + NP * LP
        self.groups = [(LS, [0])] + ([(LP, list(range(1, 1 + NP)))] if NP else [])
        self.LMAX = max(LS, LP)


def host_tables(cfg):
    tabs = {}
    L = cfg.LMAX
    t = np.arange(L)
    r = (t // 64).astype(np.float32)
    c = (t % 64).astype(np.float32)
    inv = (10000.0 ** (-np.arange(16, dtype=np.float32) / 16)).astype(np.float32)
    ang = np.stack([r[:, None] * inv, c[:, None] * inv], axis=1).astype(np.float32)
    cs, sn = np.cos(ang).astype(np.float32), np.sin(ang).astype(np.float32)
    C = np.stack([cs, cs], axis=2)
    S = np.stack([-sn, sn], axis=2)
    tabs["ropeC"] = np.ascontiguousarray(C.reshape(L, 64))
    tabs["ropeS"] = np.ascontiguousarray(S.reshape(L, 64))
    tabs["ident"] = np.eye(128, dtype=np.float32)
    for gi, (Lg, _) in enumerate(cfg.groups):
        m = np.arange(2 * Lg)
        n = np.abs(m - Lg)
        n = np.minimum(n, Lg - 1)
        tt = np.linspace(0.0, 1.0, Lg, dtype=np.float32)
        f = np.linspace(1e-4, 15.0, 16, dtype=np.float32)
        angf = ((2.0 * math.pi / Lg) * np.arange(Lg, dtype=np.float32)[:, None] * f[None, :]).astype(np.float32)
        z = np.concatenate([tt[:, None], np.cos(angf), -np.sin(angf)], axis=-1).astype(np.float32)
        tabs[f"zext{gi}"] = np.ascontiguousarray(z[n].T)
        tabs[f"trow{gi}"] = np.ascontiguousarray(tt[n][None, :])
    deltas = np.abs(np.linspace(math.log(1e-2) / 1.5, math.log(1e-2) / 0.3, 256, dtype=np.float32))
    tabs["negdelta"] = np.ascontiguousarray((-deltas).reshape(2, 128).T.astype(np.float32))
    i = np.arange(128, dtype=np.float32)
    diff = i[None, :] - i[:, None]
    tabs["dpos"] = np.maximum(diff, 0).astype(np.float32)
    tabs["dneg"] = np.maximum(-diff, 0).astype(np.float32)
    tabs["iqf"] = np.tile((i + 1)[None, :], (128, 1)).astype(np.float32)
    tabs["iqb"] = np.tile((128 - i)[None, :], (128, 1)).astype(np.float32)
    tabs["jk"] = np.stack([127 - i, i], axis=1).astype(np.float32)
    return tabs


DEBUG_OUT = set()


def build(cfg):
    nc = bass.Bass("TRN2", target_bir_lowering=False)
    mk = MK(nc)
    NT, LS = cfg.NT, cfg.LS
    depth = cfg.depth
    NTILE = NT // 512

    def din(name, shape, dt=F32):
        return nc.dram_tensor(name, list(shape), dt, kind="ExternalInput")

    def dscr(name, shape, dt):
        if name in DEBUG_OUT:
            return T(nc.dram_tensor(name, list(shape), dt, kind="ExternalOutput"), name)
        return T(nc.dram_tensor(name, list(shape), dt), name)

    xin = T(din("xin", [NT, D]), "xin")
    yout = T(nc.dram_tensor("yout", [NT, D], F32, kind="ExternalOutput"), "yout")
    w_in = din("w_in", [depth, D, 7168])
    w_br = din("w_branch", [depth, 4, 256, D])
    w_o = din("w_out", [depth, D, D])
    w_fi = din("w_ffn_in", [depth, D, 2 * DFF])
    w_fo = din("w_ffn_out", [depth, DFF, D])
    ngain = din("norm_gains", [depth, 4, D])
    qkn = din("qk_norm", [depth, 128])
    convp = din("convp", [depth, 128, 6, 4])
    scwp = din("scwp", [depth, 128, 2, 3])
    hbiasp = din("hbiasp", [depth, 128, 2])
    hy_w1 = din("hy_w1", [depth, 33, 64])
    hy_w2 = din("hy_w2", [depth, 64, 64])
    hy_w3 = din("hy_w3", [depth, 64, 512])
    hyvec = din("hyvec", [depth, 64, 4])
    rde = din("rde", [depth, 8])
    tabs = host_tables(cfg)
    tin = {k: din(k, v.shape) for k, v in tabs.items()}

    x1buf = dscr("x1buf", [NT, D], F32)
    hbuf = dscr("hbuf", [NT, D], F32)
    xnT_s = dscr("xnT_s", [D, NT], BF16)
    hnT_s = dscr("hnT_s", [D, NT], BF16)
    brT_s = [dscr(f"brT{n}", [256, NT], BF16) for n in range(4)]
    qT_s = dscr("qT_s", [256, NT], BF16)
    kT_s = dscr("kT_s", [128, NT], BF16)
    v_s = dscr("v_s", [NT, 130], BF16)
    ret_s = dscr("ret_s", [NT, 1024], BF16)
    x0T_s = dscr("x0T_s", [256, NT], BF16)
    zT_s = dscr("zT_s", [256, NT], BF16)
    zr_s = dscr("zr_s", [64, NT // 64, 256], BF16)
    kfl = [dscr(f"kfl{gi}", [256, 2 * Lg], BF16) for gi, (Lg, _) in enumerate(cfg.groups)]

    ar = Arena(nc, 18432, 229344)
    ident = ar.alloc("ident", [128, 128], BF16)
    epsT = ar.alloc("eps", [128, 1], F32)
    gT = ar.alloc("gains", [128, 4, D], F32)
    gqk = ar.alloc("gqk", [128, 6, 64], F32)
    cvp = ar.alloc("cvp", [128, 6, 4], F32)
    scw = ar.alloc("scw", [128, 2, 3], F32)
    hbi = ar.alloc("hbi", [128, 2], F32)
    ndl = ar.alloc("ndl", [128, 2], F32)
    rnrm = [ar.alloc(f"rnrm{gi}", [128, 2], F32) for gi in range(len(cfg.groups))]
    Dm = ar.alloc("Dm", [128, 4, 128], F32)
    wqf = ar.alloc("wqf", [128, 2, 128], F32)
    wqb = ar.alloc("wqb", [128, 2, 128], F32)
    wkk = ar.alloc("wkk", [128, 4, 2], F32)
    gch = ar.alloc("gch", [128, 2, 2], F32)
    PERSIST = ar.mark()
    ar.base = PERSIST

    PF = [T(nc.alloc_psum_tensor(f"pf{i}", [128, 512], F32), f"pf{i}") for i in range(6)]
    PT = [T(nc.alloc_psum_tensor(f"pt{i}", [128, 1024], BF16), f"pt{i}") for i in range(2)]
    rot = {"pf": 0, "pt": 0}

    def next_pf(n=6, base=0):
        i = rot["pf"] % n + base
        rot["pf"] += 1
        return PF[i]

    def next_pt():
        i = rot["pt"] % 2
        rot["pt"] += 1
        return PT[i]

    def bl(x):
        return [t.b if isinstance(t, T) else t for t in x]

    def mm(out, lhsT, rhs, start, stop, R, W, skip=False):
        mk.op("pe", lambda e: e.matmul(out, lhsT=lhsT, rhs=rhs, start=start, stop=stop, skip_group_check=skip), bl(R), bl(W))

    def tr(out, in_, R, W):
        idn = ident[:]
        mk.op("pe", lambda e: e.transpose(out=out, in_=in_, identity=idn), bl(R) + [ident.b], bl(W))

    def act(out, in_, func, R, W, scale=1.0, bias=None, accum=None):
        def f(e):
            kw = {}
            if bias is not None:
                kw["bias"] = bias
            if accum is not None:
                kw["accum_out"] = accum
            return e.activation(out=out, in_=in_, func=func, scale=scale, **kw)
        mk.op("act", f, bl(R), bl(W))

    def tt(eng, out, in0, in1, op, R, W):
        mk.op(eng, lambda e: e.tensor_tensor(out=out, in0=in0, in1=in1, op=op), bl(R), bl(W))

    def ts(eng, out, in0, s1, s2, op0, op1, R, W):
        if s2 is None:
            mk.op(eng, lambda e: e.tensor_scalar(out=out, in0=in0, scalar1=s1, scalar2=None, op0=op0), bl(R), bl(W))
        else:
            mk.op(eng, lambda e: e.tensor_scalar(out=out, in0=in0, scalar1=s1, scalar2=s2, op0=op0, op1=op1), bl(R), bl(W))

    def stt(eng, out, in0, sc, in1, op0, op1, R, W):
        mk.op(eng, lambda e: e.scalar_tensor_tensor(out=out, in0=in0, scalar=sc, in1=in1, op0=op0, op1=op1), bl(R), bl(W))

    def cp(eng, out, in_, R, W):
        if eng == "act":
            act(out, in_, AF.Copy, R, W)
        else:
            mk.op(eng, lambda e: e.tensor_copy(out=out, in_=in_), bl(R), bl(W))

    def red(eng, out, in_, R, W):
        mk.op(eng, lambda e: e.tensor_reduce(out=out, in_=in_, axis=AX.X, op=ALU.add), bl(R), bl(W))

    def rcp(out, in_, R, W):
        mk.op("dve", lambda e: e.reciprocal(out=out, in_=in_), bl(R), bl(W))

    def mset(eng, out, val, W):
        mk.op(eng, lambda e: e.memset(out, val), [], bl(W))

    def dma(q, out, in_, R, W):
        mk.op(q, lambda e: e.dma_start(out=out, in_=in_), bl(R), bl(W), dma=True)

    def barrier():
        mk.op("pool", lambda e: e.memset(epsT[:, 0:1], EPS), [], [epsT.b], barrier=True)

    def rstd_from_ss(ss, n, R):
        act(ss, ss, AF.Sqrt, R, R, scale=1.0 / n, bias=epsT[:, 0:1])
        rcp(ss, ss, R, R)

    mset("pool", epsT[:], EPS, [epsT])
    dma("pool", ident[:], tin["ident"].ap(), [], [ident])
    dma("sp", ndl[:], tin["negdelta"].ap(), [], [ndl])

    def phase_A(l, src):
        ar.reset()
        xt = [ar.alloc("xt", [128, D], F32) for _ in range(3)]
        junk = ar.alloc("junk", [128, D], BF16)
        ssA = [ar.alloc("ss", [128, 1], F32) for _ in range(3)]
        xn = [ar.alloc("xn", [128, D], BF16) for _ in range(2)]
        xT = [ar.alloc("xT", [128, 8, 512], BF16) for _ in range(2)]
        for ti in range(NTILE):
            xTt = xT[ti % 2]
            for s in range(4):
                i = ti * 4 + s
                x_, ss_, xn_ = xt[i % 3], ssA[i % 3], xn[i % 2]
                dma("sp", x_[:], src.h.ap()[i * 128:(i + 1) * 128, :], [src], [x_])
                act(junk[:], x_[:], AF.Square, [x_], [junk, ss_], accum=ss_[:])
                rstd_from_ss(ss_[:], D, [ss_])
                stt("dve", xn_[:], x_[:], ss_[:, 0:1], gT[:, 0, :], ALU.mult, ALU.mult, [x_, ss_, gT], [xn_])
                pt = next_pt()
                for k in range(8):
                    tr(pt[:, k * 128:(k + 1) * 128], xn_[:, k * 128:(k + 1) * 128], [xn_], [pt])
                cp("act" if s % 2 else "dve", xTt[:, :, s * 128:(s + 1) * 128],
                   pt[:].rearrange("p (k t) -> p k t", t=128), [pt], [xTt])
            dma("pool", xnT_s.h.ap().rearrange("(k p) t -> p k t", p=128)[:, :, ti * 512:(ti + 1) * 512], xTt[:], [xTt], [xnT_s])

    def load_layer_consts(l):
        for j in range(4):
            dma("sp", gT[:, j, :], bass.AP(ngain, (l * 4 + j) * D, [[0, 128], [1, D]]), [], [gT])
        for h in range(6):
            off = l * 128 + (0 if h < 4 else 64)
            dma("sp", gqk[:, h, :], bass.AP(qkn, off, [[0, 128], [1, 64]]), [], [gqk])
        dma("sp", cvp[:], convp.ap()[l], [], [cvp])
        dma("sp", scw[:], scwp.ap()[l], [], [scw])
        dma("sp", hbi[:], hbiasp.ap()[l], [], [hbi])

    def phase_R(l):
        ar.reset()
        rd = ar.alloc("rd", [128, 8], F32)
        lg = ar.alloc("lg", [128, 8], F32)
        tmp = ar.alloc("tmpR", [128, 128], F32)
        dpos = ar.alloc("dpos", [128, 128], F32)
        dneg = ar.alloc("dneg", [128, 128], F32)
        iqf = ar.alloc("iqf", [128, 128], F32)
        iqb = ar.alloc("iqb", [128, 128], F32)
        jk = ar.alloc("jk", [128, 2], F32)
        lsel = ar.alloc("lsel", [128, 2, 2], F32)
        dma("sp", rd[:], bass.AP(rde, l * 8, [[0, 128], [1, 8]]), [], [rd])
        dma("sp", dpos[:], tin["dpos"].ap(), [], [dpos])
        dma("sp", dneg[:], tin["dneg"].ap(), [], [dneg])
        dma("sp", iqf[:], tin["iqf"].ap(), [], [iqf])
        dma("sp", iqb[:], tin["iqb"].ap(), [], [iqb])
        dma("sp", jk[:], tin["jk"].ap(), [], [jk])
        act(lg[:], rd[:], AF.Exp, [rd], [lg], scale=-math.log(2.0))
        act(lg[:], lg[:], AF.Ln, [lg], [lg], scale=-1.0, bias=1.0)
        for h in range(4):
            ts("dve", tmp[:], dpos[:], lg[:, h:h + 1], None, ALU.mult, None, [dpos, lg], [tmp])
            stt("dve", tmp[:], dneg[:], lg[:, 4 + h:5 + h], tmp[:], ALU.mult, ALU.add, [dneg, lg, tmp], [tmp])
            act(Dm[:, h, :], tmp[:], AF.Exp, [tmp], [Dm])
            act(wkk[:, h, 0:1], jk[:, 0:1], AF.Exp, [jk, lg], [wkk], scale=lg[:, h:h + 1])
            act(wkk[:, h, 1:2], jk[:, 1:2], AF.Exp, [jk, lg], [wkk], scale=lg[:, 4 + h:5 + h])
        for p in range(2):
            for half in range(2):
                h = 2 * p + half
                r0, r1 = half * 64, half * 64 + 64
                for d in range(2):
                    cp("dve", lsel[r0:r1, p, d:d + 1], lg[r0:r1, 4 * d + h:4 * d + h + 1], [lg], [lsel])
            act(wqf[:, p, :], iqf[:], AF.Exp, [iqf, lsel], [wqf], scale=lsel[:, p, 0:1])
            act(wqb[:, p, :], iqb[:], AF.Exp, [iqb, lsel], [wqb], scale=lsel[:, p, 1:2])
            act(gch[:, p, :], lsel[:, p, :], AF.Exp, [lsel], [gch], scale=128.0)

    def phase_F(l, gi):
        Lg, _ = cfg.groups[gi]
        ar.reset()
        NCH = (2 * Lg) // 512
        w1 = ar.alloc("w1", [33, 64], F32)
        w2 = ar.alloc("w2", [64, 64], F32)
        w3 = ar.alloc("w3", [64, 512], F32)
        hv = ar.alloc("hv", [64, 4], F32)
        ssq = ar.alloc("ssq", [128, 2, NCH], F32)
        ze = [ar.alloc("ze", [33, 512], F32) for _ in range(2)]
        trw = [ar.alloc("trw", [128, 512], F32) for _ in range(2)]
        a1 = [ar.alloc("a1", [64, 512], F32) for _ in range(2)]
        tw = [ar.alloc("tw", [64, 512], F32) for _ in range(2)]
        h1 = [ar.alloc("h1", [64, 512], F32) for _ in range(2)]
        dec = [ar.alloc("dec", [128, 512], F32) for _ in range(2)]
        kf = [ar.alloc("kf", [128, 512], F32) for _ in range(2)]
        kb = [ar.alloc("kb", [128, 512], BF16) for _ in range(4)]
        jk2 = ar.alloc("jk2", [128, 512], BF16)
        dma("sp", w1[:], hy_w1.ap()[l], [], [w1])
        dma("sp", w2[:], hy_w2.ap()[l], [], [w2])
        dma("sp", w3[:], hy_w3.ap()[l], [], [w3])
        dma("sp", hv[:], hyvec.ap()[l], [], [hv])
        mset("pool", ssq[:], 0.0, [ssq])
        PI, TWO_PI = math.pi, 2 * math.pi

        def sin_layer(ps, dst, bcol, fcol, a_, t_):
            ts("dve", a_[:], ps[0:64, :], hv[:, bcol:bcol + 1], hv[:, fcol:fcol + 1], ALU.add, ALU.mult, [hv, ps], [a_])
            for _ in range(1):
                ts("dve", t_[:], a_[:], PI, None, ALU.is_gt, None, [a_], [t_])
                stt("dve", a_[:], t_[:], -TWO_PI, a_[:], ALU.mult, ALU.add, [t_, a_], [a_])
                ts("dve", t_[:], a_[:], -PI, None, ALU.is_lt, None, [a_], [t_])
                stt("dve", a_[:], t_[:], TWO_PI, a_[:], ALU.mult, ALU.add, [t_, a_], [a_])
            act(dst[:], a_[:], AF.Sin, [a_], [dst])

        for ch in range(NCH):
            m0 = ch * 512
            ze_, tr_, a_, t_, h_ = ze[ch % 2], trw[ch % 2], a1[ch % 2], tw[ch % 2], h1[ch % 2]
            dma("sp", ze_[:], tin[f"zext{gi}"].ap()[:, m0:m0 + 512], [], [ze_])
            dma("sp", tr_[:], bass.AP(tin[f"trow{gi}"], m0, [[0, 128], [1, 512]]), [], [tr_])
            p1 = next_pf()
            mm(p1[0:64, :], w1[:], ze_[:], True, True, [w1, ze_], [p1])
            sin_layer(p1, h_, 0, 2, a_, t_)
            p2 = next_pf()
            mm(p2[0:64, :], w2[:], h_[:], True, True, [w2, h_], [p2])
            sin_layer(p2, h_, 1, 3, a_, t_)
            dirn = 1 if m0 < Lg else 0
            for cc in range(2):
                p3 = next_pf()
                c0 = dirn * 256 + cc * 128
                mm(p3[:], w3[:, c0:c0 + 128], h_[:], True, True, [w3, h_], [p3])
                d_, k_, kb_ = dec[cc], kf[cc], kb[(ch * 2 + cc) % 4]
                act(d_[:], tr_[:], AF.Exp, [tr_, ndl], [d_], scale=ndl[:, cc:cc + 1])
                tt("dve", k_[:], p3[:], d_[:], ALU.mult, [p3, d_], [k_])
                if ch == 0:
                    mset("pool", k_[:, 0:1], 0.0, [k_])
                act(jk2[:], k_[:], AF.Square, [k_], [jk2, ssq], accum=ssq[:, cc, ch:ch + 1])
                cp("pool", kb_[:], k_[:], [k_], [kb_])
                dma("pool", kfl[gi].h.ap()[cc * 128:(cc + 1) * 128, m0:m0 + 512], kb_[:], [kb_], [kfl[gi]])
        red("dve", rnrm[gi][:], ssq[:], [ssq], [rnrm[gi]])
        rstd_from_ss(rnrm[gi][:], 1.0, [rnrm[gi]])

    def phase_1b(l):
        ar.reset()
        W1 = ar.alloc("W1", [128, 8, 3072], BF16)
        for k in range(8):
            dma("pool", W1[:, k, :], w_in.ap()[l, k * 128:(k + 1) * 128, 0:3072], [], [W1])
        xe = [ar.alloc("xe", [128, 8, 514], BF16) for _ in range(2)]
        Ct = [ar.alloc("Ct", [128, 4, 8, 64], F32) for _ in range(2)]
        St = [ar.alloc("St", [128, 4, 8, 64], F32) for _ in range(2)]
        qk32 = [ar.alloc("qk32", [128, 384], F32) for _ in range(2)]
        sq = ar.alloc("sq", [128, 384], F32)
        ss6 = [ar.alloc("ss6", [128, 6], F32) for _ in range(2)]
        t1 = [ar.alloc("t1", [128, 512], F32) for _ in range(2)]
        t2 = [ar.alloc("t2", [128, 512], F32) for _ in range(2)]
        qkbf = [ar.alloc("qkbf", [128, 384], BF16) for _ in range(2)]
        qkT = [ar.alloc("qkT", [128, 3, 512], BF16) for _ in range(2)]
        vt = [ar.alloc("vt", [128, 4, 2, 65], BF16) for _ in range(2)]
        r32 = [ar.alloc("r32", [128, 512], F32) for _ in range(2)]
        rtok = [ar.alloc("rtok", [128, 4, 1024], BF16) for _ in range(2)]
        pext = [ar.alloc("pext", [128, 514], F32) for _ in range(6)]
        uu = [ar.alloc("uu", [128, 512], F32) for _ in range(6)]
        z32 = [ar.alloc("z32", [128, 512], F32) for _ in range(2)]
        obf = [ar.alloc("obf", [128, 512], BF16) for _ in range(4)]
        zrv = [ar.alloc("zrv", [128, 512], BF16) for _ in range(2)]
        zst = [ar.alloc("zst", [64, 8, 256], BF16)] * 2
        gext = [ar.alloc("gext", [128, 514], F32) for _ in range(2)]
        for v_ in vt:
            mset("pool", v_[:], 1.0, [v_])
        PH = PF[5]
        it = [0]

        for ti in range(NTILE):
            t0 = ti * 512
            sidx = [i for i, (s0, L) in enumerate(cfg.seqs) if s0 <= t0 < s0 + L][0]
            s0, L = cfg.seqs[sidx]
            pos0 = t0 - s0
            xe_, Ct_, St_ = xe[ti % 2], Ct[ti % 2], St[ti % 2]
            lo = max(t0 - 1, s0)
            hi = min(t0 + 513, s0 + L)
            c_lo = lo - (t0 - 1)
            dma("sp", xe_[:, :, c_lo:c_lo + (hi - lo)], xnT_s.h.ap().rearrange("(k p) t -> p k t", p=128)[:, :, lo:hi], [xnT_s], [xe_])
            if c_lo > 0:
                mset("pool", xe_[:, :, 0:1], 0.0, [xe_])
            if hi < t0 + 513:
                mset("pool", xe_[:, :, 513:514], 0.0, [xe_])
            for s in range(4):
                dma("sp", Ct_[:, s, :, :], bass.AP(tin["ropeC"], (pos0 + s * 128) * 64, [[64, 128], [0, 8], [1, 64]]), [], [Ct_])
                dma("sp", St_[:, s, :, :], bass.AP(tin["ropeS"], (pos0 + s * 128) * 64, [[64, 128], [0, 8], [1, 64]]), [], [St_])
            qkT_, vt_, rtok_ = qkT[ti % 2], vt[ti % 2], rtok[ti % 2]

            def rope(src, nh, Cs, Ss, t1_, t2_):
                w = nh * 64
                tt("dve", t1_[:, 0:w], src[:, 0:w], Cs, ALU.mult, [src, Ct_], [t1_])
                sw = src.ap(16, [[32, nh * 2], [-16, 2], [1, 16]])
                tt("dve", t2_[:, 0:w].rearrange("p (a h f) -> p a h f", h=2, f=16), sw, Ss, ALU.mult, [src, St_], [t2_])

            deferred = []

            def flush():
                while deferred:
                    deferred.pop(0)()

            for s in range(4):
                j = it[0]
                it[0] += 1
                q32, ss_, t1_, t2_, qb_ = qk32[j % 2], ss6[j % 2], t1[j % 2], t2[j % 2], qkbf[j % 2]
                xs = slice(1 + s * 128, 1 + s * 128 + 128)
                pa = next_pf(5)
                for k in range(8):
                    mm(pa[:], xe_[:, k, xs], W1[:, k, 0:512], k == 0, k == 7, [xe_, W1], [pa])
                pr = next_pf(5)
                for k in range(8):
                    mm(pr[:], xe_[:, k, xs], W1[:, k, 1280:1792], k == 0, k == 7, [xe_, W1], [pr])
                pv = next_pf(5)
                for k in range(8):
                    mm(pv[:], xe_[:, k, xs], W1[:, k, 1792:2304], k == 0, k == 7, [xe_, W1], [pv])
                flush()
                act(q32[:], pa[:, 0:384], AF.Copy, [pa], [q32])
                act(vt_[:, s, :, 0:64], pa[:, 384:512].rearrange("p (g d) -> p g d", d=64), AF.Copy, [pa], [vt_])
                tt("dve", sq[:], q32[:], q32[:], ALU.mult, [q32], [sq])
                red("dve", ss_[:], sq[:].rearrange("p (h d) -> p h d", d=64), [sq], [ss_])
                rstd_from_ss(ss_[:], 64.0, [ss_])
                tt("dve", q32[:].rearrange("p (h d) -> p h d", d=64), q32[:].rearrange("p (h d) -> p h d", d=64),
                   ss_.ap(0, [[1, 6], [0, 64]]), ALU.mult, [q32, ss_], [q32])
                tt("dve", q32[:], q32[:], gqk[:].rearrange("p h d -> p (h d)"), ALU.mult, [q32, gqk], [q32])
                rope(q32, 6, Ct_[:, s, 0:6, :].rearrange("p h d -> p (h d)"),
                     St_[:, s, 0:6, :].rearrange("p h (a b f) -> p (h a) b f", a=2, b=2, f=16), t1_, t2_)
                tt("dve", qb_.ap(0, [[64, 2], [128, 2], [1, 64]]), t1_.ap(0, [[128, 2], [64, 2], [1, 64]]),
                   t2_.ap(0, [[128, 2], [64, 2], [1, 64]]), ALU.add, [t1_, t2_], [qb_])
                tt("dve", qb_[:, 256:384], t1_[:, 256:384], t2_[:, 256:384], ALU.add, [t1_, t2_], [qb_])

                def do_tr(qb_=qb_, s=s):
                    pt = next_pt()
                    for c3 in range(3):
                        tr(pt[:, c3 * 128:(c3 + 1) * 128], qb_[:, c3 * 128:(c3 + 1) * 128], [qb_], [pt])
                    cp("act", qkT_[:, :, s * 128:(s + 1) * 128], pt[:, 0:384].rearrange("p (c t) -> p c t", t=128), [pt], [qkT_])
                deferred.append(do_tr)
                r_, rt1, rt2 = r32[j % 2], t1[(j + 1) % 2], t2[(j + 1) % 2]
                act(r_[:, 0:256], pr[:, 0:256], AF.Copy, [pr], [r_])
                act(r_[:, 256:512], pr[:, 256:512], AF.Copy, [pr], [r_], scale=0.125)
                rope(r_, 8, Ct_[:, s, :, :].rearrange("p h d -> p (h d)"),
                     St_[:, s, :, :].rearrange("p h (a b f) -> p (h a) b f", a=2, b=2, f=16), rt1, rt2)
                tt("dve", rtok_[:, s, 0:512], rt1[:], rt2[:], ALU.add, [rt1, rt2], [rtok_])
                act(rtok_[:, s, 512:768], pv[:, 0:256], AF.Copy, [pv], [rtok_])
                act(rtok_[:, s, 768:1024], pv[:, 256:512], AF.Silu, [pv], [rtok_])

            def tail_dmas():
                dma("pool", qT_s.h.ap().rearrange("(c p) t -> p c t", p=128)[:, :, t0:t0 + 512], qkT_[:, 0:2, :], [qkT_], [qT_s])
                dma("pool", kT_s.h.ap()[:, t0:t0 + 512], qkT_[:, 2, :], [qkT_], [kT_s])
            deferred.append(tail_dmas)
            dma("pool", v_s.h.ap()[t0:t0 + 512, :].rearrange("(s p) c -> p s c", p=128), vt_[:].rearrange("p s g d -> p s (g d)"), [vt_], [v_s])
            dma("pool", ret_s.h.ap()[t0:t0 + 512, :].rearrange("(s p) c -> p s c", p=128), rtok_[:], [rtok_], [ret_s])

            def fm_chunk(col0, hslot, dst, halo):
                pm = next_pf(5)
                for k in range(8):
                    mm(pm[:], W1[:, k, col0:col0 + 128], xe_[:, k, 1:513], k == 0, k == 7, [xe_, W1], [pm])
                if halo:
                    for k in range(8):
                        mm(PH[:, hslot * 2:hslot * 2 + 2], W1[:, k, col0:col0 + 128], xe_.ap(k * 514, [[513, 2]]),
                           k == 0, k == 7, [xe_, W1], [PH])
                    cp("dve", dst.ap(0, [[513, 2]]), PH[:, hslot * 2:hslot * 2 + 2], [PH], [dst])
                act(dst[:, 1:513], pm[:], AF.Copy, [pm], [dst])

            def conv3(eng, dst, src, wcol, wt, bias):
                if bias is not None:
                    ts(eng, dst[:], src[:, 1:513], wt[:, wcol, 1:2], bias, ALU.mult, ALU.add, [src, wt], [dst])
                else:
                    ts(eng, dst[:], src[:, 1:513], wt[:, wcol, 1:2], None, ALU.mult, None, [src, wt], [dst])
                stt(eng, dst[:], src[:, 0:512], wt[:, wcol, 0:1], dst[:], ALU.mult, ALU.add, [src, wt, dst], [dst])
                stt(eng, dst[:], src[:, 2:514], wt[:, wcol, 2:3], dst[:], ALU.mult, ALU.add, [src, wt, dst], [dst])

            for jc in range(6):
                fm_chunk(512 + jc * 128, jc, pext[jc], True)
                if jc == 0:
                    flush()
                conv3("dve", uu[jc], pext[jc], jc, cvp, cvp[:, jc, 3:4])
            for jc in range(6):
                fm_chunk(2304 + jc * 128, 6 + jc, pext[jc], jc >= 2)
            for cc in range(2):
                ob = obf[cc]
                cp("pool", ob[:], uu[cc][:], [uu[cc]], [ob])
                dma("pool", x0T_s.h.ap()[cc * 128:(cc + 1) * 128, t0:t0 + 512], ob[:], [ob], [x0T_s])
                z_ = z32[cc]
                tt("pool", z_[:], uu[4 + cc][:], uu[2 + cc][:], ALU.mult, [uu[4 + cc], uu[2 + cc]], [z_])
                ob2 = obf[2 + cc]
                cp("pool", ob2[:], z_[:], [z_], [ob2])
                dma("pool", zT_s.h.ap()[cc * 128:(cc + 1) * 128, t0:t0 + 512], ob2[:], [ob2], [zT_s])
                zr_ = zrv[cc]
                cp("pool", zr_[:].rearrange("p (a b) -> p a b", b=64), z_.ap(63, [[64, 8], [-1, 64]]), [z_], [zr_])
                pt = next_pt()
                for b8 in range(8):
                    tr(pt[0:64, b8 * 128:(b8 + 1) * 128], zr_[:, b8 * 64:(b8 + 1) * 64], [zr_], [pt])
                zs_ = zst[ti % 2]
                cp("act", zs_[:, :, cc * 128:(cc + 1) * 128], pt[0:64, :].rearrange("p (a c) -> p a c", c=128), [pt], [zs_])
            dma("pool", zr_s.h.ap()[:, t0 // 64:t0 // 64 + 8, :], zst[ti % 2][:], [zst[ti % 2]], [zr_s])
            for cc in range(2):
                g_ = gext[cc]
                tt("pool", g_[:], pext[2 + cc][:], pext[4 + cc][:], ALU.mult, [pext[2 + cc], pext[4 + cc]], [g_])
                cv = uu[cc]
                conv3("dve", cv, g_, cc, scw, None)
                ob = obf[cc]
                tt("dve", ob[:], cv[:], pext[cc][:, 1:513], ALU.mult, [cv, pext[cc]], [ob])
                dma("pool", brT_s[3].h.ap()[cc * 128:(cc + 1) * 128, t0:t0 + 512], ob[:], [ob], [brT_s[3]])

    def phase_2A(sidx):
        s0, L = cfg.seqs[sidx]
        ar.reset()
        NKC = L // 128
        qT = ar.alloc("qT", [128, 2, L], BF16)
        kT = ar.alloc("kT", [128, L], BF16)
        V = ar.alloc("V", [128, NKC, 130], BF16)
        dma("sp", qT[:], qT_s.h.ap().rearrange("(c p) t -> p c t", p=128)[:, :, s0:s0 + L], [qT_s], [qT])
        dma("sp", kT[:], kT_s.h.ap()[:, s0:s0 + L], [kT_s], [kT])
        for v0 in range(0, NKC, 8):
            v1 = min(NKC, v0 + 8)
            dma("sp", V[:, v0:v1, :], v_s.h.ap()[s0 + v0 * 128:s0 + v1 * 128, :].rearrange("(n p) c -> p n c", p=128), [v_s], [V])
        Pt = [ar.alloc("Pt", [128, 512], BF16) for _ in range(3)]
        ones65 = ar.alloc("ones65", [65, 64], F32)
        rrow = [ar.alloc("rrow", [65, 512], F32) for _ in range(2)]
        rbc = [ar.alloc("rbc", [64, 512], F32) for _ in range(2)]
        obf = [ar.alloc("obfA", [64, 512], BF16) for _ in range(2)]
        mset("pool", ones65[:], 1.0, [ones65])
        steps = [(qb, hh, kc) for qb in range(L // 512) for hh in range(4) for kc in range(NKC)]
        nst = len(steps)
        LOOK = 2

        def emit_ST(i):
            qb, hh, kc = steps[i]
            cq, g = hh // 2, hh % 2
            rows = slice(64 * g, 64 * g + 64)
            ps = PF[i % 3]
            mm(ps[:], kT[rows, kc * 128:(kc + 1) * 128], qT[rows, cq, qb * 512:(qb + 1) * 512], True, True, [kT, qT], [ps])

        for i in range(min(LOOK, nst)):
            emit_ST(i)
        nh = 0
        for i in range(nst):
            if i + LOOK < nst:
                emit_ST(i + LOOK)
            qb, hh, kc = steps[i]
            cq, g = hh // 2, hh % 2
            head = 2 * g + cq
            po = PF[4 + (hh % 2)]
            ps = PF[i % 3]
            P_ = Pt[i % 3]
            act(P_[:], ps[:], AF.Exp, [ps], [P_], scale=0.125)
            mm(po[0:65, :], V[:, kc, g * 65:g * 65 + 65], P_[:], kc == 0, kc == NKC - 1, [P_, V], [po])
            if kc < NKC - 1:
                continue
            rr, rb, ob = rrow[nh % 2], rbc[nh % 2], obf[nh % 2]
            nh += 1
            rcp(rr[64:65, :], po[64:65, :], [po], [rr])
            pd = PF[3]
            mm(pd[0:64, :], ones65[64:65, :], rr[64:65, :], True, True, [ones65, rr], [pd])
            act(rb[:], pd[0:64, :], AF.Copy, [pd], [rb])
            tt("dve", ob[:], po[0:64, :], rb[:], ALU.mult, [po, rb], [ob])
            t0 = s0 + qb * 512
            dma("pool", brT_s[0].h.ap()[head * 64:(head + 1) * 64, t0:t0 + 512], ob[:], [ob], [brT_s[0]])

    def phase_2B(gi):
        Lg, sl = cfg.groups[gi]
        ns = len(sl)
        A = Lg // 128
        A2 = Lg // 64
        NB, NB2 = ns * A, ns * A2
        ar.reset()
        WW = 2 * Lg - 64
        WWA = max(WW, NB2 * 128)
        Zr = ar.alloc("Zr", [128, 128, NB2], BF16)
        Wc = [ar.alloc("Wc", [128, WWA], BF16) for _ in range(3)]
        wb = {(i, cc): Buf(f"wb{i}{cc}") for i in range(3) for cc in range(2)}
        Ysb = [ar.alloc("Ysb", [128, 128, NB], BF16) for _ in range(2)]
        x0t = [ar.alloc("x0t", [128, 512], BF16) for _ in range(2)]
        zt = [ar.alloc("zt", [128, 512], BF16) for _ in range(2)]
        tmpc = [ar.alloc("tmpc", [128, 512], F32) for _ in range(2)]
        obb = [ar.alloc("obb", [128, 512], BF16) for _ in range(2)]
        gb0 = cfg.seqs[sl[0]][0] // 64
        ar.n += 1
        Zl = T(nc.alloc_sbuf_tensor_at(f"Zl_{ar.n}", [128, NB2, 128], BF16, offset=Wc[2].h_off), "Zl")
        zlb = [wb[(2, 0)], wb[(2, 1)]]
        for cc in range(2):
            dma("sp", Zl[cc * 64:(cc + 1) * 64, :, :], zr_s.h.ap()[:, gb0:gb0 + NB2, cc * 128:(cc + 1) * 128], [zr_s], [zlb[cc]])
        for q4 in range(4):
            cp("dve" if q4 % 2 else "pool", Zr[:, q4 * 32:(q4 + 1) * 32, :],
               Zl[:, :, q4 * 32:(q4 + 1) * 32].rearrange("p n c -> p c n"), zlb, [Zr])
        deltas = [0] + [d for d in range(-(2 * A - 1), 2 * A - 1) if d != 0]
        for cl in range(128):
            wi = cl % 3
            W_ = Wc[wi]
            for cc in range(2):
                dma("sp" if cc == 0 else "act", W_[cc * 64:(cc + 1) * 64, 0:WW],
                    bass.AP(kfl[gi].h, (cc * 128 + cl) * 2 * Lg + 1, [[1, 64], [1, WW]]), [kfl[gi]], [wb[(wi, cc)]])
            slot = cl % 8
            if slot == 0:
                py = [PF[(cl // 8) % 2], PF[2 + (cl // 8) % 2]]
            for di, d in enumerate(deltas):
                off = 64 * (d + 2 * A - 1)
                b0 = max(0, (d + 1) // 2)
                b1 = min(A - 1, (2 * A - 1 + d) // 2)
                n = b1 - b0 + 1
                a0 = 2 * b0 - d
                for cc in range(2):
                    rhs = Zr.ap(cl * NB2 + a0, [[A2, ns], [2, n]], cc * 64, 64)
                    out = py[cc].ap(slot * NB + b0, [[A, ns], [1, n]])
                    mm(out, W_[cc * 64:(cc + 1) * 64, off:off + 128], rhs, di == 0, di == len(deltas) - 1,
                       [wb[(wi, cc)], Zr], [py[cc]], skip=True)
            if slot == 7:
                for cc in range(2):
                    act(Ysb[cc][:, cl - 7:cl + 1, :], py[cc][:, 0:8 * NB].rearrange("p (c n) -> p c n", n=NB), AF.Copy, [py[cc]], [Ysb[cc]])
        blk = 0
        for cc in range(2):
            for b4 in range(NB // 4):
                pt = next_pt()
                for bb in range(4):
                    b = b4 * 4 + bb
                    tr(pt[:, bb * 128:(bb + 1) * 128], Ysb[cc].ap(b, [[NB, 128]]), [Ysb[cc]], [pt])
                t0 = gb0 * 64 + b4 * 512
                x0_, z_, tm, ob = x0t[blk % 2], zt[blk % 2], tmpc[blk % 2], obb[blk % 2]
                blk += 1
                dma("sp", x0_[:], x0T_s.h.ap()[cc * 128:(cc + 1) * 128, t0:t0 + 512], [x0T_s], [x0_])
                dma("sp", z_[:], zT_s.h.ap()[cc * 128:(cc + 1) * 128, t0:t0 + 512], [zT_s], [z_])
                ts("dve", tm[:], pt[:, 0:512], rnrm[gi][:, cc:cc + 1], None, ALU.mult, None, [pt, rnrm[gi]], [tm])
                stt("dve", tm[:], z_[:], hbi[:, cc:cc + 1], tm[:], ALU.mult, ALU.add, [z_, hbi, tm], [tm])
                tt("dve", ob[:], tm[:], x0_[:], ALU.mult, [tm, x0_], [ob])
                dma("pool", brT_s[1].h.ap()[cc * 128:(cc + 1) * 128, t0:t0 + 512], ob[:], [ob], [brT_s[1]])

    def phase_2C(sidx):
        s0, L = cfg.seqs[sidx]
        NCK = L // 128
        ar.reset()
        Sb_all = ar.alloc("Sb_all", [128, 2, NCK, 64], BF16)
        Sst = ar.alloc("Sst", [128, 2, 2, 64], F32)
        Sfb = [ar.alloc("Sfb", [128, 2, 64], BF16) for _ in range(2)]
        rt = [ar.alloc("rt", [128, 4, 1024], BF16) for _ in range(2)]
        vk = [ar.alloc("vk", [128, 4, 64], BF16) for _ in range(2)]
        qkTs = [ar.alloc("qkTs", [128, 4, 128], BF16) for _ in range(2)]
        qsf = [ar.alloc("qsf", [128, 2, 128], BF16) for _ in range(2)]
        qsb = [ar.alloc("qsb", [128, 2, 128], BF16) for _ in range(2)]
        Pm = [ar.alloc("Pm", [128, 4, 128], BF16) for _ in range(2)]
        sqo = ar.alloc("sqo", [128, 256], F32)
        sso = [ar.alloc("sso", [128, 4], F32) for _ in range(2)]
        oc32 = [ar.alloc("oc32", [128, 256], F32) for _ in range(2)]
        ocb = [ar.alloc("ocb", [128, 256], BF16) for _ in range(2)]
        ocT = [ar.alloc("ocT", [128, 2, 512], BF16) for _ in range(2)]
        NG = NCK // 4

        def load_group(g, slot):
            t0 = s0 + g * 512
            dma("sp", rt[slot][:], ret_s.h.ap()[t0:t0 + 512, :].rearrange("(s p) c -> p s c", p=128), [ret_s], [rt[slot]])

        def kv_update(rt_, s, d, j):
            vk_ = vk[j % 2]
            tt("dve", vk_[:], rt_[:, s, 512:768].rearrange("p (h e) -> p h e", e=64), wkk.ap(d, [[2, 4], [0, 64]]),
               ALU.mult, [rt_, wkk], [vk_])
            for p in range(2):
                pk = next_pf(4)
                mm(pk[:, 0:128], rt_[:, s, 256 + p * 128:256 + (p + 1) * 128], vk_[:, 2 * p:2 * p + 2, :].rearrange("p h e -> p (h e)"),
                   True, True, [rt_, vk_], [pk])
                for half in range(2):
                    r = slice(half * 64, half * 64 + 64)
                    stt("dve", Sst[r, p, d, :], Sst[r, p, d, :], gch[r, p, d:d + 1], pk[r, half * 64:half * 64 + 64],
                        ALU.mult, ALU.add, [Sst, gch, pk], [Sst])

        mset("pool", Sst[:], 0.0, [Sst])
        j = 0
        for g in reversed(range(NG)):
            slot = g % 2
            load_group(g, slot)
            for s in reversed(range(4)):
                n = g * 4 + s
                cp("act", Sb_all[:, :, n, :], Sst[:, :, 1, :], [Sst], [Sb_all])
                if n > 0:
                    kv_update(rt[slot], s, 1, j)
                    j += 1
        import os
        DBG = int(os.environ.get("DBG2C", "9"))
        for g in range(NG if DBG > 0 else 0):
            slot = g % 2
            load_group(g, slot)
            rt_ = rt[slot]
            ocT_ = ocT[g % 2]
            for s in range(4):
                n = g * 4 + s
                qk_, qf_, qb_, Pm_, Sf_ = qkTs[n % 2], qsf[n % 2], qsb[n % 2], Pm[n % 2], Sfb[n % 2]
                SK = os.environ.get("SKIP", "").split(",")
                pt = next_pt()
                if "tr" not in SK:
                    for c4 in range(4):
                        tr(pt[:, c4 * 128:(c4 + 1) * 128], rt_[:, s, c4 * 128:(c4 + 1) * 128], [rt_], [pt])
                if "cpq" not in SK:
                    cp("act", qk_[:], pt[:, 0:512].rearrange("p (c t) -> p c t", t=128), [pt], [qk_])
                if "qf" not in SK:
                    tt("dve", qf_[:], qk_[:, 0:2, :], wqf[:], ALU.mult, [qk_, wqf], [qf_])
                    tt("dve", qb_[:], qk_[:, 0:2, :], wqb[:], ALU.mult, [qk_, wqb], [qb_])
                if "sf" not in SK:
                    cp("act", Sf_[:], Sst[:, :, 0, :], [Sst], [Sf_])
                psa = PF[(2 * n) % 4]
                psb = PF[(2 * n + 1) % 4]
                for h in range(4):
                    p, half = h // 2, h % 2
                    r = slice(half * 64, half * 64 + 64)
                    pdst = psb if half else psa
                    mm(pdst[:, p * 128:(p + 1) * 128], qk_[r, 2 + p, :], qk_[r, p, :], True, True, [qk_], [pdst])
                for half, pdst in ((0, psa), (1, psb)):
                    tt("dve", Pm_.ap(half * 128, [[256, 2], [1, 128]]), pdst[:, 0:256].rearrange("p (a i) -> p a i", i=128),
                       Dm.ap(half * 128, [[256, 2], [1, 128]]), ALU.mult, [pdst, Dm], [Pm_])
                if DBG < 2:
                    continue
                po = PF[4 + n % 2]
                for h in range(4):
                    p, half = h // 2, h % 2
                    r = slice(half * 64, half * 64 + 64)
                    oh = po[:, h * 64:(h + 1) * 64]
                    mm(oh, Pm_[:, h, :], rt_[:, s, 512 + h * 64:512 + (h + 1) * 64], True, False, [Pm_, rt_], [po], skip=True)
                    mm(oh, qf_[r, p, :], Sf_[r, p, :], False, False, [qf_, Sf_], [po], skip=True)
                    mm(oh, qb_[r, p, :], Sb_all[r, p, n, :], False, True, [qb_, Sb_all], [po], skip=True)
                if DBG < 3:
                    continue
                ss_, o32, ob_ = sso[n % 2], oc32[n % 2], ocb[n % 2]
                act(o32[:], po[:, 0:256], AF.Copy, [po], [o32])
                tt("dve", sqo[:], o32[:], o32[:], ALU.mult, [o32], [sqo])
                red("dve", ss_[:], sqo[:].rearrange("p (h e) -> p h e", e=64), [sqo], [ss_])
                rstd_from_ss(ss_[:], 64.0, [ss_])
                tt("dve", o32[:].rearrange("p (h e) -> p h e", e=64), o32[:].rearrange("p (h e) -> p h e", e=64),
                   ss_.ap(0, [[1, 4], [0, 64]]), ALU.mult, [o32, ss_], [o32])
                tt("dve", ob_[:], o32[:], rt_[:, s, 768:1024], ALU.mult, [o32, rt_], [ob_])
                pt2 = next_pt()
                for c2 in range(2):
                    tr(pt2[:, c2 * 128:(c2 + 1) * 128], ob_[:, c2 * 128:(c2 + 1) * 128], [ob_], [pt2])
                cp("act", ocT_[:, :, s * 128:(s + 1) * 128], pt2[:, 0:256].rearrange("p (c t) -> p c t", t=128), [pt2], [ocT_])
                if n < NCK - 1:
                    kv_update(rt_, s, 0, n)
            t0 = s0 + g * 512
            if DBG < 3:
                continue
            dma("pool", brT_s[2].h.ap().rearrange("(c p) t -> p c t", p=128)[:, :, t0:t0 + 512], ocT_[:], [ocT_], [brT_s[2]])

    def phase_3a(l, xsrc):
        ar.reset()
        Wg = ar.alloc("Wg", [128, 8, 4096], BF16)
        Wb = ar.alloc("Wb", [128, 4, 2, D], BF16)
        Wo = ar.alloc("Wo", [128, 8, D], BF16)
        for k in range(8):
            dma("pool", Wg[:, k, :], w_in.ap()[l, k * 128:(k + 1) * 128, 3072:7168], [], [Wg])
            dma("pool", Wo[:, k, :], w_o.ap()[l, k * 128:(k + 1) * 128, :], [], [Wo])
        for n in range(4):
            dma("pool", Wb[:, n, :, :], w_br.ap()[l, n].rearrange("(k p) c -> p k c", p=128), [], [Wb])
        xT = [ar.alloc("xT3", [128, 8, 512], BF16) for _ in range(2)]
        br = [ar.alloc("br3", [128, 4, 2, 512], BF16) for _ in range(2)]
        sg = [ar.alloc("sg", [128, 512], F32) for _ in range(2)]
        mg = [ar.alloc("mg", [128, 512], F32) for _ in range(2)]
        tm = [ar.alloc("tm3", [128, 512], F32) for _ in range(2)]
        mT = ar.alloc("mT", [128, 8, 512], BF16)
        xt = [ar.alloc("xt3", [128, D], F32)] * 2
        y32 = [ar.alloc("y32", [128, D], F32) for _ in range(2)]
        junk = ar.alloc("junk3", [128, D], BF16)
        ss2 = [ar.alloc("ss2", [128, 2], F32) for _ in range(2)]
        ss1 = [ar.alloc("ss1", [128, 1], F32) for _ in range(2)]
        hn = [ar.alloc("hn", [128, D], BF16) for _ in range(2)]
        hT = [ar.alloc("hT", [128, 8, 512], BF16) for _ in range(2)]
        jj = 0
        deferred3 = []

        def flush3():
            while deferred3:
                deferred3.pop(0)()

        for ti in range(NTILE):
            t0 = ti * 512
            xT_, br_ = xT[ti % 2], br[ti % 2]
            dma("sp", xT_[:], xnT_s.h.ap().rearrange("(k p) t -> p k t", p=128)[:, :, t0:t0 + 512], [xnT_s], [xT_])
            for n in range(4):
                dma("sp", br_[:, n, :, :], brT_s[n].h.ap().rearrange("(c p) t -> p c t", p=128)[:, :, t0:t0 + 512], [brT_s[n]], [br_])
            for j in range(8):
                mg_ = mg[j % 2]
                for n in range(4):
                    pg = next_pf()
                    for k in range(8):
                        mm(pg[:], Wg[:, k, n * D + j * 128:n * D + (j + 1) * 128], xT_[:, k, :], k == 0, k == 7, [Wg, xT_], [pg])
                    pp = next_pf()
                    for kk in range(2):
                        mm(pp[:], Wb[:, n, kk, j * 128:(j + 1) * 128], br_[:, n, kk, :], kk == 0, kk == 1, [Wb, br_], [pp])
                    if j == 0 and n == 0:
                        flush3()
                    sg_ = sg[jj % 2]
                    jj += 1
                    act(sg_[:], pg[:], AF.Sigmoid, [pg], [sg_])
                    if n == 0:
                        tt("dve", mg_[:], sg_[:], pp[:], ALU.mult, [sg_, pp], [mg_])
                    else:
                        tm_ = tm[jj % 2]
                        tt("dve", tm_[:], sg_[:], pp[:], ALU.mult, [sg_, pp], [tm_])
                        if n < 3:
                            tt("pool", mg_[:], mg_[:], tm_[:], ALU.add, [mg_, tm_], [mg_])
                        else:
                            tt("pool", mT[:, j, :], mg_[:], tm_[:], ALU.add, [mg_, tm_], [mT])
            hT_ = hT[ti % 2]
            for s in range(4):
                i = ti * 4 + s
                x_, y_, s2, s1, hn_ = xt[i % 2], y32[i % 2], ss2[i % 2], ss1[i % 2], hn[i % 2]
                dma("sp", x_[:], xsrc.h.ap()[i * 128:(i + 1) * 128, :], [xsrc], [x_])
                pos = []
                for nn in range(2):
                    po = next_pf()
                    pos.append(po)
                    for k in range(8):
                        mm(po[:], mT[:, k, s * 128:(s + 1) * 128], Wo[:, k, nn * 512:(nn + 1) * 512], k == 0, k == 7, [mT, Wo], [po])
                    act(junk[:, nn * 512:(nn + 1) * 512], po[:], AF.Square, [po], [junk, s2], accum=s2[:, nn:nn + 1])
                flush3()
                tt("dve", s1[:], s2[:, 0:1], s2[:, 1:2], ALU.add, [s2], [s1])
                rstd_from_ss(s1[:], D, [s1])
                for nn in range(2):
                    stt("dve", y_[:, nn * 512:(nn + 1) * 512], pos[nn][:], s1[:, 0:1], gT[:, 1, nn * 512:(nn + 1) * 512],
                        ALU.mult, ALU.mult, [pos[nn], s1, gT], [y_])
                tt("pool", y_[:], y_[:], x_[:], ALU.add, [y_, x_], [y_])
                dma("pool", hbuf.h.ap()[i * 128:(i + 1) * 128, :], y_[:], [y_], [hbuf])
                act(junk[:], y_[:], AF.Square, [y_], [junk, s1], accum=s1[:])
                rstd_from_ss(s1[:], D, [s1])
                stt("dve", hn_[:], y_[:], s1[:, 0:1], gT[:, 2, :], ALU.mult, ALU.mult, [y_, s1, gT], [hn_])
                def do_tr(hn_=hn_, hT_=hT_, s=s):
                    pt = next_pt()
                    for k in range(8):
                        tr(pt[:, k * 128:(k + 1) * 128], hn_[:, k * 128:(k + 1) * 128], [hn_], [pt])
                    cp("act", hT_[:, :, s * 128:(s + 1) * 128], pt[:].rearrange("p (k t) -> p k t", t=128), [pt], [hT_])
                deferred3.append(do_tr)

            def do_store(hT_=hT_, t0=t0):
                dma("pool", hnT_s.h.ap().rearrange("(k p) t -> p k t", p=128)[:, :, t0:t0 + 512], hT_[:], [hT_], [hnT_s])
            deferred3.append(do_store)
        flush3()

    def phase_3b(l, dst):
        ar.reset()
        Wi = ar.alloc("Wi", [128, 8, 2 * DFF], BF16)
        Wf = ar.alloc("Wf", [128, 22, D], BF16)
        for k in range(8):
            dma("pool", Wi[:, k, :], w_fi.ap()[l, k * 128:(k + 1) * 128, :], [], [Wi])
        for k in range(22):
            dma("pool", Wf[:, k, :], w_fo.ap()[l, k * 128:(k + 1) * 128, :], [], [Wf])
        hT = [ar.alloc("hT3", [128, 8, 512], BF16) for _ in range(2)]
        sg = [ar.alloc("sgb", [128, 512], BF16) for _ in range(2)]
        fT = ar.alloc("fT", [128, 22, 512], BF16)
        ht = [ar.alloc("ht", [128, D], F32)] * 2
        y32 = [ar.alloc("y32b", [128, D], F32)] * 2
        junk = ar.alloc("junkb", [128, D], BF16)
        ss2 = [ar.alloc("ss2b", [128, 2], F32) for _ in range(2)]
        ss1 = [ar.alloc("ss1b", [128, 1], F32) for _ in range(2)]
        jj = 0
        for ti in range(NTILE):
            t0 = ti * 512
            hT_ = hT[ti % 2]
            dma("sp", hT_[:], hnT_s.h.ap().rearrange("(k p) t -> p k t", p=128)[:, :, t0:t0 + 512], [hnT_s], [hT_])
            for j in range(22):
                pg = next_pf()
                for k in range(8):
                    mm(pg[:], Wi[:, k, j * 128:(j + 1) * 128], hT_[:, k, :], k == 0, k == 7, [Wi, hT_], [pg])
                pu = next_pf()
                for k in range(8):
                    mm(pu[:], Wi[:, k, DFF + j * 128:DFF + (j + 1) * 128], hT_[:, k, :], k == 0, k == 7, [Wi, hT_], [pu])
                sg_ = sg[jj % 2]
                jj += 1
                act(sg_[:], pg[:], AF.Silu, [pg], [sg_])
                tt("dve", fT[:, j, :], sg_[:], pu[:], ALU.mult, [sg_, pu], [fT])
            for s in range(4):
                i = ti * 4 + s
                h_, y_, s2, s1 = ht[i % 2], y32[i % 2], ss2[i % 2], ss1[i % 2]
                dma("sp", h_[:], hbuf.h.ap()[i * 128:(i + 1) * 128, :], [hbuf], [h_])
                pos = []
                for nn in range(2):
                    po = next_pf()
                    pos.append(po)
                    for k in range(22):
                        mm(po[:], fT[:, k, s * 128:(s + 1) * 128], Wf[:, k, nn * 512:(nn + 1) * 512], k == 0, k == 21, [fT, Wf], [po])
                    act(junk[:, nn * 512:(nn + 1) * 512], po[:], AF.Square, [po], [junk, s2], accum=s2[:, nn:nn + 1])
                tt("dve", s1[:], s2[:, 0:1], s2[:, 1:2], ALU.add, [s2], [s1])
                rstd_from_ss(s1[:], D, [s1])
                for nn in range(2):
                    stt("dve", y_[:, nn * 512:(nn + 1) * 512], pos[nn][:], s1[:, 0:1], gT[:, 3, nn * 512:(nn + 1) * 512],
                        ALU.mult, ALU.mult, [pos[nn], s1, gT], [y_])
                tt("pool", y_[:], y_[:], h_[:], ALU.add, [y_, h_], [y_])
                dma("pool", dst.h.ap()[i * 128:(i + 1) * 128, :], y_[:], [y_], [dst])

    PH = getattr(cfg, "phases", None)

    def on(name):
        return PH is None or name in PH

    mk.marks = []

    def mark(name):
        mk.marks.append((name, len(mk.eng_ops["pe"]), len(mk.eng_ops["act"]), len(mk.eng_ops["dve"])))

    for l in range(depth):
        src = xin if l == 0 else x1buf
        dst = x1buf if l < depth - 1 else yout
        barrier()
        mark(f"L{l}:start")
        load_layer_consts(l)
        if on("R"):
            phase_R(l)
        barrier()
        mark(f"L{l}:R")
        if on("F"):
            for gi in range(len(cfg.groups)):
                phase_F(l, gi)
                barrier()
        mark(f"L{l}:F")
        if on("A"):
            phase_A(l, src)
            barrier()
        mark(f"L{l}:A")
        if on("1b"):
            phase_1b(l)
            barrier()
        mark(f"L{l}:1b")
        if on("2A"):
            for sidx in range(len(cfg.seqs)):
                phase_2A(sidx)
                barrier()
                mark(f"L{l}:2A.{sidx}")
        if on("2B"):
            for gi in range(len(cfg.groups)):
                phase_2B(gi)
                barrier()
                mark(f"L{l}:2B.{gi}")
        if on("2C"):
            for sidx in range(len(cfg.seqs)):
                phase_2C(sidx)
                barrier()
            mark(f"L{l}:2C")
        if on("3a"):
            phase_3a(l, src)
            barrier()
        mark(f"L{l}:3a")
        if on("3b"):
            phase_3b(l, dst)
            barrier()
        mark(f"L{l}:3b")

    mk.emit()
    return nc, mk, tabs


def make_in_maps(cfg, inputs, tabs, ncore=NCORE):
    f = lambda a: np.ascontiguousarray(np.asarray(a, dtype=np.float32))
    xs, xp = f(inputs["x_sample"]), f(inputs["x_prompt"])
    dp = cfg.depth
    shared = {
        "w_in": f(inputs["w_in"])[:dp], "w_branch": f(inputs["w_branch"])[:dp], "w_out": f(inputs["w_out"])[:dp],
        "w_ffn_in": f(inputs["w_ffn_in"])[:dp], "w_ffn_out": f(inputs["w_ffn_out"])[:dp],
        "norm_gains": f(inputs["norm_gains"])[:dp],
        "qk_norm": f(inputs["qk_norm"])[:dp].reshape(dp, 128),
        "hy_w1": f(inputs["hy_w1"])[:dp], "hy_w2": f(inputs["hy_w2"])[:dp], "hy_w3": f(inputs["hy_w3"])[:dp],
        "rde": f(inputs["ret_decay_exp"])[:dp].reshape(dp, 8),
    }
    hcw, hcb = f(inputs["hy_conv_w"])[:dp], f(inputs["hy_conv_b"])[:dp]
    cv = np.concatenate([hcw, hcb[:, None, :]], axis=1)
    shared["convp"] = np.ascontiguousarray(cv.reshape(dp, 4, 6, 128).transpose(0, 3, 2, 1))
    shared["scwp"] = np.ascontiguousarray(f(inputs["sc_conv_w"])[:dp].reshape(dp, 3, 2, 128).transpose(0, 3, 2, 1))
    shared["hbiasp"] = np.ascontiguousarray(f(inputs["hy_bias"])[:dp].reshape(dp, 2, 128).transpose(0, 2, 1))
    hv = np.stack([f(inputs["hy_b1"])[:dp], f(inputs["hy_b2"])[:dp], f(inputs["hy_freq"])[:dp, 0], f(inputs["hy_freq"])[:dp, 1]], axis=-1)
    shared["hyvec"] = np.ascontiguousarray(hv)
    shared.update(tabs)
    maps = []
    for c in range(ncore):
        parts = [xs[c]] + [xp[cfg.NP * c + i] for i in range(cfg.NP)]
        m = dict(shared)
        m["xin"] = np.ascontiguousarray(np.concatenate(parts, axis=0))
        maps.append(m)
    return maps


_CACHE = {}


def kernel(**inputs):
    cfg = Cfg()
    if "nc" not in _CACHE:
        _CACHE["nc"] = build(cfg)
    nc, mk, tabs = _CACHE["nc"]
    maps = make_in_maps(cfg, inputs, tabs)
    res = run_bass_kernel_spmd(nc, maps, core_ids=list(range(NCORE)))
    ys = np.stack([r["yout"][:cfg.LS] for r in res.results], axis=0)
    yp = np.concatenate([r["yout"][cfg.LS:].reshape(cfg.NP, cfg.LP, D) for r in res.results], axis=0)
    return (np.ascontiguousarray(yp.astype(np.float32)), np.ascontiguousarray(ys.astype(np.float32)))
```

```python
import math
import numpy as np
import concourse.bass as bass
import concourse.mybir as mybir
from concourse.bass_utils import run_bass_kernel_spmd

F32 = mybir.dt.float32
BF16 = mybir.dt.bfloat16
AF = mybir.ActivationFunctionType
ALU = mybir.AluOpType
AX = mybir.AxisListType

D = 1024
DEPTH = 2
DFF = 2816
NCORE = 8
HD = 64
EPS = 1e-6
ENGS = ("pe", "act", "dve", "pool", "sp")
import os as _os
SAME_ENGINE_SYNC = _os.environ.get("SES", "1") == "1"


class Buf:
    __slots__ = ("name", "w", "r_eng", "r_dma")

    def __init__(self, name=""):
        self.name = name
        self.w = None
        self.r_eng = {}
        self.r_dma = []


class _Op:
    __slots__ = ("eng", "fn", "deps", "dma", "signal", "sigval", "ring", "ringval", "ringprev")

    def __init__(self, eng, fn, deps, dma):
        self.eng = eng
        self.fn = fn
        self.deps = deps
        self.dma = dma
        self.signal = False
        self.sigval = 0
        self.ring = None
        self.ringval = 0
        self.ringprev = 0


class MK:
    def __init__(self, nc, ring_k=8):
        self.nc = nc
        self.ops = []
        self.eng_ops = {e: [] for e in ENGS}
        self.ring_k = ring_k
        self.ring_use = {}
        self.ring_next = {e: 0 for e in ENGS}
        self.world = Buf("world")

    def op(self, eng, fn, reads=(), writes=(), dma=False, barrier=False):
        idx = len(self.ops)
        deps = set()
        reads = list(reads)
        writes = list(writes)
        if barrier:
            writes.append(self.world)
        else:
            reads.append(self.world)
        for b in reads:
            if b.w is not None:
                deps.add(b.w)
        for b in writes:
            if b.w is not None:
                deps.add(b.w)
            deps.update(b.r_eng.values())
            deps.update(b.r_dma)
        o = _Op(eng, fn, deps, dma)
        if dma:
            slot = self.ring_next[eng]
            self.ring_next[eng] = (slot + 1) % self.ring_k
            key = (eng, slot)
            u = self.ring_use.get(key, 0)
            o.ring = key
            o.ringprev = 16 * u
            o.ringval = 16 * (u + 1)
            self.ring_use[key] = u + 1
        self.ops.append(o)
        self.eng_ops[eng].append(idx)
        for b in reads:
            if dma:
                b.r_dma.append(idx)
            else:
                b.r_eng[eng] = idx
        for b in writes:
            b.w = idx
            b.r_eng = {}
            b.r_dma = []
        return idx

    def emit(self):
        nc = self.nc
        ops = self.ops
        esem = {e: nc.alloc_semaphore(name=f"es_{e}") for e in ENGS}
        rsem = {key: nc.alloc_semaphore(name=f"rs_{key[0]}_{key[1]}") for key in self.ring_use}

        def needs_sync(p, c):
            if p.dma:
                return True
            if p.eng == c.eng:
                if p.eng == "pe":
                    return False
                return SAME_ENGINE_SYNC
            return True

        for c in ops:
            for d in c.deps:
                p = ops[d]
                if (not p.dma) and needs_sync(p, c):
                    p.signal = True
        cnt = {e: 0 for e in ENGS}
        for o in ops:
            if (not o.dma) and o.signal:
                cnt[o.eng] += 1
                o.sigval = cnt[o.eng]
        known = {e: {} for e in ENGS}
        waits_of = [None] * len(ops)
        for idx, c in enumerate(ops):
            w = {}
            for d in c.deps:
                p = ops[d]
                if not needs_sync(p, c):
                    continue
                if p.dma:
                    s, v = ("r", p.ring), p.ringval
                else:
                    s, v = ("e", p.eng), p.sigval
                if v > w.get(s, 0):
                    w[s] = v
            if c.dma and c.ringprev > 0:
                s = ("r", c.ring)
                if c.ringprev > w.get(s, 0):
                    w[s] = c.ringprev
            kn = known[c.eng]
            lst = []
            for s, v in w.items():
                if v > kn.get(s, 0):
                    kn[s] = v
                    lst.append((s, v))
            waits_of[idx] = lst
        self.n_waits = sum(len(v) for v in waits_of)
        final_ring = {key: 16 * u for key, u in self.ring_use.items()}

        def semh(s):
            return rsem[s[1]] if s[0] == "r" else esem[s[1]]

        def replay(ename, eng):
            for idx in self.eng_ops[ename]:
                o = ops[idx]
                for s, v in waits_of[idx]:
                    eng.wait_ge(semh(s), v)
                ins = o.fn(eng)
                if o.dma:
                    ins.then_inc(rsem[o.ring], 16)
                elif o.signal:
                    ins.then_inc(esem[ename], 1)
            if ename == "sp":
                for key, v in final_ring.items():
                    eng.wait_ge(rsem[key], v)

        with nc.Block() as block:
            @block.tensor
            def _(e):
                replay("pe", e)

            @block.scalar
            def _(e):
                replay("act", e)

            @block.vector
            def _(e):
                replay("dve", e)

            @block.gpsimd
            def _(e):
                replay("pool", e)

            @block.sync
            def _(e):
                replay("sp", e)


class T:
    __slots__ = ("h", "b", "F")

    def __init__(self, h, name=""):
        self.h = h
        self.b = Buf(name)
        sh = list(h.shape)
        f = 1
        for s in sh[1:]:
            f *= s
        self.F = f

    def __getitem__(self, k):
        return self.h[k]

    def ap(self, off, dims, p0=0, np_=128):
        return bass.AP(self.h, p0 * self.F + off, [[self.F, np_]] + [list(d) for d in dims])


def _dsize(dt):
    return 4 if dt == F32 else 2


class Arena:
    def __init__(self, nc, base, top):
        self.nc = nc
        self.base = base
        self.top = top
        self.off = base
        self.n = 0

    def mark(self):
        return self.off

    def reset(self, to=None):
        self.off = self.base if to is None else to

    def alloc(self, name, shape, dt):
        nb = _dsize(dt)
        for s in shape[1:]:
            nb *= s
        off = (self.off + 31) // 32 * 32
        assert off + nb <= self.top, f"SBUF arena overflow allocating {name}: {off}+{nb} > {self.top}"
        self.n += 1
        h = self.nc.alloc_sbuf_tensor_at(f"{name}_{self.n}", list(shape), dt, offset=off)
        self.off = off + nb
        return T(h, name)


class Cfg:
    def __init__(self, LS=8192, LP=2048, NP=2, depth=DEPTH):
        self.LS, self.LP, self.NP, self.depth = LS, LP, NP, depth
        self.seqs = [(0, LS)] + [(LS + i * LP, LP) for i in range(NP)]
        self.NT = LS + NP * LP
        self.groups = [(LS, [0])] + ([(LP, list(range(1, 1 + NP)))] if NP else [])
        self.LMAX = max(LS, LP)


def host_tables(cfg):
    tabs = {}
    L = cfg.LMAX
    t = np.arange(L)
    r = (t // 64).astype(np.float32)
    c = (t % 64).astype(np.float32)
    inv = (10000.0 ** (-np.arange(16, dtype=np.float32) / 16)).astype(np.float32)
    ang = np.stack([r[:, None] * inv, c[:, None] * inv], axis=1).astype(np.float32)
    cs, sn = np.cos(ang).astype(np.float32), np.sin(ang).astype(np.float32)
    C = np.stack([cs, cs], axis=2)
    S = np.stack([-sn, sn], axis=2)
    tabs["ropeC"] = np.ascontiguousarray(C.reshape(L, 64))
    tabs["ropeS"] = np.ascontiguousarray(S.reshape(L, 64))
    tabs["ident"] = np.eye(128, dtype=np.float32)
    for gi, (Lg, _) in enumerate(cfg.groups):
        m = np.arange(2 * Lg)
        n = np.abs(m - Lg)
        n = np.minimum(n, Lg - 1)
        tt = np.linspace(0.0, 1.0, Lg, dtype=np.float32)
        f = np.linspace(1e-4, 15.0, 16, dtype=np.float32)
        angf = ((2.0 * math.pi / Lg) * np.arange(Lg, dtype=np.float32)[:, None] * f[None, :]).astype(np.float32)
        z = np.concatenate([tt[:, None], np.cos(angf), -np.sin(angf)], axis=-1).astype(np.float32)
        tabs[f"zext{gi}"] = np.ascontiguousarray(z[n].T)
        tabs[f"trow{gi}"] = np.ascontiguousarray(tt[n][None, :])
    deltas = np.abs(np.linspace(math.log(1e-2) / 1.5, math.log(1e-2) / 0.3, 256, dtype=np.float32))
    tabs["negdelta"] = np.ascontiguousarray((-deltas).reshape(2, 128).T.astype(np.float32))
    i = np.arange(128, dtype=np.float32)
    diff = i[None, :] - i[:, None]
    tabs["dpos"] = np.maximum(diff, 0).astype(np.float32)
    tabs["dneg"] = np.maximum(-diff, 0).astype(np.float32)
    tabs["iqf"] = np.tile((i + 1)[None, :], (128, 1)).astype(np.float32)
    tabs["iqb"] = np.tile((128 - i)[None, :], (128, 1)).astype(np.float32)
    tabs["jk"] = np.stack([127 - i, i], axis=1).astype(np.float32)
    return tabs


DEBUG_OUT = set()


def build(cfg):
    nc = bass.Bass("TRN2", target_bir_lowering=False)
    mk = MK(nc)
    NT, LS = cfg.NT, cfg.LS
    depth = cfg.depth
    NTILE = NT // 512

    def din(name, shape, dt=F32):
        return nc.dram_tensor(name, list(shape), dt, kind="ExternalInput")

    def dscr(name, shape, dt):
        if name in DEBUG_OUT:
            return T(nc.dram_tensor(name, list(shape), dt, kind="ExternalOutput"), name)
        return T(nc.dram_tensor(name, list(shape), dt), name)

    xin = T(din("xin", [NT, D]), "xin")
    yout = T(nc.dram_tensor("yout", [NT, D], F32, kind="ExternalOutput"), "yout")
    w_in = din("w_in", [depth, D, 7168])
    w_br = din("w_branch", [depth, 4, 256, D])
    w_o = din("w_out", [depth, D, D])
    w_fi = din("w_ffn_in", [depth, D, 2 * DFF])
    w_fo = din("w_ffn_out", [depth, DFF, D])
    ngain = din("norm_gains", [depth, 4, D])
    qkn = din("qk_norm", [depth, 128])
    convp = din("convp", [depth, 128, 6, 4])
    scwp = din("scwp", [depth, 128, 2, 3])
    hbiasp = din("hbiasp", [depth, 128, 2])
    hy_w1 = din("hy_w1", [depth, 33, 64])
    hy_w2 = din("hy_w2", [depth, 64, 64])
    hy_w3 = din("hy_w3", [depth, 64, 512])
    hyvec = din("hyvec", [depth, 64, 4])
    rde = din("rde", [depth, 8])
    tabs = host_tables(cfg)
    tin = {k: din(k, v.shape) for k, v in tabs.items()}

    x1buf = dscr("x1buf", [NT, D], F32)
    hbuf = dscr("hbuf", [NT, D], F32)
    xnT_s = dscr("xnT_s", [D, NT], BF16)
    hnT_s = dscr("hnT_s", [D, NT], BF16)
    brT_s = [dscr(f"brT{n}", [256, NT], BF16) for n in range(4)]
    qT_s = dscr("qT_s", [256, NT], BF16)
    kT_s = dscr("kT_s", [128, NT], BF16)
    v_s = dscr("v_s", [NT, 130], BF16)
    ret_s = dscr("ret_s", [NT, 1024], BF16)
    x0T_s = dscr("x0T_s", [256, NT], BF16)
    zT_s = dscr("zT_s", [256, NT], BF16)
    zr_s = dscr("zr_s", [128, NT // 128, 256], BF16)
    kfl = [dscr(f"kfl{gi}", [256, 2 * Lg], BF16) for gi, (Lg, _) in enumerate(cfg.groups)]

    ar = Arena(nc, 24576, 229344)
    ident = ar.alloc("ident", [128, 128], BF16)
    epsT = ar.alloc("eps", [128, 1], F32)
    gT = ar.alloc("gains", [128, 4, D], F32)
    gqk = ar.alloc("gqk", [128, 6, 64], F32)
    cvp = ar.alloc("cvp", [128, 6, 4], F32)
    scw = ar.alloc("scw", [128, 2, 3], F32)
    hbi = ar.alloc("hbi", [128, 2], F32)
    ndl = ar.alloc("ndl", [128, 2], F32)
    rnrm = [ar.alloc(f"rnrm{gi}", [128, 2], F32) for gi in range(len(cfg.groups))]
    Dm = ar.alloc("Dm", [128, 4, 128], F32)
    wqf = ar.alloc("wqf", [128, 2, 128], F32)
    wqb = ar.alloc("wqb", [128, 2, 128], F32)
    wkk = ar.alloc("wkk", [128, 4, 2], F32)
    gch = ar.alloc("gch", [128, 2, 2], F32)
    PERSIST = ar.mark()
    ar.base = PERSIST

    PF = [T(nc.alloc_psum_tensor(f"pf{i}", [128, 512], F32), f"pf{i}") for i in range(6)]
    PT = [T(nc.alloc_psum_tensor(f"pt{i}", [128, 1024], BF16), f"pt{i}") for i in range(2)]
    rot = {"pf": 0, "pt": 0}

    def next_pf(n=6, base=0):
        i = rot["pf"] % n + base
        rot["pf"] += 1
        return PF[i]

    def next_pt():
        i = rot["pt"] % 2
        rot["pt"] += 1
        return PT[i]

    def bl(x):
        return [t.b if isinstance(t, T) else t for t in x]

    def mm(out, lhsT, rhs, start, stop, R, W, skip=False):
        mk.op("pe", lambda e: e.matmul(out, lhsT=lhsT, rhs=rhs, start=start, stop=stop, skip_group_check=skip), bl(R), bl(W))

    def tr(out, in_, R, W):
        idn = ident[:]
        mk.op("pe", lambda e: e.transpose(out=out, in_=in_, identity=idn), bl(R) + [ident.b], bl(W))

    def act(out, in_, func, R, W, scale=1.0, bias=None, accum=None):
        def f(e):
            kw = {}
            if bias is not None:
                kw["bias"] = bias
            if accum is not None:
                kw["accum_out"] = accum
            return e.activation(out=out, in_=in_, func=func, scale=scale, **kw)
        mk.op("act", f, bl(R), bl(W))

    def tt(eng, out, in0, in1, op, R, W):
        mk.op(eng, lambda e: e.tensor_tensor(out=out, in0=in0, in1=in1, op=op), bl(R), bl(W))

    def ts(eng, out, in0, s1, s2, op0, op1, R, W):
        if s2 is None:
            mk.op(eng, lambda e: e.tensor_scalar(out=out, in0=in0, scalar1=s1, scalar2=None, op0=op0), bl(R), bl(W))
        else:
            mk.op(eng, lambda e: e.tensor_scalar(out=out, in0=in0, scalar1=s1, scalar2=s2, op0=op0, op1=op1), bl(R), bl(W))

    def stt(eng, out, in0, sc, in1, op0, op1, R, W):
        mk.op(eng, lambda e: e.scalar_tensor_tensor(out=out, in0=in0, scalar=sc, in1=in1, op0=op0, op1=op1), bl(R), bl(W))

    def cp(eng, out, in_, R, W):
        if eng == "act":
            act(out, in_, AF.Copy, R, W)
        else:
            mk.op(eng, lambda e: e.tensor_copy(out=out, in_=in_), bl(R), bl(W))

    def red(eng, out, in_, R, W):
        mk.op(eng, lambda e: e.tensor_reduce(out=out, in_=in_, axis=AX.X, op=ALU.add), bl(R), bl(W))

    def rcp(out, in_, R, W):
        mk.op("dve", lambda e: e.reciprocal(out=out, in_=in_), bl(R), bl(W))

    def mset(eng, out, val, W):
        mk.op(eng, lambda e: e.memset(out, val), [], bl(W))

    def dma(q, out, in_, R, W):
        mk.op(q, lambda e: e.dma_start(out=out, in_=in_), bl(R), bl(W), dma=True)

    def barrier():
        mk.op("pool", lambda e: e.memset(epsT[:, 0:1], EPS), [], [epsT.b], barrier=True)

    def rstd_from_ss(ss, n, R):
        act(ss, ss, AF.Sqrt, R, R, scale=1.0 / n, bias=epsT[:, 0:1])
        rcp(ss, ss, R, R)

    mset("pool", epsT[:], EPS, [epsT])
    dma("pool", ident[:], tin["ident"].ap(), [], [ident])
    dma("sp", ndl[:], tin["negdelta"].ap(), [], [ndl])

    def phase_A(l, src):
        ar.reset()
        xt = [ar.alloc("xt", [128, D], F32) for _ in range(3)]
        junk = ar.alloc("junk", [128, D], BF16)
        ssA = [ar.alloc("ss", [128, 1], F32) for _ in range(3)]
        xn = [ar.alloc("xn", [128, D], BF16) for _ in range(2)]
        xT = [ar.alloc("xT", [128, 8, 512], BF16) for _ in range(2)]
        for ti in range(NTILE):
            xTt = xT[ti % 2]
            for s in range(4):
                i = ti * 4 + s
                x_, ss_, xn_ = xt[i % 3], ssA[i % 3], xn[i % 2]
                dma("sp", x_[:], src.h.ap()[i * 128:(i + 1) * 128, :], [src], [x_])
                act(junk[:], x_[:], AF.Square, [x_], [junk, ss_], accum=ss_[:])
                rstd_from_ss(ss_[:], D, [ss_])
                stt("dve", xn_[:], x_[:], ss_[:, 0:1], gT[:, 0, :], ALU.mult, ALU.mult, [x_, ss_, gT], [xn_])
                pt = next_pt()
                for k in range(8):
                    tr(pt[:, k * 128:(k + 1) * 128], xn_[:, k * 128:(k + 1) * 128], [xn_], [pt])
                cp("act" if s % 2 else "dve", xTt[:, :, s * 128:(s + 1) * 128],
                   pt[:].rearrange("p (k t) -> p k t", t=128), [pt], [xTt])
            dma("pool", xnT_s.h.ap().rearrange("(k p) t -> p k t", p=128)[:, :, ti * 512:(ti + 1) * 512], xTt[:], [xTt], [xnT_s])

    def load_layer_consts(l):
        for j in range(4):
            dma("sp", gT[:, j, :], bass.AP(ngain, (l * 4 + j) * D, [[0, 128], [1, D]]), [], [gT])
        for h in range(6):
            off = l * 128 + (0 if h < 4 else 64)
            dma("sp", gqk[:, h, :], bass.AP(qkn, off, [[0, 128], [1, 64]]), [], [gqk])
        dma("sp", cvp[:], convp.ap()[l], [], [cvp])
        dma("sp", scw[:], scwp.ap()[l], [], [scw])
        dma("sp", hbi[:], hbiasp.ap()[l], [], [hbi])

    def phase_R(l):
        ar.reset()
        rd = ar.alloc("rd", [128, 8], F32)
        lg = ar.alloc("lg", [128, 8], F32)
        tmp = ar.alloc("tmpR", [128, 128], F32)
        dpos = ar.alloc("dpos", [128, 128], F32)
        dneg = ar.alloc("dneg", [128, 128], F32)
        iqf = ar.alloc("iqf", [128, 128], F32)
        iqb = ar.alloc("iqb", [128, 128], F32)
        jk = ar.alloc("jk", [128, 2], F32)
        lsel = ar.alloc("lsel", [128, 2, 2], F32)
        dma("sp", rd[:], bass.AP(rde, l * 8, [[0, 128], [1, 8]]), [], [rd])
        dma("sp", dpos[:], tin["dpos"].ap(), [], [dpos])
        dma("sp", dneg[:], tin["dneg"].ap(), [], [dneg])
        dma("sp", iqf[:], tin["iqf"].ap(), [], [iqf])
        dma("sp", iqb[:], tin["iqb"].ap(), [], [iqb])
        dma("sp", jk[:], tin["jk"].ap(), [], [jk])
        act(lg[:], rd[:], AF.Exp, [rd], [lg], scale=-math.log(2.0))
        act(lg[:], lg[:], AF.Ln, [lg], [lg], scale=-1.0, bias=1.0)
        for h in range(4):
            ts("dve", tmp[:], dpos[:], lg[:, h:h + 1], None, ALU.mult, None, [dpos, lg], [tmp])
            stt("dve", tmp[:], dneg[:], lg[:, 4 + h:5 + h], tmp[:], ALU.mult, ALU.add, [dneg, lg, tmp], [tmp])
            act(Dm[:, h, :], tmp[:], AF.Exp, [tmp], [Dm])
            act(wkk[:, h, 0:1], jk[:, 0:1], AF.Exp, [jk, lg], [wkk], scale=lg[:, h:h + 1])
            act(wkk[:, h, 1:2], jk[:, 1:2], AF.Exp, [jk, lg], [wkk], scale=lg[:, 4 + h:5 + h])
        for p in range(2):
            for half in range(2):
                h = 2 * p + half
                r0, r1 = half * 64, half * 64 + 64
                for d in range(2):
                    cp("dve", lsel[r0:r1, p, d:d + 1], lg[r0:r1, 4 * d + h:4 * d + h + 1], [lg], [lsel])
            act(wqf[:, p, :], iqf[:], AF.Exp, [iqf, lsel], [wqf], scale=lsel[:, p, 0:1])
            act(wqb[:, p, :], iqb[:], AF.Exp, [iqb, lsel], [wqb], scale=lsel[:, p, 1:2])
            act(gch[:, p, :], lsel[:, p, :], AF.Exp, [lsel], [gch], scale=128.0)

    def phase_F(l, gi):
        Lg, _ = cfg.groups[gi]
        ar.reset()
        NCH = (2 * Lg) // 512
        w1 = ar.alloc("w1", [33, 64], F32)
        w2 = ar.alloc("w2", [64, 64], F32)
        w3 = ar.alloc("w3", [64, 512], F32)
        hv = ar.alloc("hv", [64, 4], F32)
        ssq = ar.alloc("ssq", [128, 2, NCH], F32)
        ze = [ar.alloc("ze", [33, 512], F32) for _ in range(2)]
        trw = [ar.alloc("trw", [128, 512], F32) for _ in range(2)]
        a1 = [ar.alloc("a1", [64, 512], F32) for _ in range(2)]
        tw = [ar.alloc("tw", [64, 512], F32) for _ in range(2)]
        h1 = [ar.alloc("h1", [64, 512], F32) for _ in range(2)]
        dec = [ar.alloc("dec", [128, 512], F32) for _ in range(2)]
        kf = [ar.alloc("kf", [128, 512], F32) for _ in range(2)]
        kb = [ar.alloc("kb", [128, 512], BF16) for _ in range(4)]
        jk2 = ar.alloc("jk2", [128, 512], BF16)
        dma("sp", w1[:], hy_w1.ap()[l], [], [w1])
        dma("sp", w2[:], hy_w2.ap()[l], [], [w2])
        dma("sp", w3[:], hy_w3.ap()[l], [], [w3])
        dma("sp", hv[:], hyvec.ap()[l], [], [hv])
        mset("pool", ssq[:], 0.0, [ssq])
        PI, TWO_PI = math.pi, 2 * math.pi

        def sin_layer(ps, dst, bcol, fcol, a_, t_):
            ts("dve", a_[:], ps[0:64, :], hv[:, bcol:bcol + 1], hv[:, fcol:fcol + 1], ALU.add, ALU.mult, [hv, ps], [a_])
            for _ in range(1):
                ts("dve", t_[:], a_[:], PI, None, ALU.is_gt, None, [a_], [t_])
                stt("dve", a_[:], t_[:], -TWO_PI, a_[:], ALU.mult, ALU.add, [t_, a_], [a_])
                ts("dve", t_[:], a_[:], -PI, None, ALU.is_lt, None, [a_], [t_])
                stt("dve", a_[:], t_[:], TWO_PI, a_[:], ALU.mult, ALU.add, [t_, a_], [a_])
            act(dst[:], a_[:], AF.Sin, [a_], [dst])

        for ch in range(NCH):
            m0 = ch * 512
            ze_, tr_, a_, t_, h_ = ze[ch % 2], trw[ch % 2], a1[ch % 2], tw[ch % 2], h1[ch % 2]
            dma("sp", ze_[:], tin[f"zext{gi}"].ap()[:, m0:m0 + 512], [], [ze_])
            dma("sp", tr_[:], bass.AP(tin[f"trow{gi}"], m0, [[0, 128], [1, 512]]), [], [tr_])
            p1 = next_pf()
            mm(p1[0:64, :], w1[:], ze_[:], True, True, [w1, ze_], [p1])
            sin_layer(p1, h_, 0, 2, a_, t_)
            p2 = next_pf()
            mm(p2[0:64, :], w2[:], h_[:], True, True, [w2, h_], [p2])
            sin_layer(p2, h_, 1, 3, a_, t_)
            dirn = 1 if m0 < Lg else 0
            for cc in range(2):
                p3 = next_pf()
                c0 = dirn * 256 + cc * 128
                mm(p3[:], w3[:, c0:c0 + 128], h_[:], True, True, [w3, h_], [p3])
                d_, k_, kb_ = dec[cc], kf[cc], kb[(ch * 2 + cc) % 4]
                act(d_[:], tr_[:], AF.Exp, [tr_, ndl], [d_], scale=ndl[:, cc:cc + 1])
                tt("dve", k_[:], p3[:], d_[:], ALU.mult, [p3, d_], [k_])
                if ch == 0:
                    mset("pool", k_[:, 0:1], 0.0, [k_])
                act(jk2[:], k_[:], AF.Square, [k_], [jk2, ssq], accum=ssq[:, cc, ch:ch + 1])
                cp("pool", kb_[:], k_[:], [k_], [kb_])
                dma("pool", kfl[gi].h.ap()[cc * 128:(cc + 1) * 128, m0:m0 + 512], kb_[:], [kb_], [kfl[gi]])
        red("dve", rnrm[gi][:], ssq[:], [ssq], [rnrm[gi]])
        rstd_from_ss(rnrm[gi][:], 1.0, [rnrm[gi]])

    def phase_1b(l):
        ar.reset()
        W1 = ar.alloc("W1", [128, 8, 3072], BF16)
        for k in range(8):
            dma("pool", W1[:, k, :], w_in.ap()[l, k * 128:(k + 1) * 128, 0:3072], [], [W1])
        xe = [ar.alloc("xe", [128, 8, 514], BF16) for _ in range(2)]
        Ct = [ar.alloc("Ct", [128, 4, 8, 64], F32)] * 2
        St = [ar.alloc("St", [128, 4, 8, 64], F32)] * 2
        qk32 = [ar.alloc("qk32", [128, 384], F32) for _ in range(2)]
        sq = ar.alloc("sq", [128, 384], F32)
        ss6 = [ar.alloc("ss6", [128, 6], F32) for _ in range(2)]
        t1 = [ar.alloc("t1", [128, 512], F32) for _ in range(2)]
        t2 = [ar.alloc("t2", [128, 512], F32) for _ in range(2)]
        qkbf = [ar.alloc("qkbf", [128, 384], BF16) for _ in range(2)]
        qkT = [ar.alloc("qkT", [128, 3, 512], BF16) for _ in range(2)]
        vt = [ar.alloc("vt", [128, 4, 2, 65], BF16) for _ in range(2)]
        r32 = [ar.alloc("r32", [128, 512], F32) for _ in range(2)]
        rtok = [ar.alloc("rtok", [128, 4, 1024], BF16) for _ in range(2)]
        pext = [ar.alloc("pext", [128, 514], F32) for _ in range(6)]
        uu = [ar.alloc("uu", [128, 512], F32) for _ in range(6)]
        z32 = [ar.alloc("z32", [128, 512], F32) for _ in range(2)]
        obf = [ar.alloc("obf", [128, 512], BF16) for _ in range(4)]
        zrv = [ar.alloc("zrv", [128, 512], BF16) for _ in range(2)]
        zst = [ar.alloc("zst", [128, 4, 256], BF16) for _ in range(2)]
        gext = [ar.alloc("gext", [128, 514], F32) for _ in range(2)]
        for v_ in vt:
            mset("pool", v_[:], 1.0, [v_])
        PH = PF[5]
        it = [0]

        for ti in range(NTILE):
            t0 = ti * 512
            sidx = [i for i, (s0, L) in enumerate(cfg.seqs) if s0 <= t0 < s0 + L][0]
            s0, L = cfg.seqs[sidx]
            pos0 = t0 - s0
            xe_, Ct_, St_ = xe[ti % 2], Ct[ti % 2], St[ti % 2]
            lo = max(t0 - 1, s0)
            hi = min(t0 + 513, s0 + L)
            c_lo = lo - (t0 - 1)
            dma("sp", xe_[:, :, c_lo:c_lo + (hi - lo)], xnT_s.h.ap().rearrange("(k p) t -> p k t", p=128)[:, :, lo:hi], [xnT_s], [xe_])
            if c_lo > 0:
                mset("pool", xe_[:, :, 0:1], 0.0, [xe_])
            if hi < t0 + 513:
                mset("pool", xe_[:, :, 513:514], 0.0, [xe_])
            for s in range(4):
                dma("sp", Ct_[:, s, :, :], bass.AP(tin["ropeC"], (pos0 + s * 128) * 64, [[64, 128], [0, 8], [1, 64]]), [], [Ct_])
                dma("sp", St_[:, s, :, :], bass.AP(tin["ropeS"], (pos0 + s * 128) * 64, [[64, 128], [0, 8], [1, 64]]), [], [St_])
            qkT_, vt_, rtok_ = qkT[ti % 2], vt[ti % 2], rtok[ti % 2]

            def rope(src, nh, Cs, Ss, t1_, t2_):
                w = nh * 64
                tt("dve", t1_[:, 0:w], src[:, 0:w], Cs, ALU.mult, [src, Ct_], [t1_])
                sw = src.ap(16, [[32, nh * 2], [-16, 2], [1, 16]])
                tt("dve", t2_[:, 0:w].rearrange("p (a h f) -> p a h f", h=2, f=16), sw, Ss, ALU.mult, [src, St_], [t2_])

            deferred = []

            def flush():
                while deferred:
                    deferred.pop(0)()

            for s in range(4):
                j = it[0]
                it[0] += 1
                q32, ss_, t1_, t2_, qb_ = qk32[j % 2], ss6[j % 2], t1[j % 2], t2[j % 2], qkbf[j % 2]
                xs = slice(1 + s * 128, 1 + s * 128 + 128)
                pa = next_pf(5)
                for k in range(8):
                    mm(pa[:], xe_[:, k, xs], W1[:, k, 0:512], k == 0, k == 7, [xe_, W1], [pa])
                pr = next_pf(5)
                for k in range(8):
                    mm(pr[:], xe_[:, k, xs], W1[:, k, 1280:1792], k == 0, k == 7, [xe_, W1], [pr])
                pv = next_pf(5)
                for k in range(8):
                    mm(pv[:], xe_[:, k, xs], W1[:, k, 1792:2304], k == 0, k == 7, [xe_, W1], [pv])
                flush()
                act(q32[:], pa[:, 0:384], AF.Copy, [pa], [q32])
                act(vt_[:, s, :, 0:64], pa[:, 384:512].rearrange("p (g d) -> p g d", d=64), AF.Copy, [pa], [vt_])
                tt("dve", sq[:], q32[:], q32[:], ALU.mult, [q32], [sq])
                red("dve", ss_[:], sq[:].rearrange("p (h d) -> p h d", d=64), [sq], [ss_])
                rstd_from_ss(ss_[:], 64.0, [ss_])
                tt("dve", q32[:].rearrange("p (h d) -> p h d", d=64), q32[:].rearrange("p (h d) -> p h d", d=64),
                   ss_.ap(0, [[1, 6], [0, 64]]), ALU.mult, [q32, ss_], [q32])
                tt("dve", q32[:], q32[:], gqk[:].rearrange("p h d -> p (h d)"), ALU.mult, [q32, gqk], [q32])
                rope(q32, 6, Ct_[:, s, 0:6, :].rearrange("p h d -> p (h d)"),
                     St_[:, s, 0:6, :].rearrange("p h (a b f) -> p (h a) b f", a=2, b=2, f=16), t1_, t2_)
                tt("dve", qb_.ap(0, [[64, 2], [128, 2], [1, 64]]), t1_.ap(0, [[128, 2], [64, 2], [1, 64]]),
                   t2_.ap(0, [[128, 2], [64, 2], [1, 64]]), ALU.add, [t1_, t2_], [qb_])
                tt("dve", qb_[:, 256:384], t1_[:, 256:384], t2_[:, 256:384], ALU.add, [t1_, t2_], [qb_])

                def do_tr(qb_=qb_, s=s):
                    pt = next_pt()
                    for c3 in range(3):
                        tr(pt[:, c3 * 128:(c3 + 1) * 128], qb_[:, c3 * 128:(c3 + 1) * 128], [qb_], [pt])
                    cp("act", qkT_[:, :, s * 128:(s + 1) * 128], pt[:, 0:384].rearrange("p (c t) -> p c t", t=128), [pt], [qkT_])
                deferred.append(do_tr)
                r_, rt1, rt2 = r32[j % 2], t1[(j + 1) % 2], t2[(j + 1) % 2]
                act(r_[:, 0:256], pr[:, 0:256], AF.Copy, [pr], [r_])
                act(r_[:, 256:512], pr[:, 256:512], AF.Copy, [pr], [r_], scale=0.125)
                rope(r_, 8, Ct_[:, s, :, :].rearrange("p h d -> p (h d)"),
                     St_[:, s, :, :].rearrange("p h (a b f) -> p (h a) b f", a=2, b=2, f=16), rt1, rt2)
                tt("dve", rtok_[:, s, 0:512], rt1[:], rt2[:], ALU.add, [rt1, rt2], [rtok_])
                act(rtok_[:, s, 512:768], pv[:, 0:256], AF.Copy, [pv], [rtok_])
                act(rtok_[:, s, 768:1024], pv[:, 256:512], AF.Silu, [pv], [rtok_])

            def tail_dmas():
                dma("pool", qT_s.h.ap().rearrange("(c p) t -> p c t", p=128)[:, :, t0:t0 + 512], qkT_[:, 0:2, :], [qkT_], [qT_s])
                dma("pool", kT_s.h.ap()[:, t0:t0 + 512], qkT_[:, 2, :], [qkT_], [kT_s])
            deferred.append(tail_dmas)
            dma("pool", v_s.h.ap()[t0:t0 + 512, :].rearrange("(s p) c -> p s c", p=128), vt_[:].rearrange("p s g d -> p s (g d)"), [vt_], [v_s])
            dma("pool", ret_s.h.ap()[t0:t0 + 512, :].rearrange("(s p) c -> p s c", p=128), rtok_[:], [rtok_], [ret_s])

            def fm_chunk(col0, hslot, dst, halo):
                pm = next_pf(5)
                for k in range(8):
                    mm(pm[:], W1[:, k, col0:col0 + 128], xe_[:, k, 1:513], k == 0, k == 7, [xe_, W1], [pm])
                if halo:
                    for k in range(8):
                        mm(PH[:, hslot * 2:hslot * 2 + 2], W1[:, k, col0:col0 + 128], xe_.ap(k * 514, [[513, 2]]),
                           k == 0, k == 7, [xe_, W1], [PH])
                    cp("dve", dst.ap(0, [[513, 2]]), PH[:, hslot * 2:hslot * 2 + 2], [PH], [dst])
                act(dst[:, 1:513], pm[:], AF.Copy, [pm], [dst])

            def conv3(eng, dst, src, wcol, wt, bias):
                if bias is not None:
                    ts(eng, dst[:], src[:, 1:513], wt[:, wcol, 1:2], bias, ALU.mult, ALU.add, [src, wt], [dst])
                else:
                    ts(eng, dst[:], src[:, 1:513], wt[:, wcol, 1:2], None, ALU.mult, None, [src, wt], [dst])
                stt(eng, dst[:], src[:, 0:512], wt[:, wcol, 0:1], dst[:], ALU.mult, ALU.add, [src, wt, dst], [dst])
                stt(eng, dst[:], src[:, 2:514], wt[:, wcol, 2:3], dst[:], ALU.mult, ALU.add, [src, wt, dst], [dst])

            for jc in range(6):
                fm_chunk(512 + jc * 128, jc, pext[jc], True)
                if jc == 0:
                    flush()
                conv3("dve", uu[jc], pext[jc], jc, cvp, cvp[:, jc, 3:4])
            for jc in range(6):
                fm_chunk(2304 + jc * 128, 6 + jc, pext[jc], jc >= 2)
            for cc in range(2):
                ob = obf[cc]
                cp("pool", ob[:], uu[cc][:], [uu[cc]], [ob])
                dma("pool", x0T_s.h.ap()[cc * 128:(cc + 1) * 128, t0:t0 + 512], ob[:], [ob], [x0T_s])
                z_ = z32[cc]
                tt("pool", z_[:], uu[4 + cc][:], uu[2 + cc][:], ALU.mult, [uu[4 + cc], uu[2 + cc]], [z_])
                ob2 = obf[2 + cc]
                cp("pool", ob2[:], z_[:], [z_], [ob2])
                dma("pool", zT_s.h.ap()[cc * 128:(cc + 1) * 128, t0:t0 + 512], ob2[:], [ob2], [zT_s])
                zr_ = zrv[cc]
                cp("pool", zr_[:].rearrange("p (a b) -> p a b", b=128), z_.ap(127, [[128, 4], [-1, 128]]), [z_], [zr_])
                pt = next_pt()
                for b4 in range(4):
                    tr(pt[:, b4 * 128:(b4 + 1) * 128], zr_[:, b4 * 128:(b4 + 1) * 128], [zr_], [pt])
                zs_ = zst[ti % 2]
                cp("act", zs_[:, :, cc * 128:(cc + 1) * 128], pt[:, 0:512].rearrange("p (a c) -> p a c", c=128), [pt], [zs_])
            dma("pool", zr_s.h.ap()[:, t0 // 128:t0 // 128 + 4, :], zst[ti % 2][:], [zst[ti % 2]], [zr_s])
            for cc in range(2):
                g_ = gext[cc]
                tt("pool", g_[:], pext[2 + cc][:], pext[4 + cc][:], ALU.mult, [pext[2 + cc], pext[4 + cc]], [g_])
                cv = uu[cc]
                conv3("dve", cv, g_, cc, scw, None)
                ob = obf[cc]
                tt("dve", ob[:], cv[:], pext[cc][:, 1:513], ALU.mult, [cv, pext[cc]], [ob])
                dma("pool", brT_s[3].h.ap()[cc * 128:(cc + 1) * 128, t0:t0 + 512], ob[:], [ob], [brT_s[3]])

    def phase_2A(sidx):
        s0, L = cfg.seqs[sidx]
        ar.reset()
        NKC = L // 128
        qT = ar.alloc("qT", [128, 2, L], BF16)
        kT = ar.alloc("kT", [128, L], BF16)
        V = ar.alloc("V", [128, NKC, 130], BF16)
        dma("sp", qT[:], qT_s.h.ap().rearrange("(c p) t -> p c t", p=128)[:, :, s0:s0 + L], [qT_s], [qT])
        dma("sp", kT[:], kT_s.h.ap()[:, s0:s0 + L], [kT_s], [kT])
        for v0 in range(0, NKC, 8):
            v1 = min(NKC, v0 + 8)
            dma("sp", V[:, v0:v1, :], v_s.h.ap()[s0 + v0 * 128:s0 + v1 * 128, :].rearrange("(n p) c -> p n c", p=128), [v_s], [V])
        Pt = [ar.alloc("Pt", [128, 512], BF16) for _ in range(3)]
        oa = [ar.alloc("oa", [128, 4, 256], BF16) for _ in range(2)]
        rden = [ar.alloc("rden", [128, 4], F32) for _ in range(2)]
        oaT = [ar.alloc("oaT", [128, 2, 512], BF16) for _ in range(2)]
        steps = [(qb, 2 * cqp + g, kc) for qb in range(L // 512) for cqp in range(2) for kc in range(NKC) for g in range(2)]
        nst = len(steps)
        LOOK = 2

        def emit_ST(i):
            qb, hh, kc = steps[i]
            cq, g = hh // 2, hh % 2
            rows = slice(64 * g, 64 * g + 64)
            ps = PF[i % 4]
            mm(ps[:], kT[rows, kc * 128:(kc + 1) * 128], qT[rows, cq, qb * 512:(qb + 1) * 512], True, True, [kT, qT], [ps])

        for i in range(min(LOOK, nst)):
            emit_ST(i)
        for i in range(nst):
            if i % 2 == 0:
                for ii in (i + LOOK, i + LOOK + 1):
                    if ii < nst:
                        emit_ST(ii)
            qb, hh, kc = steps[i]
            cq, g = hh // 2, hh % 2
            head = 2 * g + cq
            po = PF[4 + (hh % 2)]
            ps = PF[i % 4]
            P_ = Pt[i % 3]
            oa_ = oa[qb % 2]
            act(P_[:], ps[:], AF.Exp, [ps], [P_], scale=0.125)
            for qs in range(4):
                mm(po[:, qs * 65:qs * 65 + 65], P_[:, qs * 128:(qs + 1) * 128], V[:, kc, g * 65:g * 65 + 65],
                   kc == 0 and qs == 0, kc == NKC - 1, [P_, V], [po], skip=True)
            if kc < NKC - 1:
                continue
            rd_ = rden[hh % 2]
            rcp(rd_[:], po.ap(64, [[65, 4]]), [po], [rd_])
            tt("dve", oa_[:, :, head * 64:(head + 1) * 64], po.ap(0, [[65, 4], [1, 64]]), rd_.ap(0, [[1, 4], [0, 64]]),
               ALU.mult, [po, rd_], [oa_])
            if hh < 3:
                continue
            oT = oaT[qb % 2]
            for qs in range(4):
                pt = next_pt()
                for c2 in range(2):
                    tr(pt[:, c2 * 128:(c2 + 1) * 128], oa_[:, qs, c2 * 128:(c2 + 1) * 128], [oa_], [pt])
                cp("dve", oT[:, :, qs * 128:(qs + 1) * 128], pt[:, 0:256].rearrange("p (c t) -> p c t", t=128), [pt], [oT])
            t0 = s0 + qb * 512
            dma("pool", brT_s[0].h.ap().rearrange("(c p) t -> p c t", p=128)[:, :, t0:t0 + 512], oT[:], [oT], [brT_s[0]])

    def phase_2B(gi):
        Lg, sl = cfg.groups[gi]
        ns = len(sl)
        A = Lg // 128
        NB = ns * A
        ar.reset()
        WW = 2 * Lg - 128
        Zr = ar.alloc("Zr", [128, 256, NB], BF16)
        WWA = max(WW, NB * 256)
        Wc = [ar.alloc("Wc", [128, WWA], BF16) for _ in range(3)]
        ar.n += 1
        Zl = T(nc.alloc_sbuf_tensor_at(f"Zl_{ar.n}", [128, NB, 256], BF16, offset=ar.off - WWA * 2), "Zl")
        Zl.b = Wc[2].b
        Ysb = ar.alloc("Ysb", [128, 128, NB], BF16)
        x0t = [ar.alloc("x0t", [128, 512], BF16) for _ in range(2)]
        zt = [ar.alloc("zt", [128, 512], BF16) for _ in range(2)]
        tmpc = [ar.alloc("tmpc", [128, 512], F32) for _ in range(2)]
        obb = [ar.alloc("obb", [128, 512], BF16) for _ in range(2)]
        gb0 = cfg.seqs[sl[0]][0] // 128
        dma("sp", Zl[:], zr_s.h.ap()[:, gb0:gb0 + NB, :], [zr_s], [Zl])
        for q4 in range(4):
            cp("dve" if q4 % 2 else "pool", Zr[:, q4 * 64:(q4 + 1) * 64, :],
               Zl[:, :, q4 * 64:(q4 + 1) * 64].rearrange("p n c -> p c n"), [Zl], [Zr])
        deltas = [0] + [d for d in range(-(A - 1), A) if d != 0]
        blk = 0
        for cc in range(2):
            for cl in range(128):
                c = cc * 128 + cl
                W_ = Wc[c % 3]
                dma("sp", W_[:, 0:WW], bass.AP(kfl[gi].h, c * 2 * Lg + 1, [[1, 128], [1, WW]]), [kfl[gi]], [W_])
                slot = cl % 8
                if slot == 0:
                    py = PF[(c // 8) % 4]
                for di, d in enumerate(deltas):
                    off = 128 * (d + A - 1)
                    a0, a1 = max(0, -d), min(A, A - d)
                    n = a1 - a0
                    rhs = Zr.ap(c * NB + a0, [[A, ns], [1, n]])
                    out = py.ap(slot * NB + a0 + d, [[A, ns], [1, n]])
                    mm(out, W_[:, off:off + 128], rhs, di == 0, di == len(deltas) - 1, [W_, Zr], [py], skip=True)
                if slot == 7:
                    act(Ysb[:, cl - 7:cl + 1, :], py[:, 0:8 * NB].rearrange("p (c n) -> p c n", n=NB), AF.Copy, [py], [Ysb])
            for b4 in range(NB // 4):
                pt = next_pt()
                for bb in range(4):
                    b = b4 * 4 + bb
                    tr(pt[:, bb * 128:(bb + 1) * 128], Ysb.ap(b, [[NB, 128]]), [Ysb], [pt])
                t0 = (gb0 + b4 * 4) * 128
                x0_, z_, tm, ob = x0t[blk % 2], zt[blk % 2], tmpc[blk % 2], obb[blk % 2]
                blk += 1
                dma("sp", x0_[:], x0T_s.h.ap()[cc * 128:(cc + 1) * 128, t0:t0 + 512], [x0T_s], [x0_])
                dma("sp", z_[:], zT_s.h.ap()[cc * 128:(cc + 1) * 128, t0:t0 + 512], [zT_s], [z_])
                ts("dve", tm[:], pt[:, 0:512], rnrm[gi][:, cc:cc + 1], None, ALU.mult, None, [pt, rnrm[gi]], [tm])
                stt("dve", tm[:], z_[:], hbi[:, cc:cc + 1], tm[:], ALU.mult, ALU.add, [z_, hbi, tm], [tm])
                tt("dve", ob[:], tm[:], x0_[:], ALU.mult, [tm, x0_], [ob])
                dma("pool", brT_s[1].h.ap()[cc * 128:(cc + 1) * 128, t0:t0 + 512], ob[:], [ob], [brT_s[1]])

    def phase_2C(sidx):
        s0, L = cfg.seqs[sidx]
        NCK = L // 128
        ar.reset()
        Sb_all = ar.alloc("Sb_all", [128, 2, NCK, 64], BF16)
        Sst = ar.alloc("Sst", [128, 2, 2, 64], F32)
        Sfb = [ar.alloc("Sfb", [128, 2, 64], BF16) for _ in range(2)]
        rt = [ar.alloc("rt", [128, 4, 1024], BF16) for _ in range(2)]
        vk = [ar.alloc("vk", [128, 4, 64], BF16) for _ in range(2)]
        qkTs = [ar.alloc("qkTs", [128, 4, 128], BF16) for _ in range(2)]
        qsf = [ar.alloc("qsf", [128, 2, 128], BF16) for _ in range(2)]
        qsb = [ar.alloc("qsb", [128, 2, 128], BF16) for _ in range(2)]
        Pm = [ar.alloc("Pm", [128, 4, 128], BF16) for _ in range(2)]
        sqo = ar.alloc("sqo", [128, 256], F32)
        sso = [ar.alloc("sso", [128, 4], F32) for _ in range(2)]
        oc32 = [ar.alloc("oc32", [128, 256], F32) for _ in range(2)]
        ocb = [ar.alloc("ocb", [128, 256], BF16) for _ in range(2)]
        ocT = [ar.alloc("ocT", [128, 2, 512], BF16) for _ in range(2)]
        NG = NCK // 4

        def load_group(g, slot):
            t0 = s0 + g * 512
            dma("sp", rt[slot][:], ret_s.h.ap()[t0:t0 + 512, :].rearrange("(s p) c -> p s c", p=128), [ret_s], [rt[slot]])

        def kv_update(rt_, s, d, j):
            vk_ = vk[j % 2]
            tt("dve", vk_[:], rt_[:, s, 512:768].rearrange("p (h e) -> p h e", e=64), wkk.ap(d, [[2, 4], [0, 64]]),
               ALU.mult, [rt_, wkk], [vk_])
            for p in range(2):
                pk = next_pf(4)
                mm(pk[:, 0:128], rt_[:, s, 256 + p * 128:256 + (p + 1) * 128], vk_[:, 2 * p:2 * p + 2, :].rearrange("p h e -> p (h e)"),
                   True, True, [rt_, vk_], [pk])
                for half in range(2):
                    r = slice(half * 64, half * 64 + 64)
                    stt("dve", Sst[r, p, d, :], Sst[r, p, d, :], gch[r, p, d:d + 1], pk[r, half * 64:half * 64 + 64],
                        ALU.mult, ALU.add, [Sst, gch, pk], [Sst])

        mset("pool", Sst[:], 0.0, [Sst])
        j = 0
        for g in reversed(range(NG)):
            slot = g % 2
            load_group(g, slot)
            for s in reversed(range(4)):
                n = g * 4 + s
                cp("act", Sb_all[:, :, n, :], Sst[:, :, 1, :], [Sst], [Sb_all])
                if n > 0:
                    kv_update(rt[slot], s, 1, j)
                    j += 1
        import os
        DBG = int(os.environ.get("DBG2C", "9"))
        for g in range(NG if DBG > 0 else 0):
            slot = g % 2
            load_group(g, slot)
            rt_ = rt[slot]
            ocT_ = ocT[g % 2]
            for s in range(4):
                n = g * 4 + s
                qk_, qf_, qb_, Pm_, Sf_ = qkTs[n % 2], qsf[n % 2], qsb[n % 2], Pm[n % 2], Sfb[n % 2]
                SK = os.environ.get("SKIP", "").split(",")
                pt = next_pt()
                if "tr" not in SK:
                    for c4 in range(4):
                        tr(pt[:, c4 * 128:(c4 + 1) * 128], rt_[:, s, c4 * 128:(c4 + 1) * 128], [rt_], [pt])
                if "cpq" not in SK:
                    cp("act", qk_[:], pt[:, 0:512].rearrange("p (c t) -> p c t", t=128), [pt], [qk_])
                if "qf" not in SK:
                    tt("dve", qf_[:], qk_[:, 0:2, :], wqf[:], ALU.mult, [qk_, wqf], [qf_])
                    tt("dve", qb_[:], qk_[:, 0:2, :], wqb[:], ALU.mult, [qk_, wqb], [qb_])
                if "sf" not in SK:
                    cp("act", Sf_[:], Sst[:, :, 0, :], [Sst], [Sf_])
                psa = PF[(2 * n) % 4]
                psb = PF[(2 * n + 1) % 4]
                for h in range(4):
                    p, half = h // 2, h % 2
                    r = slice(half * 64, half * 64 + 64)
                    pdst = psb if half else psa
                    mm(pdst[:, p * 128:(p + 1) * 128], qk_[r, 2 + p, :], qk_[r, p, :], True, True, [qk_], [pdst])
                for half, pdst in ((0, psa), (1, psb)):
                    tt("dve", Pm_.ap(half * 128, [[256, 2], [1, 128]]), pdst[:, 0:256].rearrange("p (a i) -> p a i", i=128),
                       Dm.ap(half * 128, [[256, 2], [1, 128]]), ALU.mult, [pdst, Dm], [Pm_])
                if DBG < 2:
                    continue
                po = PF[4 + n % 2]
                for h in range(4):
                    p, half = h // 2, h % 2
                    r = slice(half * 64, half * 64 + 64)
                    oh = po[:, h * 64:(h + 1) * 64]
                    mm(oh, Pm_[:, h, :], rt_[:, s, 512 + h * 64:512 + (h + 1) * 64], True, False, [Pm_, rt_], [po], skip=True)
                    mm(oh, qf_[r, p, :], Sf_[r, p, :], False, False, [qf_, Sf_], [po], skip=True)
                    mm(oh, qb_[r, p, :], Sb_all[r, p, n, :], False, True, [qb_, Sb_all], [po], skip=True)
                if DBG < 3:
                    continue
                ss_, o32, ob_ = sso[n % 2], oc32[n % 2], ocb[n % 2]
                act(o32[:], po[:, 0:256], AF.Copy, [po], [o32])
                tt("dve", sqo[:], o32[:], o32[:], ALU.mult, [o32], [sqo])
                red("dve", ss_[:], sqo[:].rearrange("p (h e) -> p h e", e=64), [sqo], [ss_])
                rstd_from_ss(ss_[:], 64.0, [ss_])
                tt("dve", o32[:].rearrange("p (h e) -> p h e", e=64), o32[:].rearrange("p (h e) -> p h e", e=64),
                   ss_.ap(0, [[1, 4], [0, 64]]), ALU.mult, [o32, ss_], [o32])
                tt("dve", ob_[:], o32[:], rt_[:, s, 768:1024], ALU.mult, [o32, rt_], [ob_])
                pt2 = next_pt()
                for c2 in range(2):
                    tr(pt2[:, c2 * 128:(c2 + 1) * 128], ob_[:, c2 * 128:(c2 + 1) * 128], [ob_], [pt2])
                cp("act", ocT_[:, :, s * 128:(s + 1) * 128], pt2[:, 0:256].rearrange("p (c t) -> p c t", t=128), [pt2], [ocT_])
                if n < NCK - 1:
                    kv_update(rt_, s, 0, n)
            t0 = s0 + g * 512
            if DBG < 3:
                continue
            dma("pool", brT_s[2].h.ap().rearrange("(c p) t -> p c t", p=128)[:, :, t0:t0 + 512], ocT_[:], [ocT_], [brT_s[2]])

    def phase_3a(l, xsrc):
        ar.reset()
        Wg = ar.alloc("Wg", [128, 8, 4096], BF16)
        Wb = ar.alloc("Wb", [128, 4, 2, D], BF16)
        Wo = ar.alloc("Wo", [128, 8, D], BF16)
        for k in range(8):
            dma("pool", Wg[:, k, :], w_in.ap()[l, k * 128:(k + 1) * 128, 3072:7168], [], [Wg])
            dma("pool", Wo[:, k, :], w_o.ap()[l, k * 128:(k + 1) * 128, :], [], [Wo])
        for n in range(4):
            dma("pool", Wb[:, n, :, :], w_br.ap()[l, n].rearrange("(k p) c -> p k c", p=128), [], [Wb])
        xT = [ar.alloc("xT3", [128, 8, 512], BF16) for _ in range(2)]
        br = [ar.alloc("br3", [128, 4, 2, 512], BF16) for _ in range(2)]
        sg = [ar.alloc("sg", [128, 512], F32) for _ in range(2)]
        mg = [ar.alloc("mg", [128, 512], F32) for _ in range(2)]
        tm = [ar.alloc("tm3", [128, 512], F32) for _ in range(2)]
        mT = ar.alloc("mT", [128, 8, 512], BF16)
        xt = [ar.alloc("xt3", [128, D], F32)] * 2
        y32 = [ar.alloc("y32", [128, D], F32) for _ in range(2)]
        junk = ar.alloc("junk3", [128, D], BF16)
        ss2 = [ar.alloc("ss2", [128, 2], F32) for _ in range(2)]
        ss1 = [ar.alloc("ss1", [128, 1], F32) for _ in range(2)]
        hn = [ar.alloc("hn", [128, D], BF16) for _ in range(2)]
        hT = [ar.alloc("hT", [128, 8, 512], BF16)] * 2
        jj = 0
        deferred3 = []

        def flush3():
            while deferred3:
                deferred3.pop(0)()

        for ti in range(NTILE):
            t0 = ti * 512
            xT_, br_ = xT[ti % 2], br[ti % 2]
            dma("sp", xT_[:], xnT_s.h.ap().rearrange("(k p) t -> p k t", p=128)[:, :, t0:t0 + 512], [xnT_s], [xT_])
            for n in range(4):
                dma("sp", br_[:, n, :, :], brT_s[n].h.ap().rearrange("(c p) t -> p c t", p=128)[:, :, t0:t0 + 512], [brT_s[n]], [br_])
            for j in range(8):
                mg_ = mg[j % 2]
                for n in range(4):
                    pg = next_pf()
                    for k in range(8):
                        mm(pg[:], Wg[:, k, n * D + j * 128:n * D + (j + 1) * 128], xT_[:, k, :], k == 0, k == 7, [Wg, xT_], [pg])
                    pp = next_pf()
                    for kk in range(2):
                        mm(pp[:], Wb[:, n, kk, j * 128:(j + 1) * 128], br_[:, n, kk, :], kk == 0, kk == 1, [Wb, br_], [pp])
                    if j == 0 and n == 0:
                        flush3()
                    sg_ = sg[jj % 2]
                    jj += 1
                    act(sg_[:], pg[:], AF.Sigmoid, [pg], [sg_])
                    if n == 0:
                        tt("dve", mg_[:], sg_[:], pp[:], ALU.mult, [sg_, pp], [mg_])
                    else:
                        tm_ = tm[jj % 2]
                        tt("dve", tm_[:], sg_[:], pp[:], ALU.mult, [sg_, pp], [tm_])
                        if n < 3:
                            tt("pool", mg_[:], mg_[:], tm_[:], ALU.add, [mg_, tm_], [mg_])
                        else:
                            tt("pool", mT[:, j, :], mg_[:], tm_[:], ALU.add, [mg_, tm_], [mT])
            hT_ = hT[ti % 2]
            for s in range(4):
                i = ti * 4 + s
                x_, y_, s2, s1, hn_ = xt[i % 2], y32[i % 2], ss2[i % 2], ss1[i % 2], hn[i % 2]
                dma("sp", x_[:], xsrc.h.ap()[i * 128:(i + 1) * 128, :], [xsrc], [x_])
                pos = []
                for nn in range(2):
                    po = next_pf()
                    pos.append(po)
                    for k in range(8):
                        mm(po[:], mT[:, k, s * 128:(s + 1) * 128], Wo[:, k, nn * 512:(nn + 1) * 512], k == 0, k == 7, [mT, Wo], [po])
                    act(junk[:, nn * 512:(nn + 1) * 512], po[:], AF.Square, [po], [junk, s2], accum=s2[:, nn:nn + 1])
                flush3()
                tt("dve", s1[:], s2[:, 0:1], s2[:, 1:2], ALU.add, [s2], [s1])
                rstd_from_ss(s1[:], D, [s1])
                for nn in range(2):
                    stt("dve", y_[:, nn * 512:(nn + 1) * 512], pos[nn][:], s1[:, 0:1], gT[:, 1, nn * 512:(nn + 1) * 512],
                        ALU.mult, ALU.mult, [pos[nn], s1, gT], [y_])
                tt("pool", y_[:], y_[:], x_[:], ALU.add, [y_, x_], [y_])
                dma("pool", hbuf.h.ap()[i * 128:(i + 1) * 128, :], y_[:], [y_], [hbuf])
                act(junk[:], y_[:], AF.Square, [y_], [junk, s1], accum=s1[:])
                rstd_from_ss(s1[:], D, [s1])
                stt("dve", hn_[:], y_[:], s1[:, 0:1], gT[:, 2, :], ALU.mult, ALU.mult, [y_, s1, gT], [hn_])
                def do_tr(hn_=hn_, hT_=hT_, s=s):
                    pt = next_pt()
                    for k in range(8):
                        tr(pt[:, k * 128:(k + 1) * 128], hn_[:, k * 128:(k + 1) * 128], [hn_], [pt])
                    cp("act", hT_[:, :, s * 128:(s + 1) * 128], pt[:].rearrange("p (k t) -> p k t", t=128), [pt], [hT_])
                deferred3.append(do_tr)

            def do_store(hT_=hT_, t0=t0):
                dma("pool", hnT_s.h.ap().rearrange("(k p) t -> p k t", p=128)[:, :, t0:t0 + 512], hT_[:], [hT_], [hnT_s])
            deferred3.append(do_store)
        flush3()

    def phase_3b(l, dst):
        ar.reset()
        Wi = ar.alloc("Wi", [128, 8, 2 * DFF], BF16)
        Wf = ar.alloc("Wf", [128, 22, D], BF16)
        for k in range(8):
            dma("pool", Wi[:, k, :], w_fi.ap()[l, k * 128:(k + 1) * 128, :], [], [Wi])
        for k in range(22):
            dma("pool", Wf[:, k, :], w_fo.ap()[l, k * 128:(k + 1) * 128, :], [], [Wf])
        hT = [ar.alloc("hT3", [128, 8, 512], BF16)] * 2
        sg = [ar.alloc("sgb", [128, 512], BF16) for _ in range(2)]
        fT = ar.alloc("fT", [128, 22, 512], BF16)
        ht = [ar.alloc("ht", [128, D], F32)] * 2
        y32 = [ar.alloc("y32b", [128, D], F32)] * 2
        junk = ar.alloc("junkb", [128, D], BF16)
        ss2 = [ar.alloc("ss2b", [128, 2], F32) for _ in range(2)]
        ss1 = [ar.alloc("ss1b", [128, 1], F32) for _ in range(2)]
        jj = 0
        for ti in range(NTILE):
            t0 = ti * 512
            hT_ = hT[ti % 2]
            dma("sp", hT_[:], hnT_s.h.ap().rearrange("(k p) t -> p k t", p=128)[:, :, t0:t0 + 512], [hnT_s], [hT_])
            for j in range(22):
                pg = next_pf()
                for k in range(8):
                    mm(pg[:], Wi[:, k, j * 128:(j + 1) * 128], hT_[:, k, :], k == 0, k == 7, [Wi, hT_], [pg])
                pu = next_pf()
                for k in range(8):
                    mm(pu[:], Wi[:, k, DFF + j * 128:DFF + (j + 1) * 128], hT_[:, k, :], k == 0, k == 7, [Wi, hT_], [pu])
                sg_ = sg[jj % 2]
                jj += 1
                act(sg_[:], pg[:], AF.Silu, [pg], [sg_])
                tt("dve", fT[:, j, :], sg_[:], pu[:], ALU.mult, [sg_, pu], [fT])
            for s in range(4):
                i = ti * 4 + s
                h_, y_, s2, s1 = ht[i % 2], y32[i % 2], ss2[i % 2], ss1[i % 2]
                dma("sp", h_[:], hbuf.h.ap()[i * 128:(i + 1) * 128, :], [hbuf], [h_])
                pos = []
                for nn in range(2):
                    po = next_pf()
                    pos.append(po)
                    for k in range(22):
                        mm(po[:], fT[:, k, s * 128:(s + 1) * 128], Wf[:, k, nn * 512:(nn + 1) * 512], k == 0, k == 21, [fT, Wf], [po])
                    act(junk[:, nn * 512:(nn + 1) * 512], po[:], AF.Square, [po], [junk, s2], accum=s2[:, nn:nn + 1])
                tt("dve", s1[:], s2[:, 0:1], s2[:, 1:2], ALU.add, [s2], [s1])
                rstd_from_ss(s1[:], D, [s1])
                for nn in range(2):
                    stt("dve", y_[:, nn * 512:(nn + 1) * 512], pos[nn][:], s1[:, 0:1], gT[:, 3, nn * 512:(nn + 1) * 512],
                        ALU.mult, ALU.mult, [pos[nn], s1, gT], [y_])
                tt("pool", y_[:], y_[:], h_[:], ALU.add, [y_, h_], [y_])
                dma("pool", dst.h.ap()[i * 128:(i + 1) * 128, :], y_[:], [y_], [dst])

    PH = getattr(cfg, "phases", None)

    def on(name):
        return PH is None or name in PH

    mk.marks = []

    def mark(name):
        mk.marks.append((name, len(mk.eng_ops["pe"]), len(mk.eng_ops["act"]), len(mk.eng_ops["dve"])))

    for l in range(depth):
        src = xin if l == 0 else x1buf
        dst = x1buf if l < depth - 1 else yout
        barrier()
        mark(f"L{l}:start")
        load_layer_consts(l)
        if on("R"):
            phase_R(l)
        barrier()
        mark(f"L{l}:R")
        if on("F"):
            for gi in range(len(cfg.groups)):
                phase_F(l, gi)
                barrier()
        mark(f"L{l}:F")
        if on("A"):
            phase_A(l, src)
            barrier()
        mark(f"L{l}:A")
        if on("1b"):
            phase_1b(l)
            barrier()
        mark(f"L{l}:1b")
        if on("2A"):
            for sidx in range(len(cfg.seqs)):
                phase_2A(sidx)
                barrier()
                mark(f"L{l}:2A.{sidx}")
        if on("2B"):
            for gi in range(len(cfg.groups)):
                phase_2B(gi)
                barrier()
                mark(f"L{l}:2B.{gi}")
        if on("2C"):
            for sidx in range(len(cfg.seqs)):
                phase_2C(sidx)
                barrier()
            mark(f"L{l}:2C")
        if on("3a"):
            phase_3a(l, src)
            barrier()
        mark(f"L{l}:3a")
        if on("3b"):
            phase_3b(l, dst)
            barrier()
        mark(f"L{l}:3b")

    mk.emit()
    return nc, mk, tabs


def make_in_maps(cfg, inputs, tabs, ncore=NCORE):
    f = lambda a: np.ascontiguousarray(np.asarray(a, dtype=np.float32))
    xs, xp = f(inputs["x_sample"]), f(inputs["x_prompt"])
    dp = cfg.depth
    shared = {
        "w_in": f(inputs["w_in"])[:dp], "w_branch": f(inputs["w_branch"])[:dp], "w_out": f(inputs["w_out"])[:dp],
        "w_ffn_in": f(inputs["w_ffn_in"])[:dp], "w_ffn_out": f(inputs["w_ffn_out"])[:dp],
        "norm_gains": f(inputs["norm_gains"])[:dp],
        "qk_norm": f(inputs["qk_norm"])[:dp].reshape(dp, 128),
        "hy_w1": f(inputs["hy_w1"])[:dp], "hy_w2": f(inputs["hy_w2"])[:dp], "hy_w3": f(inputs["hy_w3"])[:dp],
        "rde": f(inputs["ret_decay_exp"])[:dp].reshape(dp, 8),
    }
    hcw, hcb = f(inputs["hy_conv_w"])[:dp], f(inputs["hy_conv_b"])[:dp]
    cv = np.concatenate([hcw, hcb[:, None, :]], axis=1)
    shared["convp"] = np.ascontiguousarray(cv.reshape(dp, 4, 6, 128).transpose(0, 3, 2, 1))
    shared["scwp"] = np.ascontiguousarray(f(inputs["sc_conv_w"])[:dp].reshape(dp, 3, 2, 128).transpose(0, 3, 2, 1))
    shared["hbiasp"] = np.ascontiguousarray(f(inputs["hy_bias"])[:dp].reshape(dp, 2, 128).transpose(0, 2, 1))
    hv = np.stack([f(inputs["hy_b1"])[:dp], f(inputs["hy_b2"])[:dp], f(inputs["hy_freq"])[:dp, 0], f(inputs["hy_freq"])[:dp, 1]], axis=-1)
    shared["hyvec"] = np.ascontiguousarray(hv)
    shared.update(tabs)
    maps = []
    for c in range(ncore):
        parts = [xs[c]] + [xp[cfg.NP * c + i] for i in range(cfg.NP)]
        m = dict(shared)
        m["xin"] = np.ascontiguousarray(np.concatenate(parts, axis=0))
        maps.append(m)
    return maps


_CACHE = {}


def kernel(**inputs):
    cfg = Cfg()
    if "nc" not in _CACHE:
        _CACHE["nc"] = build(cfg)
    nc, mk, tabs = _CACHE["nc"]
    maps = make_in_maps(cfg, inputs, tabs)
    res = run_bass_kernel_spmd(nc, maps, core_ids=list(range(NCORE)))
    ys = np.stack([r["yout"][:cfg.LS] for r in res.results], axis=0)
    yp = np.concatenate([r["yout"][cfg.LS:].reshape(cfg.NP, cfg.LP, D) for r in res.results], axis=0)
    return (np.ascontiguousarray(yp.astype(np.float32)), np.ascontiguousarray(ys.astype(np.float32)))
```

```python
import math
import numpy as np
import concourse.bass as bass
import concourse.mybir as mybir
from concourse.bass_utils import run_bass_kernel_spmd

F32 = mybir.dt.float32
BF16 = mybir.dt.bfloat16
AF = mybir.ActivationFunctionType
ALU = mybir.AluOpType
AX = mybir.AxisListType

D = 1024
DEPTH = 2
DFF = 2816
NCORE = 8
HD = 64
EPS = 1e-6
ENGS = ("pe", "act", "dve", "pool", "sp")
import os as _os
SAME_ENGINE_SYNC = _os.environ.get("SES", "1") == "1"


class Buf:
    __slots__ = ("name", "w", "r_eng", "r_dma")

    def __init__(self, name=""):
        self.name = name
        self.w = None
        self.r_eng = {}
        self.r_dma = []


class _Op:
    __slots__ = ("eng", "fn", "deps", "dma", "signal", "sigval", "ring", "ringval", "ringprev")

    def __init__(self, eng, fn, deps, dma):
        self.eng = eng
        self.fn = fn
        self.deps = deps
        self.dma = dma
        self.signal = False
        self.sigval = 0
        self.ring = None
        self.ringval = 0
        self.ringprev = 0


class MK:
    def __init__(self, nc, ring_k=8):
        self.nc = nc
        self.ops = []
        self.eng_ops = {e: [] for e in ENGS}
        self.ring_k = ring_k
        self.ring_use = {}
        self.ring_next = {e: 0 for e in ENGS}
        self.world = Buf("world")

    def op(self, eng, fn, reads=(), writes=(), dma=False, barrier=False):
        idx = len(self.ops)
        deps = set()
        reads = list(reads)
        writes = list(writes)
        if barrier:
            writes.append(self.world)
        else:
            reads.append(self.world)
        for b in reads:
            if b.w is not None:
                deps.add(b.w)
        for b in writes:
            if b.w is not None:
                deps.add(b.w)
            deps.update(b.r_eng.values())
            deps.update(b.r_dma)
        o = _Op(eng, fn, deps, dma)
        if dma:
            slot = self.ring_next[eng]
            self.ring_next[eng] = (slot + 1) % self.ring_k
            key = (eng, slot)
            u = self.ring_use.get(key, 0)
            o.ring = key
            o.ringprev = 16 * u
            o.ringval = 16 * (u + 1)
            self.ring_use[key] = u + 1
        self.ops.append(o)
        self.eng_ops[eng].append(idx)
        for b in reads:
            if dma:
                b.r_dma.append(idx)
            else:
                b.r_eng[eng] = idx
        for b in writes:
            b.w = idx
            b.r_eng = {}
            b.r_dma = []
        return idx

    def emit(self):
        nc = self.nc
        ops = self.ops
        esem = {e: nc.alloc_semaphore(name=f"es_{e}") for e in ENGS}
        rsem = {key: nc.alloc_semaphore(name=f"rs_{key[0]}_{key[1]}") for key in self.ring_use}

        def needs_sync(p, c):
            if p.dma:
                return True
            if p.eng == c.eng:
                if p.eng == "pe":
                    return False
                return SAME_ENGINE_SYNC
            return True

        for c in ops:
            for d in c.deps:
                p = ops[d]
                if (not p.dma) and needs_sync(p, c):
                    p.signal = True
        cnt = {e: 0 for e in ENGS}
        for o in ops:
            if (not o.dma) and o.signal:
                cnt[o.eng] += 1
                o.sigval = cnt[o.eng]
        known = {e: {} for e in ENGS}
        waits_of = [None] * len(ops)
        for idx, c in enumerate(ops):
            w = {}
            for d in c.deps:
                p = ops[d]
                if not needs_sync(p, c):
                    continue
                if p.dma:
                    s, v = ("r", p.ring), p.ringval
                else:
                    s, v = ("e", p.eng), p.sigval
                if v > w.get(s, 0):
                    w[s] = v
            if c.dma and c.ringprev > 0:
                s = ("r", c.ring)
                if c.ringprev > w.get(s, 0):
                    w[s] = c.ringprev
            kn = known[c.eng]
            lst = []
            for s, v in w.items():
                if v > kn.get(s, 0):
                    kn[s] = v
                    lst.append((s, v))
            waits_of[idx] = lst
        self.n_waits = sum(len(v) for v in waits_of)
        final_ring = {key: 16 * u for key, u in self.ring_use.items()}

        def semh(s):
            return rsem[s[1]] if s[0] == "r" else esem[s[1]]

        def replay(ename, eng):
            for idx in self.eng_ops[ename]:
                o = ops[idx]
                for s, v in waits_of[idx]:
                    eng.wait_ge(semh(s), v)
                ins = o.fn(eng)
                if o.dma:
                    ins.then_inc(rsem[o.ring], 16)
                elif o.signal:
                    ins.then_inc(esem[ename], 1)
            if ename == "sp":
                for key, v in final_ring.items():
                    eng.wait_ge(rsem[key], v)

        with nc.Block() as block:
            @block.tensor
            def _(e):
                replay("pe", e)

            @block.scalar
            def _(e):
                replay("act", e)

            @block.vector
            def _(e):
                replay("dve", e)

            @block.gpsimd
            def _(e):
                replay("pool", e)

            @block.sync
            def _(e):
                replay("sp", e)


class T:
    __slots__ = ("h", "b", "F")

    def __init__(self, h, name=""):
        self.h = h
        self.b = Buf(name)
        sh = list(h.shape)
        f = 1
        for s in sh[1:]:
            f *= s
        self.F = f

    def __getitem__(self, k):
        return self.h[k]

    def ap(self, off, dims, p0=0, np_=128):
        return bass.AP(self.h, p0 * self.F + off, [[self.F, np_]] + [list(d) for d in dims])


def _dsize(dt):
    return 4 if dt == F32 else 2


class Arena:
    def __init__(self, nc, base, top):
        self.nc = nc
        self.base = base
        self.top = top
        self.off = base
        self.n = 0

    def mark(self):
        return self.off

    def reset(self, to=None):
        self.off = self.base if to is None else to

    def alloc(self, name, shape, dt):
        nb = _dsize(dt)
        for s in shape[1:]:
            nb *= s
        off = (self.off + 31) // 32 * 32
        assert off + nb <= self.top, f"SBUF arena overflow allocating {name}: {off}+{nb} > {self.top}"
        self.n += 1
        h = self.nc.alloc_sbuf_tensor_at(f"{name}_{self.n}", list(shape), dt, offset=off)
        self.off = off + nb
        return T(h, name)


class Cfg:
    def __init__(self, LS=8192, LP=2048, NP=2, depth=DEPTH):
        self.LS, self.LP, self.NP, self.depth = LS, LP, NP, depth
        self.seqs = [(0, LS)] + [(LS + i * LP, LP) for i in range(NP)]
        self.NT = LS + NP * LP
        self.groups = [(LS, [0])] + ([(LP, list(range(1, 1 + NP)))] if NP else [])
        self.LMAX = max(LS, LP)


def host_tables(cfg):
    tabs = {}
    L = cfg.LMAX
    t = np.arange(L)
    r = (t // 64).astype(np.float32)
    c = (t % 64).astype(np.float32)
    inv = (10000.0 ** (-np.arange(16, dtype=np.float32) / 16)).astype(np.float32)
    ang = np.stack([r[:, None] * inv, c[:, None] * inv], axis=1).astype(np.float32)
    cs, sn = np.cos(ang).astype(np.float32), np.sin(ang).astype(np.float32)
    C = np.stack([cs, cs], axis=2)
    S = np.stack([-sn, sn], axis=2)
    tabs["ropeC"] = np.ascontiguousarray(C.reshape(L, 64))
    tabs["ropeS"] = np.ascontiguousarray(S.reshape(L, 64))
    tabs["ident"] = np.eye(128, dtype=np.float32)
    for gi, (Lg, _) in enumerate(cfg.groups):
        m = np.arange(2 * Lg)
        n = np.abs(m - Lg)
        n = np.minimum(n, Lg - 1)
        tt = np.linspace(0.0, 1.0, Lg, dtype=np.float32)
        f = np.linspace(1e-4, 15.0, 16, dtype=np.float32)
        angf = ((2.0 * math.pi / Lg) * np.arange(Lg, dtype=np.float32)[:, None] * f[None, :]).astype(np.float32)
        z = np.concatenate([tt[:, None], np.cos(angf), -np.sin(angf)], axis=-1).astype(np.float32)
        tabs[f"zext{gi}"] = np.ascontiguousarray(z[n].T)
        tabs[f"trow{gi}"] = np.ascontiguousarray(tt[n][None, :])
    deltas = np.abs(np.linspace(math.log(1e-2) / 1.5, math.log(1e-2) / 0.3, 256, dtype=np.float32))
    tabs["negdelta"] = np.ascontiguousarray((-deltas).reshape(2, 128).T.astype(np.float32))
    i = np.arange(128, dtype=np.float32)
    diff = i[None, :] - i[:, None]
    tabs["dpos"] = np.maximum(diff, 0).astype(np.float32)
    tabs["dneg"] = np.maximum(-diff, 0).astype(np.float32)
    tabs["iqf"] = np.tile((i + 1)[None, :], (128, 1)).astype(np.float32)
    tabs["iqb"] = np.tile((128 - i)[None, :], (128, 1)).astype(np.float32)
    tabs["jk"] = np.stack([127 - i, i], axis=1).astype(np.float32)
    return tabs


DEBUG_OUT = set()


def build(cfg):
    nc = bass.Bass("TRN2", target_bir_lowering=False)
    mk = MK(nc)
    NT, LS = cfg.NT, cfg.LS
    depth = cfg.depth
    NTILE = NT // 512

    def din(name, shape, dt=F32):
        return nc.dram_tensor(name, list(shape), dt, kind="ExternalInput")

    def dscr(name, shape, dt):
        if name in DEBUG_OUT:
            return T(nc.dram_tensor(name, list(shape), dt, kind="ExternalOutput"), name)
        return T(nc.dram_tensor(name, list(shape), dt), name)

    xin = T(din("xin", [NT, D]), "xin")
    yout = T(nc.dram_tensor("yout", [NT, D], F32, kind="ExternalOutput"), "yout")
    w_in = din("w_in", [depth, D, 7168])
    w_br = din("w_branch", [depth, 4, 256, D])
    w_o = din("w_out", [depth, D, D])
    w_fi = din("w_ffn_in", [depth, D, 2 * DFF])
    w_fo = din("w_ffn_out", [depth, DFF, D])
    ngain = din("norm_gains", [depth, 4, D])
    qkn = din("qk_norm", [depth, 128])
    convp = din("convp", [depth, 128, 6, 4])
    scwp = din("scwp", [depth, 128, 2, 3])
    hbiasp = din("hbiasp", [depth, 128, 2])
    hy_w1 = din("hy_w1", [depth, 33, 64])
    hy_w2 = din("hy_w2", [depth, 64, 64])
    hy_w3 = din("hy_w3", [depth, 64, 512])
    hyvec = din("hyvec", [depth, 64, 4])
    rde = din("rde", [depth, 8])
    tabs = host_tables(cfg)
    tin = {k: din(k, v.shape) for k, v in tabs.items()}

    x1buf = dscr("x1buf", [NT, D], F32)
    hbuf = dscr("hbuf", [NT, D], F32)
    xnT_s = dscr("xnT_s", [D, NT], BF16)
    hnT_s = dscr("hnT_s", [D, NT], BF16)
    brT_s = [dscr(f"brT{n}", [256, NT], BF16) for n in range(4)]
    qT_s = dscr("qT_s", [256, NT], BF16)
    kT_s = dscr("kT_s", [128, NT], BF16)
    v_s = dscr("v_s", [NT, 130], BF16)
    ret_s = dscr("ret_s", [NT, 1024], BF16)
    x0T_s = dscr("x0T_s", [256, NT], BF16)
    zT_s = dscr("zT_s", [256, NT], BF16)
    zr_s = dscr("zr_s", [128, NT // 128, 256], BF16)
    kfl = [dscr(f"kfl{gi}", [256, 2 * Lg], BF16) for gi, (Lg, _) in enumerate(cfg.groups)]

    ar = Arena(nc, 18432, 229344)
    ident = ar.alloc("ident", [128, 128], BF16)
    epsT = ar.alloc("eps", [128, 1], F32)
    gT = ar.alloc("gains", [128, 4, D], F32)
    gqk = ar.alloc("gqk", [128, 6, 64], F32)
    cvp = ar.alloc("cvp", [128, 6, 4], F32)
    scw = ar.alloc("scw", [128, 2, 3], F32)
    hbi = ar.alloc("hbi", [128, 2], F32)
    ndl = ar.alloc("ndl", [128, 2], F32)
    rnrm = [ar.alloc(f"rnrm{gi}", [128, 2], F32) for gi in range(len(cfg.groups))]
    Dm = ar.alloc("Dm", [128, 4, 128], F32)
    wqf = ar.alloc("wqf", [128, 2, 128], F32)
    wqb = ar.alloc("wqb", [128, 2, 128], F32)
    wkk = ar.alloc("wkk", [128, 4, 2], F32)
    gch = ar.alloc("gch", [128, 2, 2], F32)
    PERSIST = ar.mark()
    ar.base = PERSIST

    PF = [T(nc.alloc_psum_tensor(f"pf{i}", [128, 512], F32), f"pf{i}") for i in range(6)]
    PT = [T(nc.alloc_psum_tensor(f"pt{i}", [128, 1024], BF16), f"pt{i}") for i in range(2)]
    rot = {"pf": 0, "pt": 0}

    def next_pf(n=6, base=0):
        i = rot["pf"] % n + base
        rot["pf"] += 1
        return PF[i]

    def next_pt():
        i = rot["pt"] % 2
        rot["pt"] += 1
        return PT[i]

    def bl(x):
        return [t.b if isinstance(t, T) else t for t in x]

    def mm(out, lhsT, rhs, start, stop, R, W, skip=False):
        mk.op("pe", lambda e: e.matmul(out, lhsT=lhsT, rhs=rhs, start=start, stop=stop, skip_group_check=skip), bl(R), bl(W))

    def tr(out, in_, R, W):
        idn = ident[:]
        mk.op("pe", lambda e: e.transpose(out=out, in_=in_, identity=idn), bl(R) + [ident.b], bl(W))

    def act(out, in_, func, R, W, scale=1.0, bias=None, accum=None):
        def f(e):
            kw = {}
            if bias is not None:
                kw["bias"] = bias
            if accum is not None:
                kw["accum_out"] = accum
            return e.activation(out=out, in_=in_, func=func, scale=scale, **kw)
        mk.op("act", f, bl(R), bl(W))

    def tt(eng, out, in0, in1, op, R, W):
        mk.op(eng, lambda e: e.tensor_tensor(out=out, in0=in0, in1=in1, op=op), bl(R), bl(W))

    def ts(eng, out, in0, s1, s2, op0, op1, R, W):
        if s2 is None:
            mk.op(eng, lambda e: e.tensor_scalar(out=out, in0=in0, scalar1=s1, scalar2=None, op0=op0), bl(R), bl(W))
        else:
            mk.op(eng, lambda e: e.tensor_scalar(out=out, in0=in0, scalar1=s1, scalar2=s2, op0=op0, op1=op1), bl(R), bl(W))

    def stt(eng, out, in0, sc, in1, op0, op1, R, W):
        mk.op(eng, lambda e: e.scalar_tensor_tensor(out=out, in0=in0, scalar=sc, in1=in1, op0=op0, op1=op1), bl(R), bl(W))

    def cp(eng, out, in_, R, W):
        if eng == "act":
            act(out, in_, AF.Copy, R, W)
        else:
            mk.op(eng, lambda e: e.tensor_copy(out=out, in_=in_), bl(R), bl(W))

    def red(eng, out, in_, R, W):
        mk.op(eng, lambda e: e.tensor_reduce(out=out, in_=in_, axis=AX.X, op=ALU.add), bl(R), bl(W))

    def rcp(out, in_, R, W):
        mk.op("dve", lambda e: e.reciprocal(out=out, in_=in_), bl(R), bl(W))

    def mset(eng, out, val, W):
        mk.op(eng, lambda e: e.memset(out, val), [], bl(W))

    def dma(q, out, in_, R, W):
        mk.op(q, lambda e: e.dma_start(out=out, in_=in_), bl(R), bl(W), dma=True)

    def barrier():
        mk.op("pool", lambda e: e.memset(epsT[:, 0:1], EPS), [], [epsT.b], barrier=True)

    def rstd_from_ss(ss, n, R):
        act(ss, ss, AF.Sqrt, R, R, scale=1.0 / n, bias=epsT[:, 0:1])
        rcp(ss, ss, R, R)

    mset("pool", epsT[:], EPS, [epsT])
    dma("pool", ident[:], tin["ident"].ap(), [], [ident])
    dma("sp", ndl[:], tin["negdelta"].ap(), [], [ndl])

    def phase_A(l, src):
        ar.reset()
        xt = [ar.alloc("xt", [128, D], F32) for _ in range(3)]
        junk = ar.alloc("junk", [128, D], BF16)
        ssA = [ar.alloc("ss", [128, 1], F32) for _ in range(3)]
        xn = [ar.alloc("xn", [128, D], BF16) for _ in range(2)]
        xT = [ar.alloc("xT", [128, 8, 512], BF16) for _ in range(2)]
        for ti in range(NTILE):
            xTt = xT[ti % 2]
            for s in range(4):
                i = ti * 4 + s
                x_, ss_, xn_ = xt[i % 3], ssA[i % 3], xn[i % 2]
                dma("sp", x_[:], src.h.ap()[i * 128:(i + 1) * 128, :], [src], [x_])
                act(junk[:], x_[:], AF.Square, [x_], [junk, ss_], accum=ss_[:])
                rstd_from_ss(ss_[:], D, [ss_])
                stt("dve", xn_[:], x_[:], ss_[:, 0:1], gT[:, 0, :], ALU.mult, ALU.mult, [x_, ss_, gT], [xn_])
                pt = next_pt()
                for k in range(8):
                    tr(pt[:, k * 128:(k + 1) * 128], xn_[:, k * 128:(k + 1) * 128], [xn_], [pt])
                cp("act" if s % 2 else "dve", xTt[:, :, s * 128:(s + 1) * 128],
                   pt[:].rearrange("p (k t) -> p k t", t=128), [pt], [xTt])
            dma("pool", xnT_s.h.ap().rearrange("(k p) t -> p k t", p=128)[:, :, ti * 512:(ti + 1) * 512], xTt[:], [xTt], [xnT_s])

    def load_layer_consts(l):
        for j in range(4):
            dma("sp", gT[:, j, :], bass.AP(ngain, (l * 4 + j) * D, [[0, 128], [1, D]]), [], [gT])
        for h in range(6):
            off = l * 128 + (0 if h < 4 else 64)
            dma("sp", gqk[:, h, :], bass.AP(qkn, off, [[0, 128], [1, 64]]), [], [gqk])
        dma("sp", cvp[:], convp.ap()[l], [], [cvp])
        dma("sp", scw[:], scwp.ap()[l], [], [scw])
        dma("sp", hbi[:], hbiasp.ap()[l], [], [hbi])

    def phase_R(l):
        ar.reset()
        rd = ar.alloc("rd", [128, 8], F32)
        lg = ar.alloc("lg", [128, 8], F32)
        tmp = ar.alloc("tmpR", [128, 128], F32)
        dpos = ar.alloc("dpos", [128, 128], F32)
        dneg = ar.alloc("dneg", [128, 128], F32)
        iqf = ar.alloc("iqf", [128, 128], F32)
        iqb = ar.alloc("iqb", [128, 128], F32)
        jk = ar.alloc("jk", [128, 2], F32)
        lsel = ar.alloc("lsel", [128, 2, 2], F32)
        dma("sp", rd[:], bass.AP(rde, l * 8, [[0, 128], [1, 8]]), [], [rd])
        dma("sp", dpos[:], tin["dpos"].ap(), [], [dpos])
        dma("sp", dneg[:], tin["dneg"].ap(), [], [dneg])
        dma("sp", iqf[:], tin["iqf"].ap(), [], [iqf])
        dma("sp", iqb[:], tin["iqb"].ap(), [], [iqb])
        dma("sp", jk[:], tin["jk"].ap(), [], [jk])
        act(lg[:], rd[:], AF.Exp, [rd], [lg], scale=-math.log(2.0))
        act(lg[:], lg[:], AF.Ln, [lg], [lg], scale=-1.0, bias=1.0)
        for h in range(4):
            ts("dve", tmp[:], dpos[:], lg[:, h:h + 1], None, ALU.mult, None, [dpos, lg], [tmp])
            stt("dve", tmp[:], dneg[:], lg[:, 4 + h:5 + h], tmp[:], ALU.mult, ALU.add, [dneg, lg, tmp], [tmp])
            act(Dm[:, h, :], tmp[:], AF.Exp, [tmp], [Dm])
            act(wkk[:, h, 0:1], jk[:, 0:1], AF.Exp, [jk, lg], [wkk], scale=lg[:, h:h + 1])
            act(wkk[:, h, 1:2], jk[:, 1:2], AF.Exp, [jk, lg], [wkk], scale=lg[:, 4 + h:5 + h])
        for p in range(2):
            for half in range(2):
                h = 2 * p + half
                r0, r1 = half * 64, half * 64 + 64
                for d in range(2):
                    cp("dve", lsel[r0:r1, p, d:d + 1], lg[r0:r1, 4 * d + h:4 * d + h + 1], [lg], [lsel])
            act(wqf[:, p, :], iqf[:], AF.Exp, [iqf, lsel], [wqf], scale=lsel[:, p, 0:1])
            act(wqb[:, p, :], iqb[:], AF.Exp, [iqb, lsel], [wqb], scale=lsel[:, p, 1:2])
            act(gch[:, p, :], lsel[:, p, :], AF.Exp, [lsel], [gch], scale=128.0)

    def phase_F(l, gi):
        Lg, _ = cfg.groups[gi]
        ar.reset()
        NCH = (2 * Lg) // 512
        w1 = ar.alloc("w1", [33, 64], F32)
        w2 = ar.alloc("w2", [64, 64], F32)
        w3 = ar.alloc("w3", [64, 512], F32)
        hv = ar.alloc("hv", [64, 4], F32)
        ssq = ar.alloc("ssq", [128, 2, NCH], F32)
        ze = [ar.alloc("ze", [33, 512], F32) for _ in range(2)]
        trw = [ar.alloc("trw", [128, 512], F32) for _ in range(2)]
        a1 = [ar.alloc("a1", [64, 512], F32) for _ in range(2)]
        tw = [ar.alloc("tw", [64, 512], F32) for _ in range(2)]
        h1 = [ar.alloc("h1", [64, 512], F32) for _ in range(2)]
        dec = [ar.alloc("dec", [128, 512], F32) for _ in range(2)]
        kf = [ar.alloc("kf", [128, 512], F32) for _ in range(2)]
        kb = [ar.alloc("kb", [128, 512], BF16) for _ in range(4)]
        jk2 = ar.alloc("jk2", [128, 512], BF16)
        dma("sp", w1[:], hy_w1.ap()[l], [], [w1])
        dma("sp", w2[:], hy_w2.ap()[l], [], [w2])
        dma("sp", w3[:], hy_w3.ap()[l], [], [w3])
        dma("sp", hv[:], hyvec.ap()[l], [], [hv])
        mset("pool", ssq[:], 0.0, [ssq])
        PI, TWO_PI = math.pi, 2 * math.pi

        def sin_layer(ps, dst, bcol, fcol, a_, t_):
            ts("dve", a_[:], ps[0:64, :], hv[:, bcol:bcol + 1], hv[:, fcol:fcol + 1], ALU.add, ALU.mult, [hv, ps], [a_])
            for _ in range(1):
                ts("dve", t_[:], a_[:], PI, None, ALU.is_gt, None, [a_], [t_])
                stt("dve", a_[:], t_[:], -TWO_PI, a_[:], ALU.mult, ALU.add, [t_, a_], [a_])
                ts("dve", t_[:], a_[:], -PI, None, ALU.is_lt, None, [a_], [t_])
                stt("dve", a_[:], t_[:], TWO_PI, a_[:], ALU.mult, ALU.add, [t_, a_], [a_])
            act(dst[:], a_[:], AF.Sin, [a_], [dst])

        for ch in range(NCH):
            m0 = ch * 512
            ze_, tr_, a_, t_, h_ = ze[ch % 2], trw[ch % 2], a1[ch % 2], tw[ch % 2], h1[ch % 2]
            dma("sp", ze_[:], tin[f"zext{gi}"].ap()[:, m0:m0 + 512], [], [ze_])
            dma("sp", tr_[:], bass.AP(tin[f"trow{gi}"], m0, [[0, 128], [1, 512]]), [], [tr_])
            p1 = next_pf()
            mm(p1[0:64, :], w1[:], ze_[:], True, True, [w1, ze_], [p1])
            sin_layer(p1, h_, 0, 2, a_, t_)
            p2 = next_pf()
            mm(p2[0:64, :], w2[:], h_[:], True, True, [w2, h_], [p2])
            sin_layer(p2, h_, 1, 3, a_, t_)
            dirn = 1 if m0 < Lg else 0
            for cc in range(2):
                p3 = next_pf()
                c0 = dirn * 256 + cc * 128
                mm(p3[:], w3[:, c0:c0 + 128], h_[:], True, True, [w3, h_], [p3])
                d_, k_, kb_ = dec[cc], kf[cc], kb[(ch * 2 + cc) % 4]
                act(d_[:], tr_[:], AF.Exp, [tr_, ndl], [d_], scale=ndl[:, cc:cc + 1])
                tt("dve", k_[:], p3[:], d_[:], ALU.mult, [p3, d_], [k_])
                if ch == 0:
                    mset("pool", k_[:, 0:1], 0.0, [k_])
                act(jk2[:], k_[:], AF.Square, [k_], [jk2, ssq], accum=ssq[:, cc, ch:ch + 1])
                cp("pool", kb_[:], k_[:], [k_], [kb_])
                dma("pool", kfl[gi].h.ap()[cc * 128:(cc + 1) * 128, m0:m0 + 512], kb_[:], [kb_], [kfl[gi]])
        red("dve", rnrm[gi][:], ssq[:], [ssq], [rnrm[gi]])
        rstd_from_ss(rnrm[gi][:], 1.0, [rnrm[gi]])

    def phase_1b(l):
        ar.reset()
        W1 = ar.alloc("W1", [128, 8, 3072], BF16)
        for k in range(8):
            dma("pool", W1[:, k, :], w_in.ap()[l, k * 128:(k + 1) * 128, 0:3072], [], [W1])
        xe = [ar.alloc("xe", [128, 8, 514], BF16) for _ in range(2)]
        Ct = [ar.alloc("Ct", [128, 4, 8, 64], F32)] * 2
        St = [ar.alloc("St", [128, 4, 8, 64], F32)] * 2
        qk32 = [ar.alloc("qk32", [128, 384], F32) for _ in range(2)]
        sq = ar.alloc("sq", [128, 384], F32)
        ss6 = [ar.alloc("ss6", [128, 6], F32) for _ in range(2)]
        t1 = [ar.alloc("t1", [128, 512], F32) for _ in range(2)]
        t2 = [ar.alloc("t2", [128, 512], F32) for _ in range(2)]
        qkbf = [ar.alloc("qkbf", [128, 384], BF16) for _ in range(4)]
        qkT = [ar.alloc("qkT", [128, 3, 512], BF16) for _ in range(2)]
        vt = [ar.alloc("vt", [128, 4, 2, 65], BF16) for _ in range(2)]
        r32 = [ar.alloc("r32", [128, 512], F32) for _ in range(2)]
        rtok = [ar.alloc("rtok", [128, 4, 1024], BF16) for _ in range(2)]
        pext = [ar.alloc("pext", [128, 514], F32) for _ in range(6)]
        uu = [ar.alloc("uu", [128, 512], F32) for _ in range(6)]
        z32 = [ar.alloc("z32", [128, 512], F32) for _ in range(2)]
        obf = [ar.alloc("obf", [128, 512], BF16) for _ in range(4)]
        zrv = [ar.alloc("zrv", [128, 512], BF16) for _ in range(2)]
        zst = [ar.alloc("zst", [128, 4, 256], BF16) for _ in range(2)]
        gext = [ar.alloc("gext", [128, 514], F32) for _ in range(2)]
        for v_ in vt:
            mset("pool", v_[:], 1.0, [v_])
        PH = PF[5]
        it = [0]

        for ti in range(NTILE):
            t0 = ti * 512
            sidx = [i for i, (s0, L) in enumerate(cfg.seqs) if s0 <= t0 < s0 + L][0]
            s0, L = cfg.seqs[sidx]
            pos0 = t0 - s0
            xe_, Ct_, St_ = xe[ti % 2], Ct[ti % 2], St[ti % 2]
            lo = max(t0 - 1, s0)
            hi = min(t0 + 513, s0 + L)
            c_lo = lo - (t0 - 1)
            dma("sp", xe_[:, :, c_lo:c_lo + (hi - lo)], xnT_s.h.ap().rearrange("(k p) t -> p k t", p=128)[:, :, lo:hi], [xnT_s], [xe_])
            if c_lo > 0:
                mset("pool", xe_[:, :, 0:1], 0.0, [xe_])
            if hi < t0 + 513:
                mset("pool", xe_[:, :, 513:514], 0.0, [xe_])
            for s in range(4):
                dma("sp", Ct_[:, s, :, :], bass.AP(tin["ropeC"], (pos0 + s * 128) * 64, [[64, 128], [0, 8], [1, 64]]), [], [Ct_])
                dma("sp", St_[:, s, :, :], bass.AP(tin["ropeS"], (pos0 + s * 128) * 64, [[64, 128], [0, 8], [1, 64]]), [], [St_])
            qkT_, vt_, rtok_ = qkT[ti % 2], vt[ti % 2], rtok[ti % 2]

            def rope(src, nh, Cs, Ss, t1_, t2_):
                w = nh * 64
                tt("dve", t1_[:, 0:w], src[:, 0:w], Cs, ALU.mult, [src, Ct_], [t1_])
                sw = src.ap(16, [[32, nh * 2], [-16, 2], [1, 16]])
                tt("dve", t2_[:, 0:w].rearrange("p (a h f) -> p a h f", h=2, f=16), sw, Ss, ALU.mult, [src, St_], [t2_])

            deferred = []

            def flush():
                while deferred:
                    deferred.pop(0)()

            for s in range(4):
                j = it[0]
                it[0] += 1
                q32, ss_, t1_, t2_, qb_ = qk32[j % 2], ss6[j % 2], t1[j % 2], t2[j % 2], qkbf[j % 4]
                xs = slice(1 + s * 128, 1 + s * 128 + 128)
                pa = next_pf(5)
                for k in range(8):
                    mm(pa[:], xe_[:, k, xs], W1[:, k, 0:512], k == 0, k == 7, [xe_, W1], [pa])
                pr = next_pf(5)
                for k in range(8):
                    mm(pr[:], xe_[:, k, xs], W1[:, k, 1280:1792], k == 0, k == 7, [xe_, W1], [pr])
                pv = next_pf(5)
                for k in range(8):
                    mm(pv[:], xe_[:, k, xs], W1[:, k, 1792:2304], k == 0, k == 7, [xe_, W1], [pv])
                act(q32[:], pa[:, 0:384], AF.Copy, [pa], [q32])
                act(vt_[:, s, :, 0:64], pa[:, 384:512].rearrange("p (g d) -> p g d", d=64), AF.Copy, [pa], [vt_])
                tt("dve", sq[:], q32[:], q32[:], ALU.mult, [q32], [sq])
                red("dve", ss_[:], sq[:].rearrange("p (h d) -> p h d", d=64), [sq], [ss_])
                rstd_from_ss(ss_[:], 64.0, [ss_])
                tt("dve", q32[:].rearrange("p (h d) -> p h d", d=64), q32[:].rearrange("p (h d) -> p h d", d=64),
                   ss_.ap(0, [[1, 6], [0, 64]]), ALU.mult, [q32, ss_], [q32])
                tt("dve", q32[:], q32[:], gqk[:].rearrange("p h d -> p (h d)"), ALU.mult, [q32, gqk], [q32])
                rope(q32, 6, Ct_[:, s, 0:6, :].rearrange("p h d -> p (h d)"),
                     St_[:, s, 0:6, :].rearrange("p h (a b f) -> p (h a) b f", a=2, b=2, f=16), t1_, t2_)
                tt("dve", qb_.ap(0, [[64, 2], [128, 2], [1, 64]]), t1_.ap(0, [[128, 2], [64, 2], [1, 64]]),
                   t2_.ap(0, [[128, 2], [64, 2], [1, 64]]), ALU.add, [t1_, t2_], [qb_])
                tt("dve", qb_[:, 256:384], t1_[:, 256:384], t2_[:, 256:384], ALU.add, [t1_, t2_], [qb_])

                def do_tr(qb_=qb_, s=s):
                    pt = next_pt()
                    for c3 in range(3):
                        tr(pt[:, c3 * 128:(c3 + 1) * 128], qb_[:, c3 * 128:(c3 + 1) * 128], [qb_], [pt])
                    cp("act", qkT_[:, :, s * 128:(s + 1) * 128], pt[:, 0:384].rearrange("p (c t) -> p c t", t=128), [pt], [qkT_])
                deferred.append(do_tr)
                r_, rt1, rt2 = r32[j % 2], t1[(j + 1) % 2], t2[(j + 1) % 2]
                act(r_[:, 0:256], pr[:, 0:256], AF.Copy, [pr], [r_])
                act(r_[:, 256:512], pr[:, 256:512], AF.Copy, [pr], [r_], scale=0.125)
                rope(r_, 8, Ct_[:, s, :, :].rearrange("p h d -> p (h d)"),
                     St_[:, s, :, :].rearrange("p h (a b f) -> p (h a) b f", a=2, b=2, f=16), rt1, rt2)
                tt("dve", rtok_[:, s, 0:512], rt1[:], rt2[:], ALU.add, [rt1, rt2], [rtok_])
                act(rtok_[:, s, 512:768], pv[:, 0:256], AF.Copy, [pv], [rtok_])
                act(rtok_[:, s, 768:1024], pv[:, 256:512], AF.Silu, [pv], [rtok_])

            def tail_dmas():
                dma("pool", qT_s.h.ap().rearrange("(c p) t -> p c t", p=128)[:, :, t0:t0 + 512], qkT_[:, 0:2, :], [qkT_], [qT_s])
                dma("pool", kT_s.h.ap()[:, t0:t0 + 512], qkT_[:, 2, :], [qkT_], [kT_s])
            deferred.append(tail_dmas)
            dma("pool", v_s.h.ap()[t0:t0 + 512, :].rearrange("(s p) c -> p s c", p=128), vt_[:].rearrange("p s g d -> p s (g d)"), [vt_], [v_s])
            dma("pool", ret_s.h.ap()[t0:t0 + 512, :].rearrange("(s p) c -> p s c", p=128), rtok_[:], [rtok_], [ret_s])

            def fm_chunk(col0, hslot, dst, halo):
                pm = next_pf(5)
                for k in range(8):
                    mm(pm[:], W1[:, k, col0:col0 + 128], xe_[:, k, 1:513], k == 0, k == 7, [xe_, W1], [pm])
                if halo:
                    for k in range(8):
                        mm(PH[:, hslot * 2:hslot * 2 + 2], W1[:, k, col0:col0 + 128], xe_.ap(k * 514, [[513, 2]]),
                           k == 0, k == 7, [xe_, W1], [PH])
                    cp("dve", dst.ap(0, [[513, 2]]), PH[:, hslot * 2:hslot * 2 + 2], [PH], [dst])
                act(dst[:, 1:513], pm[:], AF.Copy, [pm], [dst])

            def conv3(eng, dst, src, wcol, wt, bias):
                if bias is not None:
                    ts(eng, dst[:], src[:, 1:513], wt[:, wcol, 1:2], bias, ALU.mult, ALU.add, [src, wt], [dst])
                else:
                    ts(eng, dst[:], src[:, 1:513], wt[:, wcol, 1:2], None, ALU.mult, None, [src, wt], [dst])
                stt(eng, dst[:], src[:, 0:512], wt[:, wcol, 0:1], dst[:], ALU.mult, ALU.add, [src, wt, dst], [dst])
                stt(eng, dst[:], src[:, 2:514], wt[:, wcol, 2:3], dst[:], ALU.mult, ALU.add, [src, wt, dst], [dst])

            for jc in range(6):
                fm_chunk(512 + jc * 128, jc, pext[jc], True)
                if deferred:
                    deferred.pop(0)()
                conv3("dve", uu[jc], pext[jc], jc, cvp, cvp[:, jc, 3:4])
            flush()
            for jc in range(6):
                fm_chunk(2304 + jc * 128, 6 + jc, pext[jc], jc >= 2)
            for cc in range(2):
                ob = obf[cc]
                cp("pool", ob[:], uu[cc][:], [uu[cc]], [ob])
                dma("pool", x0T_s.h.ap()[cc * 128:(cc + 1) * 128, t0:t0 + 512], ob[:], [ob], [x0T_s])
                z_ = z32[cc]
                tt("pool", z_[:], uu[4 + cc][:], uu[2 + cc][:], ALU.mult, [uu[4 + cc], uu[2 + cc]], [z_])
                ob2 = obf[2 + cc]
                cp("pool", ob2[:], z_[:], [z_], [ob2])
                dma("pool", zT_s.h.ap()[cc * 128:(cc + 1) * 128, t0:t0 + 512], ob2[:], [ob2], [zT_s])
                zr_ = zrv[cc]
                cp("pool", zr_[:].rearrange("p (a b) -> p a b", b=128), z_.ap(127, [[128, 4], [-1, 128]]), [z_], [zr_])
                pt = next_pt()
                for b4 in range(4):
                    tr(pt[:, b4 * 128:(b4 + 1) * 128], zr_[:, b4 * 128:(b4 + 1) * 128], [zr_], [pt])
                zs_ = zst[ti % 2]
                cp("act", zs_[:, :, cc * 128:(cc + 1) * 128], pt[:, 0:512].rearrange("p (a c) -> p a c", c=128), [pt], [zs_])
            dma("pool", zr_s.h.ap()[:, t0 // 128:t0 // 128 + 4, :], zst[ti % 2][:], [zst[ti % 2]], [zr_s])
            for cc in range(2):
                g_ = gext[cc]
                tt("pool", g_[:], pext[2 + cc][:], pext[4 + cc][:], ALU.mult, [pext[2 + cc], pext[4 + cc]], [g_])
                cv = uu[cc]
                conv3("dve", cv, g_, cc, scw, None)
                ob = obf[cc]
                tt("dve", ob[:], cv[:], pext[cc][:, 1:513], ALU.mult, [cv, pext[cc]], [ob])
                dma("pool", brT_s[3].h.ap()[cc * 128:(cc + 1) * 128, t0:t0 + 512], ob[:], [ob], [brT_s[3]])

    def phase_2A(sidx):
        s0, L = cfg.seqs[sidx]
        ar.reset()
        NKC = L // 128
        qT = ar.alloc("qT", [128, 2, L], BF16)
        kT = ar.alloc("kT", [128, L], BF16)
        V = ar.alloc("V", [128, NKC, 130], BF16)
        dma("sp", qT[:], qT_s.h.ap().rearrange("(c p) t -> p c t", p=128)[:, :, s0:s0 + L], [qT_s], [qT])
        dma("sp", kT[:], kT_s.h.ap()[:, s0:s0 + L], [kT_s], [kT])
        for v0 in range(0, NKC, 8):
            v1 = min(NKC, v0 + 8)
            dma("sp", V[:, v0:v1, :], v_s.h.ap()[s0 + v0 * 128:s0 + v1 * 128, :].rearrange("(n p) c -> p n c", p=128), [v_s], [V])
        Pt = [ar.alloc("Pt", [128, 512], BF16) for _ in range(3)]
        oa = [ar.alloc("oa", [128, 4, 256], BF16) for _ in range(2)]
        rden = [ar.alloc("rden", [128, 4], F32) for _ in range(2)]
        oaT = [ar.alloc("oaT", [128, 2, 512], BF16) for _ in range(2)]
        steps = [(qb, 2 * cqp + g, kc) for qb in range(L // 512) for cqp in range(2) for kc in range(NKC) for g in range(2)]
        nst = len(steps)
        LOOK = 2

        def emit_ST(i):
            qb, hh, kc = steps[i]
            cq, g = hh // 2, hh % 2
            rows = slice(64 * g, 64 * g + 64)
            ps = PF[i % 4]
            mm(ps[:], kT[rows, kc * 128:(kc + 1) * 128], qT[rows, cq, qb * 512:(qb + 1) * 512], True, True, [kT, qT], [ps])

        for i in range(min(LOOK, nst)):
            emit_ST(i)
        for i in range(nst):
            if i % 2 == 0:
                for ii in (i + LOOK, i + LOOK + 1):
                    if ii < nst:
                        emit_ST(ii)
            qb, hh, kc = steps[i]
            cq, g = hh // 2, hh % 2
            head = 2 * g + cq
            po = PF[4 + (hh % 2)]
            ps = PF[i % 4]
            P_ = Pt[i % 3]
            oa_ = oa[qb % 2]
            act(P_[:], ps[:], AF.Exp, [ps], [P_], scale=0.125)
            for qs in range(4):
                mm(po[:, qs * 65:qs * 65 + 65], P_[:, qs * 128:(qs + 1) * 128], V[:, kc, g * 65:g * 65 + 65],
                   kc == 0 and qs == 0, kc == NKC - 1, [P_, V], [po], skip=True)
            if kc < NKC - 1:
                continue
            rd_ = rden[hh % 2]
            rcp(rd_[:], po.ap(64, [[65, 4]]), [po], [rd_])
            tt("dve", oa_[:, :, head * 64:(head + 1) * 64], po.ap(0, [[65, 4], [1, 64]]), rd_.ap(0, [[1, 4], [0, 64]]),
               ALU.mult, [po, rd_], [oa_])
            if hh < 3:
                continue
            oT = oaT[qb % 2]
            for qs in range(4):
                pt = next_pt()
                for c2 in range(2):
                    tr(pt[:, c2 * 128:(c2 + 1) * 128], oa_[:, qs, c2 * 128:(c2 + 1) * 128], [oa_], [pt])
                cp("dve", oT[:, :, qs * 128:(qs + 1) * 128], pt[:, 0:256].rearrange("p (c t) -> p c t", t=128), [pt], [oT])
            t0 = s0 + qb * 512
            dma("pool", brT_s[0].h.ap().rearrange("(c p) t -> p c t", p=128)[:, :, t0:t0 + 512], oT[:], [oT], [brT_s[0]])

    def phase_2B(gi):
        Lg, sl = cfg.groups[gi]
        ns = len(sl)
        A = Lg // 128
        NB = ns * A
        ar.reset()
        WW = 2 * Lg - 128
        Zr = ar.alloc("Zr", [128, 256, NB], BF16)
        WWA = max(WW, NB * 256)
        Wc = [ar.alloc("Wc", [128, WWA], BF16) for _ in range(3)]
        ar.n += 1
        Zl = T(nc.alloc_sbuf_tensor_at(f"Zl_{ar.n}", [128, NB, 256], BF16, offset=ar.off - WWA * 2), "Zl")
        Zl.b = Wc[2].b
        Ysb = ar.alloc("Ysb", [128, 128, NB], BF16)
        x0t = [ar.alloc("x0t", [128, 512], BF16) for _ in range(2)]
        zt = [ar.alloc("zt", [128, 512], BF16) for _ in range(2)]
        tmpc = [ar.alloc("tmpc", [128, 512], F32) for _ in range(2)]
        obb = [ar.alloc("obb", [128, 512], BF16) for _ in range(2)]
        gb0 = cfg.seqs[sl[0]][0] // 128
        dma("sp", Zl[:], zr_s.h.ap()[:, gb0:gb0 + NB, :], [zr_s], [Zl])
        for q4 in range(4):
            cp("dve" if q4 % 2 else "pool", Zr[:, q4 * 64:(q4 + 1) * 64, :],
               Zl[:, :, q4 * 64:(q4 + 1) * 64].rearrange("p n c -> p c n"), [Zl], [Zr])
        deltas = [0] + [d for d in range(-(A - 1), A) if d != 0]
        blk = 0
        for cc in range(2):
            for cl in range(128):
                c = cc * 128 + cl
                W_ = Wc[c % 3]
                dma("sp", W_[:, 0:WW], bass.AP(kfl[gi].h, c * 2 * Lg + 1, [[1, 128], [1, WW]]), [kfl[gi]], [W_])
                slot = cl % 8
                if slot == 0:
                    py = PF[(c // 8) % 4]
                for di, d in enumerate(deltas):
                    off = 128 * (d + A - 1)
                    a0, a1 = max(0, -d), min(A, A - d)
                    n = a1 - a0
                    rhs = Zr.ap(c * NB + a0, [[A, ns], [1, n]])
                    out = py.ap(slot * NB + a0 + d, [[A, ns], [1, n]])
                    mm(out, W_[:, off:off + 128], rhs, di == 0, di == len(deltas) - 1, [W_, Zr], [py], skip=True)
                if slot == 7:
                    act(Ysb[:, cl - 7:cl + 1, :], py[:, 0:8 * NB].rearrange("p (c n) -> p c n", n=NB), AF.Copy, [py], [Ysb])
            for b4 in range(NB // 4):
                pt = next_pt()
                for bb in range(4):
                    b = b4 * 4 + bb
                    tr(pt[:, bb * 128:(bb + 1) * 128], Ysb.ap(b, [[NB, 128]]), [Ysb], [pt])
                t0 = (gb0 + b4 * 4) * 128
                x0_, z_, tm, ob = x0t[blk % 2], zt[blk % 2], tmpc[blk % 2], obb[blk % 2]
                blk += 1
                dma("sp", x0_[:], x0T_s.h.ap()[cc * 128:(cc + 1) * 128, t0:t0 + 512], [x0T_s], [x0_])
                dma("sp", z_[:], zT_s.h.ap()[cc * 128:(cc + 1) * 128, t0:t0 + 512], [zT_s], [z_])
                ts("dve", tm[:], pt[:, 0:512], rnrm[gi][:, cc:cc + 1], None, ALU.mult, None, [pt, rnrm[gi]], [tm])
                stt("dve", tm[:], z_[:], hbi[:, cc:cc + 1], tm[:], ALU.mult, ALU.add, [z_, hbi, tm], [tm])
                tt("dve", ob[:], tm[:], x0_[:], ALU.mult, [tm, x0_], [ob])
                dma("pool", brT_s[1].h.ap()[cc * 128:(cc + 1) * 128, t0:t0 + 512], ob[:], [ob], [brT_s[1]])

    def phase_2C(sidx):
        s0, L = cfg.seqs[sidx]
        NCK = L // 128
        ar.reset()
        Sb_all = ar.alloc("Sb_all", [128, 2, NCK, 64], BF16)
        Sst = ar.alloc("Sst", [128, 2, 2, 64], F32)
        Sfb = [ar.alloc("Sfb", [128, 2, 64], BF16) for _ in range(2)]
        rt = [ar.alloc("rt", [128, 4, 1024], BF16) for _ in range(2)]
        vk = [ar.alloc("vk", [128, 4, 64], BF16) for _ in range(2)]
        qkTs = [ar.alloc("qkTs", [128, 4, 128], BF16) for _ in range(2)]
        qsf = [ar.alloc("qsf", [128, 2, 128], BF16) for _ in range(2)]
        qsb = [ar.alloc("qsb", [128, 2, 128], BF16) for _ in range(2)]
        Pm = [ar.alloc("Pm", [128, 4, 128], BF16) for _ in range(2)]
        sqo = ar.alloc("sqo", [128, 256], F32)
        sso = [ar.alloc("sso", [128, 4], F32) for _ in range(2)]
        oc32 = [ar.alloc("oc32", [128, 256], F32) for _ in range(2)]
        ocb = [ar.alloc("ocb", [128, 256], BF16) for _ in range(2)]
        ocT = [ar.alloc("ocT", [128, 2, 512], BF16) for _ in range(2)]
        NG = NCK // 4

        def load_group(g, slot):
            t0 = s0 + g * 512
            dma("sp", rt[slot][:], ret_s.h.ap()[t0:t0 + 512, :].rearrange("(s p) c -> p s c", p=128), [ret_s], [rt[slot]])

        def kv_update(rt_, s, d, j):
            vk_ = vk[j % 2]
            tt("dve", vk_[:], rt_[:, s, 512:768].rearrange("p (h e) -> p h e", e=64), wkk.ap(d, [[2, 4], [0, 64]]),
               ALU.mult, [rt_, wkk], [vk_])
            for p in range(2):
                pk = next_pf(4)
                mm(pk[:, 0:128], rt_[:, s, 256 + p * 128:256 + (p + 1) * 128], vk_[:, 2 * p:2 * p + 2, :].rearrange("p h e -> p (h e)"),
                   True, True, [rt_, vk_], [pk])
                for half in range(2):
                    r = slice(half * 64, half * 64 + 64)
                    stt("dve", Sst[r, p, d, :], Sst[r, p, d, :], gch[r, p, d:d + 1], pk[r, half * 64:half * 64 + 64],
                        ALU.mult, ALU.add, [Sst, gch, pk], [Sst])

        mset("pool", Sst[:], 0.0, [Sst])
        j = 0
        for g in reversed(range(NG)):
            slot = g % 2
            load_group(g, slot)
            for s in reversed(range(4)):
                n = g * 4 + s
                cp("act", Sb_all[:, :, n, :], Sst[:, :, 1, :], [Sst], [Sb_all])
                if n > 0:
                    kv_update(rt[slot], s, 1, j)
                    j += 1
        import os
        DBG = int(os.environ.get("DBG2C", "9"))
        for g in range(NG if DBG > 0 else 0):
            slot = g % 2
            load_group(g, slot)
            rt_ = rt[slot]
            ocT_ = ocT[g % 2]
            for s in range(4):
                n = g * 4 + s
                qk_, qf_, qb_, Pm_, Sf_ = qkTs[n % 2], qsf[n % 2], qsb[n % 2], Pm[n % 2], Sfb[n % 2]
                SK = os.environ.get("SKIP", "").split(",")
                pt = next_pt()
                if "tr" not in SK:
                    for c4 in range(4):
                        tr(pt[:, c4 * 128:(c4 + 1) * 128], rt_[:, s, c4 * 128:(c4 + 1) * 128], [rt_], [pt])
                if "cpq" not in SK:
                    cp("act", qk_[:], pt[:, 0:512].rearrange("p (c t) -> p c t", t=128), [pt], [qk_])
                if "qf" not in SK:
                    tt("dve", qf_[:], qk_[:, 0:2, :], wqf[:], ALU.mult, [qk_, wqf], [qf_])
                    tt("dve", qb_[:], qk_[:, 0:2, :], wqb[:], ALU.mult, [qk_, wqb], [qb_])
                if "sf" not in SK:
                    cp("act", Sf_[:], Sst[:, :, 0, :], [Sst], [Sf_])
                psa = PF[(2 * n) % 4]
                psb = PF[(2 * n + 1) % 4]
                for h in range(4):
                    p, half = h // 2, h % 2
                    r = slice(half * 64, half * 64 + 64)
                    pdst = psb if half else psa
                    mm(pdst[:, p * 128:(p + 1) * 128], qk_[r, 2 + p, :], qk_[r, p, :], True, True, [qk_], [pdst])
                for half, pdst in ((0, psa), (1, psb)):
                    tt("dve", Pm_.ap(half * 128, [[256, 2], [1, 128]]), pdst[:, 0:256].rearrange("p (a i) -> p a i", i=128),
                       Dm.ap(half * 128, [[256, 2], [1, 128]]), ALU.mult, [pdst, Dm], [Pm_])
                if DBG < 2:
                    continue
                po = PF[4 + n % 2]
                for h in range(4):
                    p, half = h // 2, h % 2
                    r = slice(half * 64, half * 64 + 64)
                    oh = po[:, h * 64:(h + 1) * 64]
                    mm(oh, Pm_[:, h, :], rt_[:, s, 512 + h * 64:512 + (h + 1) * 64], True, False, [Pm_, rt_], [po], skip=True)
                    mm(oh, qf_[r, p, :], Sf_[r, p, :], False, False, [qf_, Sf_], [po], skip=True)
                    mm(oh, qb_[r, p, :], Sb_all[r, p, n, :], False, True, [qb_, Sb_all], [po], skip=True)
                if DBG < 3:
                    continue
                ss_, o32, ob_ = sso[n % 2], oc32[n % 2], ocb[n % 2]
                act(o32[:], po[:, 0:256], AF.Copy, [po], [o32])
                tt("dve", sqo[:], o32[:], o32[:], ALU.mult, [o32], [sqo])
                red("dve", ss_[:], sqo[:].rearrange("p (h e) -> p h e", e=64), [sqo], [ss_])
                rstd_from_ss(ss_[:], 64.0, [ss_])
                tt("dve", o32[:].rearrange("p (h e) -> p h e", e=64), o32[:].rearrange("p (h e) -> p h e", e=64),
                   ss_.ap(0, [[1, 4], [0, 64]]), ALU.mult, [o32, ss_], [o32])
                tt("dve", ob_[:], o32[:], rt_[:, s, 768:1024], ALU.mult, [o32, rt_], [ob_])
                pt2 = next_pt()
                for c2 in range(2):
                    tr(pt2[:, c2 * 128:(c2 + 1) * 128], ob_[:, c2 * 128:(c2 + 1) * 128], [ob_], [pt2])
                cp("act", ocT_[:, :, s * 128:(s + 1) * 128], pt2[:, 0:256].rearrange("p (c t) -> p c t", t=128), [pt2], [ocT_])
                if n < NCK - 1:
                    kv_update(rt_, s, 0, n)
            t0 = s0 + g * 512
            if DBG < 3:
                continue
            dma("pool", brT_s[2].h.ap().rearrange("(c p) t -> p c t", p=128)[:, :, t0:t0 + 512], ocT_[:], [ocT_], [brT_s[2]])

    def phase_3a(l, xsrc):
        ar.reset()
        Wg = ar.alloc("Wg", [128, 8, 4096], BF16)
        Wb = ar.alloc("Wb", [128, 4, 2, D], BF16)
        Wo = ar.alloc("Wo", [128, 8, D], BF16)
        for k in range(8):
            dma("pool", Wg[:, k, :], w_in.ap()[l, k * 128:(k + 1) * 128, 3072:7168], [], [Wg])
            dma("pool", Wo[:, k, :], w_o.ap()[l, k * 128:(k + 1) * 128, :], [], [Wo])
        for n in range(4):
            dma("pool", Wb[:, n, :, :], w_br.ap()[l, n].rearrange("(k p) c -> p k c", p=128), [], [Wb])
        xT = [ar.alloc("xT3", [128, 8, 512], BF16) for _ in range(2)]
        br = [ar.alloc("br3", [128, 4, 2, 512], BF16) for _ in range(2)]
        sg = [ar.alloc("sg", [128, 512], F32) for _ in range(2)]
        mg = [ar.alloc("mg", [128, 512], F32) for _ in range(2)]
        tm = [ar.alloc("tm3", [128, 512], F32) for _ in range(2)]
        mT = ar.alloc("mT", [128, 8, 512], BF16)
        xt = [ar.alloc("xt3", [128, D], F32)] * 2
        y32 = [ar.alloc("y32", [128, D], F32) for _ in range(2)]
        junk = ar.alloc("junk3", [128, D], BF16)
        ss2 = [ar.alloc("ss2", [128, 2], F32) for _ in range(2)]
        ss1 = [ar.alloc("ss1", [128, 1], F32) for _ in range(2)]
        hn = [ar.alloc("hn", [128, D], BF16) for _ in range(4)]
        hT = [ar.alloc("hT", [128, 8, 512], BF16)] * 2
        jj = 0
        deferred3 = []

        def flush3():
            while deferred3:
                deferred3.pop(0)()

        for ti in range(NTILE):
            t0 = ti * 512
            xT_, br_ = xT[ti % 2], br[ti % 2]
            dma("sp", xT_[:], xnT_s.h.ap().rearrange("(k p) t -> p k t", p=128)[:, :, t0:t0 + 512], [xnT_s], [xT_])
            for n in range(4):
                dma("sp", br_[:, n, :, :], brT_s[n].h.ap().rearrange("(c p) t -> p c t", p=128)[:, :, t0:t0 + 512], [brT_s[n]], [br_])
            for j in range(8):
                mg_ = mg[j % 2]
                for n in range(4):
                    pg = next_pf()
                    for k in range(8):
                        mm(pg[:], Wg[:, k, n * D + j * 128:n * D + (j + 1) * 128], xT_[:, k, :], k == 0, k == 7, [Wg, xT_], [pg])
                    pp = next_pf()
                    for kk in range(2):
                        mm(pp[:], Wb[:, n, kk, j * 128:(j + 1) * 128], br_[:, n, kk, :], kk == 0, kk == 1, [Wb, br_], [pp])
                    if n == 0 and deferred3:
                        deferred3.pop(0)()
                    sg_ = sg[jj % 2]
                    jj += 1
                    act(sg_[:], pg[:], AF.Sigmoid, [pg], [sg_])
                    if n == 0:
                        tt("dve", mg_[:], sg_[:], pp[:], ALU.mult, [sg_, pp], [mg_])
                    else:
                        tm_ = tm[jj % 2]
                        tt("dve", tm_[:], sg_[:], pp[:], ALU.mult, [sg_, pp], [tm_])
                        if n < 3:
                            tt("pool", mg_[:], mg_[:], tm_[:], ALU.add, [mg_, tm_], [mg_])
                        else:
                            tt("pool", mT[:, j, :], mg_[:], tm_[:], ALU.add, [mg_, tm_], [mT])
            hT_ = hT[ti % 2]
            for s in range(4):
                i = ti * 4 + s
                x_, y_, s2, s1, hn_ = xt[i % 2], y32[i % 2], ss2[i % 2], ss1[i % 2], hn[i % 4]
                dma("sp", x_[:], xsrc.h.ap()[i * 128:(i + 1) * 128, :], [xsrc], [x_])
                pos = []
                for nn in range(2):
                    po = next_pf()
                    pos.append(po)
                    for k in range(8):
                        mm(po[:], mT[:, k, s * 128:(s + 1) * 128], Wo[:, k, nn * 512:(nn + 1) * 512], k == 0, k == 7, [mT, Wo], [po])
                    act(junk[:, nn * 512:(nn + 1) * 512], po[:], AF.Square, [po], [junk, s2], accum=s2[:, nn:nn + 1])
                tt("dve", s1[:], s2[:, 0:1], s2[:, 1:2], ALU.add, [s2], [s1])
                rstd_from_ss(s1[:], D, [s1])
                for nn in range(2):
                    stt("dve", y_[:, nn * 512:(nn + 1) * 512], pos[nn][:], s1[:, 0:1], gT[:, 1, nn * 512:(nn + 1) * 512],
                        ALU.mult, ALU.mult, [pos[nn], s1, gT], [y_])
                tt("pool", y_[:], y_[:], x_[:], ALU.add, [y_, x_], [y_])
                dma("pool", hbuf.h.ap()[i * 128:(i + 1) * 128, :], y_[:], [y_], [hbuf])
                act(junk[:], y_[:], AF.Square, [y_], [junk, s1], accum=s1[:])
                rstd_from_ss(s1[:], D, [s1])
                stt("dve", hn_[:], y_[:], s1[:, 0:1], gT[:, 2, :], ALU.mult, ALU.mult, [y_, s1, gT], [hn_])
                def do_tr(hn_=hn_, hT_=hT_, s=s):
                    pt = next_pt()
                    for k in range(8):
                        tr(pt[:, k * 128:(k + 1) * 128], hn_[:, k * 128:(k + 1) * 128], [hn_], [pt])
                    cp("act", hT_[:, :, s * 128:(s + 1) * 128], pt[:].rearrange("p (k t) -> p k t", t=128), [pt], [hT_])
                deferred3.append(do_tr)

            def do_store(hT_=hT_, t0=t0):
                dma("pool", hnT_s.h.ap().rearrange("(k p) t -> p k t", p=128)[:, :, t0:t0 + 512], hT_[:], [hT_], [hnT_s])
            deferred3.append(do_store)
        flush3()

    def phase_3b(l, dst):
        ar.reset()
        Wi = ar.alloc("Wi", [128, 8, 2 * DFF], BF16)
        Wf = ar.alloc("Wf", [128, 22, D], BF16)
        for k in range(8):
            dma("pool", Wi[:, k, :], w_fi.ap()[l, k * 128:(k + 1) * 128, :], [], [Wi])
        for k in range(22):
            dma("pool", Wf[:, k, :], w_fo.ap()[l, k * 128:(k + 1) * 128, :], [], [Wf])
        hT = [ar.alloc("hT3", [128, 8, 512], BF16)] * 2
        sg = [ar.alloc("sgb", [128, 512], BF16) for _ in range(2)]
        fT = ar.alloc("fT", [128, 22, 512], BF16)
        ht = [ar.alloc("ht", [128, D], F32)] * 2
        y32 = [ar.alloc("y32b", [128, D], F32)] * 2
        junk = ar.alloc("junkb", [128, D], BF16)
        ss2 = [ar.alloc("ss2b", [128, 2], F32) for _ in range(2)]
        ss1 = [ar.alloc("ss1b", [128, 1], F32) for _ in range(2)]
        jj = 0
        for ti in range(NTILE):
            t0 = ti * 512
            hT_ = hT[ti % 2]
            dma("sp", hT_[:], hnT_s.h.ap().rearrange("(k p) t -> p k t", p=128)[:, :, t0:t0 + 512], [hnT_s], [hT_])
            for j in range(22):
                pg = next_pf()
                for k in range(8):
                    mm(pg[:], Wi[:, k, j * 128:(j + 1) * 128], hT_[:, k, :], k == 0, k == 7, [Wi, hT_], [pg])
                pu = next_pf()
                for k in range(8):
                    mm(pu[:], Wi[:, k, DFF + j * 128:DFF + (j + 1) * 128], hT_[:, k, :], k == 0, k == 7, [Wi, hT_], [pu])
                sg_ = sg[jj % 2]
                jj += 1
                act(sg_[:], pg[:], AF.Silu, [pg], [sg_])
                tt("dve", fT[:, j, :], sg_[:], pu[:], ALU.mult, [sg_, pu], [fT])
            for s in range(4):
                i = ti * 4 + s
                h_, y_, s2, s1 = ht[i % 2], y32[i % 2], ss2[i % 2], ss1[i % 2]
                dma("sp", h_[:], hbuf.h.ap()[i * 128:(i + 1) * 128, :], [hbuf], [h_])
                pos = []
                for nn in range(2):
                    po = next_pf()
                    pos.append(po)
                    for k in range(22):
                        mm(po[:], fT[:, k, s * 128:(s + 1) * 128], Wf[:, k, nn * 512:(nn + 1) * 512], k == 0, k == 21, [fT, Wf], [po])
                    act(junk[:, nn * 512:(nn + 1) * 512], po[:], AF.Square, [po], [junk, s2], accum=s2[:, nn:nn + 1])
                tt("dve", s1[:], s2[:, 0:1], s2[:, 1:2], ALU.add, [s2], [s1])
                rstd_from_ss(s1[:], D, [s1])
                for nn in range(2):
                    stt("dve", y_[:, nn * 512:(nn + 1) * 512], pos[nn][:], s1[:, 0:1], gT[:, 3, nn * 512:(nn + 1) * 512],
                        ALU.mult, ALU.mult, [pos[nn], s1, gT], [y_])
                tt("pool", y_[:], y_[:], h_[:], ALU.add, [y_, h_], [y_])
                dma("pool", dst.h.ap()[i * 128:(i + 1) * 128, :], y_[:], [y_], [dst])

    PH = getattr(cfg, "phases", None)

    def on(name):
        return PH is None or name in PH

    mk.marks = []

    def mark(name):
        mk.marks.append((name, len(mk.eng_ops["pe"]), len(mk.eng_ops["act"]), len(mk.eng_ops["dve"])))

    for l in range(depth):
        src = xin if l == 0 else x1buf
        dst = x1buf if l < depth - 1 else yout
        barrier()
        mark(f"L{l}:start")
        load_layer_consts(l)
        if on("R"):
            phase_R(l)
        barrier()
        mark(f"L{l}:R")
        if on("F"):
            for gi in range(len(cfg.groups)):
                phase_F(l, gi)
                barrier()
        mark(f"L{l}:F")
        if on("A"):
            phase_A(l, src)
            barrier()
        mark(f"L{l}:A")
        if on("1b"):
            phase_1b(l)
            barrier()
        mark(f"L{l}:1b")
        if on("2A"):
            for sidx in range(len(cfg.seqs)):
                phase_2A(sidx)
                barrier()
                mark(f"L{l}:2A.{sidx}")
        if on("2B"):
            for gi in range(len(cfg.groups)):
                phase_2B(gi)
                barrier()
                mark(f"L{l}:2B.{gi}")
        if on("2C"):
            for sidx in range(len(cfg.seqs)):
                phase_2C(sidx)
                barrier()
            mark(f"L{l}:2C")
        if on("3a"):
            phase_3a(l, src)
            barrier()
        mark(f"L{l}:3a")
        if on("3b"):
            phase_3b(l, dst)
            barrier()
        mark(f"L{l}:3b")

    mk.emit()
    return nc, mk, tabs


def make_in_maps(cfg, inputs, tabs, ncore=NCORE):
    f = lambda a: np.ascontiguousarray(np.asarray(a, dtype=np.float32))
    xs, xp = f(inputs["x_sample"]), f(inputs["x_prompt"])
    dp = cfg.depth
    shared = {
        "w_in": f(inputs["w_in"])[:dp], "w_branch": f(inputs["w_branch"])[:dp], "w_out": f(inputs["w_out"])[:dp],
        "w_ffn_in": f(inputs["w_ffn_in"])[:dp], "w_ffn_out": f(inputs["w_ffn_out"])[:dp],
        "norm_gains": f(inputs["norm_gains"])[:dp],
        "qk_norm": f(inputs["qk_norm"])[:dp].reshape(dp, 128),
        "hy_w1": f(inputs["hy_w1"])[:dp], "hy_w2": f(inputs["hy_w2"])[:dp], "hy_w3": f(inputs["hy_w3"])[:dp],
        "rde": f(inputs["ret_decay_exp"])[:dp].reshape(dp, 8),
    }
    hcw, hcb = f(inputs["hy_conv_w"])[:dp], f(inputs["hy_conv_b"])[:dp]
    cv = np.concatenate([hcw, hcb[:, None, :]], axis=1)
    shared["convp"] = np.ascontiguousarray(cv.reshape(dp, 4, 6, 128).transpose(0, 3, 2, 1))
    shared["scwp"] = np.ascontiguousarray(f(inputs["sc_conv_w"])[:dp].reshape(dp, 3, 2, 128).transpose(0, 3, 2, 1))
    shared["hbiasp"] = np.ascontiguousarray(f(inputs["hy_bias"])[:dp].reshape(dp, 2, 128).transpose(0, 2, 1))
    hv = np.stack([f(inputs["hy_b1"])[:dp], f(inputs["hy_b2"])[:dp], f(inputs["hy_freq"])[:dp, 0], f(inputs["hy_freq"])[:dp, 1]], axis=-1)
    shared["hyvec"] = np.ascontiguousarray(hv)
    shared.update(tabs)
    maps = []
    for c in range(ncore):
        parts = [xs[c]] + [xp[cfg.NP * c + i] for i in range(cfg.NP)]
        m = dict(shared)
        m["xin"] = np.ascontiguousarray(np.concatenate(parts, axis=0))
        maps.append(m)
    return maps


_CACHE = {}


def kernel(**inputs):
    cfg = Cfg()
    if "nc" not in _CACHE:
        _CACHE["nc"] = build(cfg)
    nc, mk, tabs = _CACHE["nc"]
    maps = make_in_maps(cfg, inputs, tabs)
    res = run_bass_kernel_spmd(nc, maps, core_ids=list(range(NCORE)))
    ys = np.stack([r["yout"][:cfg.LS] for r in res.results], axis=0)
    yp = np.concatenate([r["yout"][cfg.LS:].reshape(cfg.NP, cfg.LP, D) for r in res.results], axis=0)
    return (np.ascontiguousarray(yp.astype(np.float32)), np.ascontiguousarray(ys.astype(np.float32)))
```
